# Optimizing a Trainium2 kernel written in Bass

```python
import jax, jax.numpy as jnp
from jax import lax
import numpy as np

D_MODEL = 1024
BATCH = 16
SEQ = 2048
DEPTH = 2

N_GROUPS = 4
GROUP_HEADS = 4
GROUP_WIDTH = D_MODEL // N_GROUPS
HEAD_DIM = GROUP_WIDTH // GROUP_HEADS
D_MIX = N_GROUPS * GROUP_WIDTH
OFF_RG_X = 0
OFF_RG_G = GROUP_WIDTH
OFF_FOX = 2 * GROUP_WIDTH
OFF_FOX_F = 5 * GROUP_WIDTH
OFF_SB = OFF_FOX_F + GROUP_HEADS
OFF_SC = OFF_SB + 3 * GROUP_WIDTH
N_IN = OFF_SC + 3 * GROUP_WIDTH
RG_CONV = 4
RGLRU_C = 8.0
SC_CONV = 3
Q_BLOCK = 128
N_MEM = 256
XA_HEADS = 4
XA_HEAD_DIM = D_MODEL // XA_HEADS
PEER_HEADS = 8
N_KEYS = 128
N_EXPERTS = N_KEYS * N_KEYS
PEER_QDIM = 256
PEER_HALF = PEER_QDIM // 2
PEER_TOPK = 16
PEER_TOKEN_BLOCK = 128
ALPHA = (2.0 * DEPTH) ** 0.25
BETA = (8.0 * DEPTH) ** -0.25
LN_EPS = 1e-5

kernel_name = 'hymba_style_hybrid_peer_deepnorm'

F32 = jnp.float32


def layer_norm(x, g, b):
    xf = x.astype(F32)
    mu = jnp.mean(xf, axis=-1, keepdims=True)
    xc = xf - mu
    var = jnp.mean(xc * xc, axis=-1, keepdims=True)
    return (xc * lax.rsqrt(var + LN_EPS) * g.astype(F32) + b.astype(F32)).astype(x.dtype)


def group_rms(y):
    yf = y.astype(F32)
    return yf * lax.rsqrt(jnp.mean(yf * yf, axis=-1, keepdims=True) + 1e-6)


def causal_depthwise_conv(x, w):
    K = w.shape[0]
    S = x.shape[1]
    xp = jnp.pad(x, ((0, 0), (K - 1, 0), (0, 0)))
    y = xp[:, 0:S] * w[0]
    for k in range(1, K):
        y = y + xp[:, k:k + S] * w[k]
    return y


def block_diag_linear(x, w, b):
    B, S, _ = x.shape
    H, d, _ = w.shape
    y = jnp.einsum('bshi,hij->bshj', x.reshape(B, S, H, d), w)
    return y.reshape(B, S, H * d) + b


def rg_lru_branch(x, gate, conv_w, conv_b, wa, ba, wi, bi, lam):
    xc = (causal_depthwise_conv(x, conv_w) + conv_b).astype(F32)
    r = jax.nn.sigmoid(block_diag_linear(xc, wa.astype(F32), ba.astype(F32)))
    i = jax.nn.sigmoid(block_diag_linear(xc, wi.astype(F32), bi.astype(F32)))
    log_a = -RGLRU_C * r * jax.nn.softplus(-lam.astype(F32))
    a = jnp.exp(log_a)
    u = jnp.sqrt(-jnp.expm1(2.0 * log_a)) * (i * xc)

    def combine(left, right):
        a_l, b_l = left
        a_r, b_r = right
        return a_l * a_r, a_r * b_l + b_r

    _, h = lax.associative_scan(combine, (a, u), axis=1)
    return (h * jax.nn.gelu(gate.astype(F32))).astype(x.dtype)


def split_heads(t):
    B, S, _ = t.shape
    return t.reshape(B, S, GROUP_HEADS, HEAD_DIM).transpose(0, 2, 1, 3)


def merge_heads(t):
    B, H, S, d = t.shape
    return t.transpose(0, 2, 1, 3).reshape(B, S, H * d)


def to_blocks(t):
    nb = t.shape[2] // Q_BLOCK
    t = t.reshape(t.shape[:2] + (nb, Q_BLOCK) + t.shape[3:])
    return jnp.moveaxis(t, 2, 0)


def from_blocks(o):
    o = jnp.moveaxis(o, 0, 2)
    return o.reshape(o.shape[0], o.shape[1], -1, o.shape[-1])


def forgetting_attention(q, k, v, f_logit, b_f):
    S, d = q.shape[2], q.shape[3]
    log_f = jax.nn.log_sigmoid(f_logit.astype(F32) + b_f.astype(F32))
    cum = jnp.cumsum(log_f, axis=1).transpose(0, 2, 1)
    key_pos = jnp.arange(S)
    scale = d ** -0.5

    def block(args):
        qi, ci, blk = args
        q_pos = blk * Q_BLOCK + jnp.arange(Q_BLOCK)
        logits = (jnp.einsum('bhqd,bhkd->bhqk', qi, k).astype(F32) * scale
                  + ci[..., :, None] - cum[..., None, :])
        logits = jnp.where(key_pos[None, :] <= q_pos[:, None], logits, -jnp.inf)
        p = jax.nn.softmax(logits, axis=-1)
        return jnp.einsum('bhqk,bhkd->bhqd', p.astype(v.dtype), v)

    out = lax.map(block, (to_blocks(q), to_blocks(cum), jnp.arange(S // Q_BLOCK)))
    return from_blocks(out)


def stick_breaking_attention(q, k, v):
    S, d = q.shape[2], q.shape[3]
    key_pos = jnp.arange(S)
    scale = d ** -0.5

    def block(args):
        qi, blk = args
        q_pos = blk * Q_BLOCK + jnp.arange(Q_BLOCK)
        z = jnp.einsum('bhqd,bhkd->bhqk', qi, k).astype(F32) * scale
        strict = key_pos[None, :] < q_pos[:, None]
        log_1m = jnp.where(strict, jax.nn.log_sigmoid(-z), 0.0)
        tail = lax.cumsum(log_1m, axis=3, reverse=True) - log_1m
        log_w = jnp.where(strict, jax.nn.log_sigmoid(z) + tail, -jnp.inf)
        w = jnp.exp(log_w)
        return jnp.einsum('bhqk,bhkd->bhqd', w.astype(v.dtype), v)

    out = lax.map(block, (to_blocks(q), jnp.arange(S // Q_BLOCK)))
    return from_blocks(out)


def hybrid_mixer(x, w_in, w_out, rg_conv_w, rg_conv_b, rg_wa, rg_ba, rg_wi, rg_bi,
                 rg_lambda, fox_bf, sc_conv_w, mix_norm_g):
    GW = GROUP_WIDTH
    p = x @ w_in
    y_rg = rg_lru_branch(p[..., OFF_RG_X:OFF_RG_X + GW], p[..., OFF_RG_G:OFF_RG_G + GW],
                         rg_conv_w, rg_conv_b, rg_wa, rg_ba, rg_wi, rg_bi, rg_lambda)
    fq = split_heads(p[..., OFF_FOX:OFF_FOX + GW])
    fk = split_heads(p[..., OFF_FOX + GW:OFF_FOX + 2 * GW])
    fv = split_heads(p[..., OFF_FOX + 2 * GW:OFF_FOX + 3 * GW])
    y_fox = merge_heads(forgetting_attention(fq, fk, fv,
                                             p[..., OFF_FOX_F:OFF_FOX_F + GROUP_HEADS], fox_bf))
    sq = split_heads(p[..., OFF_SB:OFF_SB + GW])
    sk = split_heads(p[..., OFF_SB + GW:OFF_SB + 2 * GW])
    sv = split_heads(p[..., OFF_SB + 2 * GW:OFF_SB + 3 * GW])
    y_sb = merge_heads(stick_breaking_attention(sq, sk, sv))
    bg = p[..., OFF_SC:OFF_SC + GW]
    cg = p[..., OFF_SC + GW:OFF_SC + 2 * GW]
    hh = p[..., OFF_SC + 2 * GW:OFF_SC + 3 * GW]
    y_sc = bg * causal_depthwise_conv(cg * hh, sc_conv_w)
    y = jnp.concatenate([group_rms(y_rg), group_rms(y_fox), group_rms(y_sb), group_rms(y_sc)],
                        axis=-1) * mix_norm_g.astype(F32)
    return y.astype(x.dtype) @ w_out


def memory_cross_attention(x, mem, wq, wkv, wo):
    B, S, D = x.shape
    M = mem.shape[1]
    q = (x @ wq).reshape(B, S, XA_HEADS, XA_HEAD_DIM)
    kv = mem @ wkv
    k = kv[..., :D].reshape(B, M, XA_HEADS, XA_HEAD_DIM)
    v = kv[..., D:].reshape(B, M, XA_HEADS, XA_HEAD_DIM)
    logits = jnp.einsum('bshd,bmhd->bhsm', q, k).astype(F32) * XA_HEAD_DIM ** -0.5
    p = jax.nn.softmax(logits, axis=-1)
    o = jnp.einsum('bhsm,bmhd->bshd', p.astype(v.dtype), v).reshape(B, S, D)
    return o @ wo


def peer_ffn(x, wq, k1, k2, u, v):
    B, S, D = x.shape
    qry = (x @ wq).reshape(B, S, PEER_HEADS, PEER_QDIM)
    s1 = jnp.einsum('bshc,nc->bshn', qry[..., :PEER_HALF], k1).astype(F32)
    s2 = jnp.einsum('bshc,nc->bshn', qry[..., PEER_HALF:], k2).astype(F32)
    v1, i1 = lax.top_k(s1, PEER_TOPK)
    v2, i2 = lax.top_k(s2, PEER_TOPK)
    cand = (v1[..., :, None] + v2[..., None, :]).reshape(B, S, PEER_HEADS, PEER_TOPK * PEER_TOPK)
    sc, ci = lax.top_k(cand, PEER_TOPK)
    e1 = jnp.take_along_axis(i1, ci // PEER_TOPK, axis=-1)
    e2 = jnp.take_along_axis(i2, ci % PEER_TOPK, axis=-1)
    expert = e1 * N_KEYS + e2
    gate = jax.nn.softmax(sc, axis=-1)
    nt = (B * S) // PEER_TOKEN_BLOCK
    E = PEER_HEADS * PEER_TOPK
    xt = x.reshape(nt, PEER_TOKEN_BLOCK, D)
    et = expert.reshape(nt, PEER_TOKEN_BLOCK, E)
    gt = gate.reshape(nt, PEER_TOKEN_BLOCK, E)

    def block(args):
        xb, eb, gb = args
        act = jax.nn.gelu(jnp.einsum('td,ted->te', xb, u[eb]).astype(F32), approximate=False)
        coef = (gb * act).astype(x.dtype)
        return jnp.einsum('te,ted->td', coef, v[eb])

    return lax.map(block, (xt, et, gt)).reshape(B, S, D)


def setup_inputs(seed: int = 0) -> dict:
    key = jax.random.key(seed)
    keys = iter(jax.random.split(key, 40))
    L, D, GW, GH, HD = DEPTH, D_MODEL, GROUP_WIDTH, GROUP_HEADS, HEAD_DIM

    def nrm(shape, scale):
        return jax.random.normal(next(keys), shape, F32) * scale

    a0 = jax.random.uniform(next(keys), (L, GW), F32, minval=0.9, maxval=0.999)
    s0 = a0 ** (1.0 / RGLRU_C)
    rg_lambda = jnp.log(s0) - jnp.log1p(-s0)
    return {
        'x': nrm((BATCH, SEQ, D), 1.0),
        'mem': nrm((BATCH, N_MEM, D), 1.0),
        'w_in': nrm((L, D, N_IN), D ** -0.5),
        'w_out': nrm((L, D_MIX, D), D_MIX ** -0.5 * BETA),
        'rg_conv_w': nrm((L, RG_CONV, GW), RG_CONV ** -0.5),
        'rg_conv_b': nrm((L, GW), 0.01),
        'rg_wa': nrm((L, GH, HD, HD), HD ** -0.5),
        'rg_ba': nrm((L, GW), 0.01),
        'rg_wi': nrm((L, GH, HD, HD), HD ** -0.5),
        'rg_bi': nrm((L, GW), 0.01),
        'rg_lambda': rg_lambda,
        'fox_bf': jax.random.uniform(next(keys), (L, GH), F32, minval=1.0, maxval=4.0),
        'sc_conv_w': nrm((L, SC_CONV, GW), SC_CONV ** -0.5),
        'mix_norm_g': 1.0 + nrm((L, D_MIX), 0.01),
        'ln1_g': 1.0 + nrm((L, D), 0.01),
        'ln1_b': nrm((L, D), 0.01),
        'xa_wq': nrm((L, D, D), D ** -0.5),
        'xa_wkv': nrm((L, D, 2 * D), D ** -0.5),
        'xa_wo': nrm((L, D, D), D ** -0.5 * BETA),
        'ln2_g': 1.0 + nrm((L, D), 0.01),
        'ln2_b': nrm((L, D), 0.01),
        'peer_wq': nrm((L, D, PEER_HEADS * PEER_QDIM), D ** -0.5),
        'peer_k1': nrm((L, N_KEYS, PEER_HALF), PEER_HALF ** -0.5),
        'peer_k2': nrm((L, N_KEYS, PEER_HALF), PEER_HALF ** -0.5),
        'peer_u': nrm((L, N_EXPERTS, D), D ** -0.5),
        'peer_v': nrm((L, N_EXPERTS, D), PEER_HEADS ** -0.5 * BETA),
        'ln3_g': 1.0 + nrm((L, D), 0.01),
        'ln3_b': nrm((L, D), 0.01),
    }


def reference(x, mem, w_in, w_out, rg_conv_w, rg_conv_b, rg_wa, rg_ba, rg_wi, rg_bi,
              rg_lambda, fox_bf, sc_conv_w, mix_norm_g, ln1_g, ln1_b, xa_wq, xa_wkv, xa_wo,
              ln2_g, ln2_b, peer_wq, peer_k1, peer_k2, peer_u, peer_v, ln3_g, ln3_b):
    for l in range(DEPTH):
        mix = hybrid_mixer(x, w_in[l], w_out[l], rg_conv_w[l], rg_conv_b[l], rg_wa[l], rg_ba[l],
                           rg_wi[l], rg_bi[l], rg_lambda[l], fox_bf[l], sc_conv_w[l],
                           mix_norm_g[l])
        x = layer_norm(ALPHA * x + mix, ln1_g[l], ln1_b[l])
        xa = memory_cross_attention(x, mem, xa_wq[l], xa_wkv[l], xa_wo[l])
        x = layer_norm(ALPHA * x + xa, ln2_g[l], ln2_b[l])
        ff = peer_ffn(x, peer_wq[l], peer_k1[l], peer_k2[l], peer_u[l], peer_v[l])
        x = layer_norm(ALPHA * x + ff, ln3_g[l], ln3_b[l])
    return x
```

```python
import numpy as np
from contextlib import ExitStack
import concourse.bass as bass
import concourse.mybir as mybir
from concourse.bass_utils import run_bass_kernel_spmd

F32 = mybir.dt.float32
BF16 = mybir.dt.bfloat16
U32 = mybir.dt.uint32
AF = mybir.ActivationFunctionType
ALU = mybir.AluOpType
AX = mybir.AxisListType

D = 1024
KC = 8
GW = 256
OFF_RG_X = 0
OFF_RG_G = 256
OFF_FOX = 512
OFF_FOX_F = 1280
OFF_SB = 1284
OFF_SC = 2052
N_IN = 2820
NMEM = 256
NE = 16384
DEPTH = 2
ALPHA = (2.0 * DEPTH) ** 0.25
LN_EPS = 1e-5
NEG = -1.0e30

ENGS = ("pe", "act", "dve", "pool", "sp")


class Buf:
    __slots__ = ("name", "w", "r", "sem", "cnt")

    def __init__(self, name=""):
        self.name = name
        self.w = None
        self.r = {}
        self.sem = None
        self.cnt = 0


def bufs(n, name=""):
    return [Buf(f"{name}{i}") for i in range(n)]


class Ctx:
    def __init__(self, nc, stack):
        self.nc = nc
        self.stack = stack
        self.q = {e: [] for e in ENGS}
        self.esem = {}
        self.ecnt = {e: 0 for e in ENGS}
        for e in ("pe", "act", "dve", "pool"):
            self.esem[e] = stack.enter_context(nc.semaphore("s_" + e))
        self.nsem = 4
        self.dsb = []

    def barrier(self):
        evs = [(self.esem[e], self.ecnt[e], "x") for e in self.esem if self.ecnt[e] > 0]
        evs += [(b.sem, b.cnt, "dma") for b in self.dsb if b.cnt > 0]
        for e in ENGS:
            self.q[e].append((list(evs), None, None, 0))

    def newsem(self, name):
        self.nsem += 1
        return self.stack.enter_context(self.nc.semaphore(name))

    def _deps(self, reads, writes):
        deps = []
        for b in reads:
            if b.w is not None:
                deps.append(b.w)
        for b in writes:
            if b.w is not None:
                deps.append(b.w)
            deps.extend(b.r.values())
        return deps

    def _commit(self, ev, reads, writes):
        k = id(ev[0])
        for b in reads:
            o = b.r.get(k)
            if o is None or o[1] < ev[1]:
                b.r[k] = ev
        for b in writes:
            b.w = ev
            b.r = {}

    def op(self, eng, fn, reads=(), writes=()):
        deps = self._deps(reads, writes)
        self.ecnt[eng] += 1
        ev = (self.esem[eng], self.ecnt[eng], eng)
        self.q[eng].append((deps, fn, ev[0], 1))
        self._commit(ev, reads, writes)
        return ev

    def dma(self, eng, fns, reads=(), writes=(), sembuf=None):
        if not isinstance(fns, (list, tuple)):
            fns = [fns]
        sb = sembuf if sembuf is not None else writes[0]
        if sb.sem is None:
            sb.sem = self.newsem("d%d" % self.nsem)
            self.dsb.append(sb)
        deps = self._deps(reads, writes)
        if sb.cnt > 0:
            deps.append((sb.sem, sb.cnt, "dma"))
        for i, fn in enumerate(fns):
            self.q[eng].append((deps if i == 0 else [], fn, sb.sem, 16))
        sb.cnt += 16 * len(fns)
        ev = (sb.sem, sb.cnt, "dma")
        self._commit(ev, reads, writes)
        return ev

    def emit(self, final_events):
        nc = self.nc
        engmap = {"pe": "tensor", "act": "scalar", "dve": "vector", "pool": "gpsimd", "sp": "sync"}
        with nc.Block() as block:
            for e in ENGS:
                ops = self.q[e]
                fin = final_events if e == "sp" else []

                def body(engine, ops=ops, e=e, fin=fin):
                    waited = {}
                    for deps, fn, sem, inc in ops:
                        need = {}
                        for (s, v, pe) in deps:
                            if pe == e and e == "pe":
                                continue
                            k = id(s)
                            if waited.get(k, 0) >= v:
                                continue
                            if k not in need or need[k][1] < v:
                                need[k] = (s, v)
                        for k, (s, v) in need.items():
                            engine.wait_ge(s, v)
                            waited[k] = v
                        if fn is None:
                            continue
                        ins = fn(engine)
                        ins.then_inc(sem, inc)
                    for (s, v, pe) in fin:
                        engine.wait_ge(s, v)

                getattr(block, engmap[e])(body)


ARENA_BYTES = 87 * 1024 + 512
LN_OFF = 79 * 1024 + 512


def build_program(NSEQ=2, S=2048, L=2, stop_after=None, phases=("mixer", "xattn", "peer")):
    nc = bass.Bass("TRN2", target_bir_lowering=False)
    NT = S // 128
    NQ = S // 512
    NB2 = S // 256

    def din(name, shape, dt=F32):
        return nc.dram_tensor(name, list(shape), dt, kind="ExternalInput").ap()

    x_d = din("x", [NSEQ, S, D])
    memT_d = din("memT", [NSEQ, D, NMEM])
    w_in_d = din("w_in", [L, D, N_IN])
    w_out_d = din("w_out", [L, D, D])
    rgp_d = din("rgp", [L, 2, 128, 8])
    wa_d = din("wa_bd", [L, 2, 128, 128])
    wi_d = din("wi_bd", [L, 2, 128, 128])
    foxbf_d = din("fox_bf", [L, 4, 1])
    scp_d = din("scp", [L, 2, 128, 4])
    mng_d = din("mng", [L, 128, 8])
    lng_d = din("ln_g", [L, 3, D])
    lnb_d = din("ln_b", [L, 3, D])
    xwq_d = din("xa_wq", [L, D, D])
    xwkv_d = din("xa_wkv", [L, D, 2 * D])
    xwo_d = din("xa_wo", [L, D, D])
    pwq_d = din("peer_wq", [L, D, 2048])
    k12T_d = din("k12T", [L, 128, 2, 128])
    pu_d = [din(f"peer_u{l}", [NE, D]) for l in range(L)]
    pv_d = [din(f"peer_v{l}", [NE, D]) for l in range(L)]
    cid_d = din("c_ident", [128, 128])
    cmle_d = din("c_nble", [128, 896])
    cmlt_d = din("c_mlt", [128, 896])
    cntri_d = din("c_negtri", [128, 128])
    ciota_d = din("c_iota16", [128, 16])
    ciota256_d = din("c_iota256", [128, 256])
    out_d = nc.dram_tensor("out", [NSEQ, S, D], F32, kind="ExternalOutput").ap()
    uvb_d = [nc.dram_tensor(f"uvb{l}", [NE, 2 * D], BF16, kind="Internal").ap() for l in range(L)]

    with ExitStack() as st:
        c = Ctx(nc, st)

        def sb(name, shape, dt=F32):
            return st.enter_context(nc.sbuf_tensor("sb_" + name, list(shape), dt))

        b_lnp_shared = Buf("lnp")
        b_misc = Buf("misc")
        b_outd = Buf("outd")
        xs = sb("xs", [128, NT, D]); b_xs = bufs(NT, "xs")
        xT = sb("xT", [128, KC, S], BF16); b_xT = bufs(NT, "xT")
        ps = [st.enter_context(nc.psum_tensor(f"ps{i}", [128, 512], F32)) for i in range(8)]
        b_ps = bufs(8, "ps")
        ident = sb("ident", [128, 128]); b_ident = Buf("ident")
        nbl = sb("nbl", [128, 896], BF16); b_mle = Buf("nbl")
        mlt = sb("mlt", [128, 896], BF16); b_mlt = Buf("mlt")
        negtri = sb("negtri", [128, 128], BF16); b_negtri = Buf("negtri")
        negid = sb("negid", [128, 128], BF16); b_negid = Buf("negid")
        negones = sb("negones", [128, 128], BF16); b_negones = Buf("negones")
        ones_bf = sb("ones_bf", [128, 128], BF16); b_ones_bf = Buf("ones_bf")
        ones_f = sb("ones_f", [128, 128]); b_ones_f = Buf("ones_f")
        iota16 = sb("iota16", [128, 16]); b_iota = Buf("iota")
        WST = 2
        wstg = [sb(f"wstg{i}", [128, KC, 128]) for i in range(WST)]; b_wstg = bufs(WST, "wstg")
        w_a = sb("w_a", [128, KC, 256], BF16); b_w_a = Buf("w_a")
        w_b = sb("w_b", [128, KC, 256], BF16); b_w_b = Buf("w_b")
        rgp = sb("rgp", [128, 2, 8]); b_rgp = Buf("rgp")
        rgc = sb("rgc", [128, 2, 4]); b_rgc = Buf("rgc")
        wabd = sb("wabd", [128, 2, 128]); wibd = sb("wibd", [128, 2, 128]); b_wbd = Buf("wbd")
        scp = sb("scp", [128, 2, 4]); b_scp = Buf("scp")
        mng = sb("mng", [128, 8]); b_mng = Buf("mng")
        foxbf = sb("foxbf", [4, 1]); b_foxbf = Buf("foxbf")
        lnst = sb("lnst", [128, 2, 6]); b_lnst = Buf("lnst")
        lnmv = sb("lnmv", [128, 8]); b_lnmv = Buf("lnmv")
        cst = sb("cst", [128, 4]); b_cst = Buf("cst")
        scr = sb("scr", [128, ARENA_BYTES // 4])

        def V(off, shape, dt=F32):
            esz = 4 if dt in (F32, U32) else 2
            n = 1
            for d_ in shape[1:]:
                n *= d_
            nb = n * esz
            assert off % 4 == 0 and nb % 4 == 0 and off + nb <= ARENA_BYTES, (off, nb)
            a = scr[:, off // 4:(off + nb) // 4]
            if dt != F32:
                a = a.bitcast(dt)
            if len(shape) == 3:
                a = a.rearrange("p (a b) -> p a b", a=shape[1])
            elif len(shape) == 4:
                a = a.rearrange("p (a b c) -> p a b c", a=shape[1], b=shape[2])
            if shape[0] != 128:
                a = a[0:shape[0]]
            return a

        wst_i = [0]
        wst_n = [WST]

        def load_w(dram2d, col0, ncols, dst, b_dst, eng_cast="pool"):
            done = 0
            while done < ncols:
                n = min(128, ncols - done)
                i = wst_i[0] % wst_n[0]
                wst_i[0] += 1
                src = dram2d[:, col0 + done:col0 + done + n].rearrange("(kc p) c -> p kc c", p=128)
                c.dma("sp", lambda e, i=i, n=n, src=src: e.dma_start(out=wstg[i][:, :, 0:n], in_=src), writes=[b_wstg[i]])
                if eng_cast == "act":
                    c.op("act", lambda e, i=i, n=n, done=done: e.activation(out=dst[:, :, done:done + n], in_=wstg[i][:, :, 0:n], func=AF.Copy),
                         reads=[b_wstg[i]], writes=[b_dst])
                else:
                    c.op(eng_cast, lambda e, i=i, n=n, done=done: e.tensor_copy(out=dst[:, :, done:done + n], in_=wstg[i][:, :, 0:n]),
                         reads=[b_wstg[i]], writes=[b_dst])
                done += n

        def load_rows(dram_rows, dst, b_dst, eng_cast="pool"):
            for hf in range(2):
                i = wst_i[0] % wst_n[0]
                wst_i[0] += 1
                stg = wstg[i][:].rearrange("p a b -> p (a b)").rearrange("p (a b) -> p a b", a=2)
                src = dram_rows[:, hf * 512:(hf + 1) * 512].rearrange("(c p) d -> p c d", p=128)
                c.dma("sp", lambda e, stg=stg, src=src: e.dma_start(out=stg, in_=src), writes=[b_wstg[i]])
                if eng_cast == "act":
                    c.op("act", lambda e, stg=stg, hf=hf: e.activation(out=dst[:, :, hf * 512:(hf + 1) * 512], in_=stg, func=AF.Copy),
                         reads=[b_wstg[i]], writes=[b_dst])
                else:
                    c.op(eng_cast, lambda e, stg=stg, hf=hf: e.tensor_copy(out=dst[:, :, hf * 512:(hf + 1) * 512], in_=stg),
                         reads=[b_wstg[i]], writes=[b_dst])

        cstage = V(0, [128, 896]); b_cstage = Buf("cstage")
        c.dma("sp", lambda e: e.dma_start(out=ident[:], in_=cid_d[:, :]), writes=[b_ident], sembuf=b_misc)
        c.dma("sp", lambda e: e.dma_start(out=iota16[:], in_=ciota_d[:, :]), writes=[b_iota], sembuf=b_misc)
        c.dma("sp", lambda e: e.dma_start(out=cstage, in_=cmle_d[:, :]), writes=[b_cstage], sembuf=b_misc)
        c.op("dve", lambda e: e.tensor_copy(out=nbl[:], in_=cstage), reads=[b_cstage], writes=[b_mle])
        c.dma("sp", lambda e: e.dma_start(out=cstage, in_=cmlt_d[:, :]), writes=[b_cstage], sembuf=b_misc)
        c.op("dve", lambda e: e.tensor_copy(out=mlt[:], in_=cstage), reads=[b_cstage], writes=[b_mlt])
        c.dma("sp", lambda e: e.dma_start(out=cstage[:, 0:128], in_=cntri_d[:, :]), writes=[b_cstage], sembuf=b_misc)
        c.op("dve", lambda e: e.tensor_copy(out=negtri[:], in_=cstage[:, 0:128]), reads=[b_cstage], writes=[b_negtri])
        c.op("dve", lambda e: e.tensor_scalar(out=negid[:], in0=ident[:], scalar1=-1.0, scalar2=None, op0=ALU.mult),
             reads=[b_ident], writes=[b_negid])
        c.op("dve", lambda e: e.memset(negones[:], -1.0), writes=[b_negones])
        c.op("dve", lambda e: e.memset(ones_bf[:], 1.0), writes=[b_ones_bf])
        c.op("dve", lambda e: e.memset(ones_f[:], 1.0), writes=[b_ones_f])
        c.op("dve", lambda e: e.memset(cst[:, 0:1], LN_EPS), writes=[b_cst])
        c.op("dve", lambda e: e.memset(cst[:, 1:2], 1e-6), writes=[b_cst])
        c.op("dve", lambda e: e.memset(cst[:, 2:3], 1.0), writes=[b_cst])
        EPS, EPS6, ONE = cst[:, 0:1], cst[:, 1:2], cst[:, 2:3]

        rr = [0]

        def evac_eng():
            rr[0] += 1
            return "act" if rr[0] % 2 == 0 else "dve"

        def copy_op(eng, out_ap, in_ap, reads, writes, scale=None):
            if eng == "act":
                if scale is None:
                    c.op("act", lambda e: e.activation(out=out_ap, in_=in_ap, func=AF.Copy), reads, writes)
                else:
                    c.op("act", lambda e: e.activation(out=out_ap, in_=in_ap, func=AF.Copy, scale=scale), reads, writes)
            else:
                if scale is None:
                    c.op(eng, lambda e: e.tensor_copy(out=out_ap, in_=in_ap), reads, writes)
                else:
                    c.op(eng, lambda e: e.tensor_scalar(out=out_ap, in0=in_ap, scalar1=scale, scalar2=None, op0=ALU.mult),
                         reads, writes)

        def mm(out_ap, lhsT, rhs, start, stop, reads, writes):
            c.op("pe", lambda e: e.matmul(out_ap, lhsT, rhs, start=start, stop=stop), reads, writes)

        def make_xT():
            for tt in range(NT):
                for g in range(2):
                    bk = (tt * 2 + g) % 8
                    for j in range(4):
                        kc = 4 * g + j
                        c.op("pe", lambda e, tt=tt, kc=kc, j=j, bk=bk: e.transpose(
                            ps[bk][:, j * 128:(j + 1) * 128], xs[:, tt, kc * 128:(kc + 1) * 128], ident[:]),
                            reads=[b_xs[tt], b_ident], writes=[b_ps[bk]])
                    copy_op(evac_eng(), xT[:, 4 * g:4 * g + 4, tt * 128:(tt + 1) * 128],
                            ps[bk][:].rearrange("p (j q) -> p j q", j=4), [b_ps[bk]], [b_xT[tt]])

        def projT(w_ap, b_w, M, q, bk, c0=0):
            for kc in range(KC):
                mm(ps[bk][0:M, :], w_ap[:, kc, c0:c0 + M], xT[:, kc, q * 512:(q + 1) * 512], kc == 0, kc == KC - 1,
                   [b_w] + b_xT[4 * q:4 * q + 4], [b_ps[bk]])

        lng = V(LN_OFF, [128, D]); lnb = V(LN_OFF + 4096, [128, D])

        def load_ln(l, which):
            b = b_lnp_shared
            c.dma("sp", [lambda e: e.dma_start(out=lng, in_=lng_d[l, which].partition_broadcast(128)),
                         lambda e: e.dma_start(out=lnb, in_=lnb_d[l, which].partition_broadcast(128))], writes=[b])
            return b

        def layer_norm(tts, b_lnp, aff_eng="dve", add_eng="dve"):
            for tt in tts:
                c.op("dve", lambda e, tt=tt: e.bn_stats(out=lnst[:, 0, :], in_=xs[:, tt, 0:512]),
                     reads=[b_xs[tt]], writes=[b_lnst])
                c.op("dve", lambda e, tt=tt: e.bn_stats(out=lnst[:, 1, :], in_=xs[:, tt, 512:1024]),
                     reads=[b_xs[tt]], writes=[b_lnst])
                c.op("dve", lambda e: e.bn_aggr(out=lnmv[:, 0:2], in_=lnst[:].rearrange("p a b -> p (a b)")),
                     reads=[b_lnst], writes=[b_lnmv])
                c.op("act", lambda e: e.activation(out=lnmv[:, 2:3], in_=lnmv[:, 1:2], func=AF.Sqrt, bias=EPS),
                     reads=[b_cst], writes=[b_lnmv])
                c.op("dve", lambda e: e.reciprocal(out=lnmv[:, 3:4], in_=lnmv[:, 2:3]), reads=[], writes=[b_lnmv])
                c.op("dve", lambda e: e.scalar_tensor_tensor(out=lnmv[:, 4:5], in0=lnmv[:, 0:1], scalar=-1.0,
                                                              in1=lnmv[:, 3:4], op0=ALU.mult, op1=ALU.mult),
                     reads=[], writes=[b_lnmv])
                c.op("act", lambda e, tt=tt: e.activation(out=xs[:, tt, :], in_=xs[:, tt, :], func=AF.Identity,
                                                          scale=lnmv[:, 3:4], bias=lnmv[:, 4:5]),
                     reads=[b_lnmv], writes=[b_xs[tt]])
                c.op(aff_eng, lambda e, tt=tt: e.tensor_tensor(out=xs[:, tt, :], in0=xs[:, tt, :], in1=lng, op=ALU.mult),
                     reads=[b_lnp], writes=[b_xs[tt]])
                c.op(add_eng, lambda e, tt=tt: e.tensor_tensor(out=xs[:, tt, :], in0=xs[:, tt, :], in1=lnb, op=ALU.add),
                     reads=[b_lnp], writes=[b_xs[tt]])

        def mixer(l):
            c.barrier()
            w_in_l = w_in_d[l]
            K1 = 1024
            Yg = V(0, [128, 2, S]); b_Yg = bufs(2, "Yg")
            yTn = V(16 * K1, [128, 2, S], BF16); b_yTn = Buf("yTn")
            woutg = V(24 * K1, [128, 2, D], BF16); b_woutg = Buf("woutg")
            rden = V(28 * K1, [128, 512]); b_rden = Buf("rden")
            TB = 30 * K1
            TW = 8256

            def out_proj_group(g, first):
                load_rows(w_out_d[l][g * 256:(g + 1) * 256, :], woutg, b_woutg)
                for tt in range(NT):
                    for h in range(2):
                        bk = (tt * 2 + h) % 8
                        for cc in range(2):
                            mm(ps[bk][:, :], yTn[:, cc, tt * 128:(tt + 1) * 128], woutg[:, cc, h * 512:(h + 1) * 512],
                               cc == 0, cc == 1, [b_yTn, b_woutg], [b_ps[bk]])
                        if first:
                            c.op("dve", lambda e, tt=tt, h=h, bk=bk: e.scalar_tensor_tensor(
                                out=xs[:, tt, h * 512:(h + 1) * 512], in0=xs[:, tt, h * 512:(h + 1) * 512], scalar=ALPHA,
                                in1=ps[bk][:, :], op0=ALU.mult, op1=ALU.add), reads=[b_ps[bk]], writes=[b_xs[tt]])
                        else:
                            c.op("dve", lambda e, tt=tt, h=h, bk=bk: e.tensor_tensor(
                                out=xs[:, tt, h * 512:(h + 1) * 512], in0=xs[:, tt, h * 512:(h + 1) * 512],
                                in1=ps[bk][:, :], op=ALU.add), reads=[b_ps[bk]], writes=[b_xs[tt]])

            def group_rms_and_proj(g, first, sqA, b_sqA, sqB, b_sqB):
                for q in range(NQ):
                    sl = slice(q * 512, (q + 1) * 512)
                    bk = q % 4
                    c.op("act", lambda e, sl=sl: e.activation(out=sqA[:, 0:512], in_=Yg[:, 0, sl], func=AF.Square),
                         reads=[b_Yg[0]], writes=[b_sqA])
                    c.op("act", lambda e, sl=sl: e.activation(out=sqB[:, 0:512], in_=Yg[:, 1, sl], func=AF.Square),
                         reads=[b_Yg[1]], writes=[b_sqB])
                    mm(ps[bk][:, :], ones_f[:], sqA[:, 0:512], True, False, [b_ones_f, b_sqA], [b_ps[bk]])
                    mm(ps[bk][:, :], ones_f[:], sqB[:, 0:512], False, True, [b_ones_f, b_sqB], [b_ps[bk]])
                    c.op("act", lambda e, bk=bk: e.activation(out=rden, in_=ps[bk][:, :], func=AF.Sqrt,
                                                              scale=1.0 / 256.0, bias=EPS6),
                         reads=[b_ps[bk], b_cst], writes=[b_rden])
                    c.op("dve", lambda e: e.reciprocal(out=rden, in_=rden), reads=[], writes=[b_rden])
                    for cc in range(2):
                        c.op("dve", lambda e, cc=cc, sl=sl: e.scalar_tensor_tensor(
                            out=yTn[:, cc, sl], in0=Yg[:, cc, sl], scalar=mng[:, 2 * g + cc:2 * g + cc + 1], in1=rden,
                            op0=ALU.mult, op1=ALU.mult), reads=[b_Yg[cc], b_mng, b_rden], writes=[b_yTn])
                out_proj_group(g, first)

            c.dma("sp", lambda e: e.dma_start(out=rgp[:], in_=rgp_d[l].rearrange("j p k -> p j k")), writes=[b_rgp], sembuf=b_misc)
            c.dma("sp", [lambda e: e.dma_start(out=wabd[:], in_=wa_d[l].rearrange("j p k -> p j k")),
                         lambda e: e.dma_start(out=wibd[:], in_=wi_d[l].rearrange("j p k -> p j k"))], writes=[b_wbd], sembuf=b_misc)
            c.dma("sp", lambda e: e.dma_start(out=scp[:], in_=scp_d[l].rearrange("j p k -> p j k")), writes=[b_scp], sembuf=b_misc)
            c.dma("sp", lambda e: e.dma_start(out=mng[:], in_=mng_d[l]), writes=[b_mng], sembuf=b_misc)
            c.dma("sp", lambda e: e.dma_start(out=foxbf[:], in_=foxbf_d[l]), writes=[b_foxbf], sembuf=b_misc)
            c.op("act", lambda e: e.activation(out=rgc[:, :, 2], in_=rgp[:, :, 7], func=AF.Exp, scale=-1.0),
                 reads=[b_rgp], writes=[b_rgc])
            c.op("act", lambda e: e.activation(out=rgc[:, :, 3], in_=rgc[:, :, 2], func=AF.Ln, bias=ONE),
                 reads=[b_cst], writes=[b_rgc])
            c.op("dve", lambda e: e.tensor_scalar(out=rgc[:, :, 0], in0=rgc[:, :, 3], scalar1=-8.0, scalar2=None, op0=ALU.mult),
                 reads=[], writes=[b_rgc])
            c.op("dve", lambda e: e.tensor_scalar(out=rgc[:, :, 1], in0=rgc[:, :, 3], scalar1=-16.0, scalar2=None, op0=ALU.mult),
                 reads=[], writes=[b_rgc])

            T1 = V(TB, [128, S + 4]); b_T1 = Buf("T1")
            T2 = V(TB + TW, [128, S + 4]); b_T2 = Buf("T2")
            T3 = V(TB + 2 * TW, [128, S + 4]); b_T3 = Buf("T3")
            T4 = V(TB + 3 * TW, [128, S + 4]); b_T4 = Buf("T4")

            for j in range(2):
                T5 = Yg[:, j, :]; b_T5 = b_Yg[j]
                load_w(w_in_l, OFF_RG_X + j * 128, 128, w_a, b_w_a)
                load_w(w_in_l, OFF_RG_G + j * 128, 128, w_b, b_w_b)
                c.op("pool", lambda e: e.memset(T1[:, 0:4], 0.0), writes=[b_T1])
                for q in range(NQ):
                    projT(w_a, b_w_a, 128, q, q)
                    copy_op("act", T1[:, 4 + q * 512:4 + (q + 1) * 512], ps[q][:, :], [b_ps[q]], [b_T1])
                c.op("dve", lambda e, j=j: e.tensor_scalar(out=T2[:, 0:S], in0=T1[:, 4:4 + S], scalar1=rgp[:, j, 3:4],
                                                           scalar2=rgp[:, j, 4:5], op0=ALU.mult, op1=ALU.add),
                     reads=[b_T1, b_rgp], writes=[b_T2])
                for k in range(3):
                    c.op("dve", lambda e, j=j, k=k: e.scalar_tensor_tensor(
                        out=T2[:, 0:S], in0=T1[:, 1 + k:1 + k + S], scalar=rgp[:, j, k:k + 1], in1=T2[:, 0:S],
                        op0=ALU.mult, op1=ALU.add), reads=[b_T1, b_rgp], writes=[b_T2])
                for q in range(NQ):
                    sl = slice(q * 512, (q + 1) * 512)
                    mm(ps[4 + q % 4][:, :], wabd[:, j, :], T2[:, sl], True, True, [b_wbd, b_T2], [b_ps[4 + q % 4]])
                    c.op("act", lambda e, j=j, q=q, sl=sl: e.activation(out=T3[:, sl], in_=ps[4 + q % 4][:, :], func=AF.Sigmoid,
                                                                       bias=rgp[:, j, 5:6]),
                         reads=[b_ps[4 + q % 4], b_rgp], writes=[b_T3])
                for q in range(NQ):
                    sl = slice(q * 512, (q + 1) * 512)
                    mm(ps[q % 4][:, :], wibd[:, j, :], T2[:, sl], True, True, [b_wbd, b_T2], [b_ps[q % 4]])
                    c.op("act", lambda e, j=j, q=q, sl=sl: e.activation(out=T4[:, sl], in_=ps[q % 4][:, :], func=AF.Sigmoid,
                                                                       bias=rgp[:, j, 6:7]),
                         reads=[b_ps[q % 4], b_rgp], writes=[b_T4])
                c.op("act", lambda e, j=j, T5=T5: e.activation(out=T5, in_=T3[:, 0:S], func=AF.Exp, scale=rgc[:, j, 0:1]),
                     reads=[b_T3, b_rgc], writes=[b_T5])
                c.op("act", lambda e, j=j: e.activation(out=T3[:, 0:S], in_=T3[:, 0:S], func=AF.Exp, scale=rgc[:, j, 1:2]),
                     reads=[b_rgc], writes=[b_T3])
                c.op("act", lambda e: e.activation(out=T3[:, 0:S], in_=T3[:, 0:S], func=AF.Sqrt, scale=-1.0, bias=ONE),
                     reads=[b_cst], writes=[b_T3])
                c.op("dve", lambda e: e.tensor_tensor(out=T4[:, 0:S], in0=T4[:, 0:S], in1=T3[:, 0:S], op=ALU.mult),
                     reads=[b_T3], writes=[b_T4])
                c.op("dve", lambda e: e.tensor_tensor(out=T4[:, 0:S], in0=T4[:, 0:S], in1=T2[:, 0:S], op=ALU.mult),
                     reads=[b_T2], writes=[b_T4])
                c.op("dve", lambda e, T5=T5: e.tensor_tensor_scan(out=T3[:, 0:S], data0=T5, data1=T4[:, 0:S], initial=0.0,
                                                                  op0=ALU.mult, op1=ALU.add),
                     reads=[b_T5, b_T4], writes=[b_T3])
                for q in range(NQ):
                    sl = slice(q * 512, (q + 1) * 512)
                    projT(w_b, b_w_b, 128, q, 4 + q % 4)
                    c.op("act", lambda e, q=q, sl=sl: e.activation(out=T2[:, sl], in_=ps[4 + q % 4][:, :], func=AF.Gelu_apprx_tanh),
                         reads=[b_ps[4 + q % 4]], writes=[b_T2])
                c.op("dve", lambda e, j=j: e.tensor_tensor(out=Yg[:, j, :], in0=T3[:, 0:S], in1=T2[:, 0:S], op=ALU.mult),
                     reads=[b_T3, b_T2], writes=[b_Yg[j]])
            group_rms_and_proj(0, True, T1, b_T1, T4, b_T4)

            qTh = V(30 * K1, [68, 4, S], BF16)
            kTh = V(46 * K1, [68, 4, S], BF16)
            Vt = V(62 * K1, [128, NT, 256], BF16)
            PB0 = 70 * K1
            NPB = 3
            Pb = [V(PB0 + i * K1, [128, 512], BF16) for i in range(NPB)]
            Eb = [V(PB0 + 3 * K1 + i * 2 * K1, [128, 512]) for i in range(2)]
            SPb = [V(PB0 + 7 * K1 + i * K1, [128, 512], BF16) for i in range(3)]
            Lpb = [V(PB0 + 10 * K1 + i * K1, [128, 512], BF16) for i in range(3)]
            Lcb = [V(PB0 + 13 * K1 + i * K1, [128, 512], BF16) for i in range(3)]
            for grp, off in ((1, OFF_FOX), (2, OFF_SB)):
                is_fox = grp == 1
                c.barrier()
                b_qTh = bufs(4, "qTh"); b_kTh = bufs(4, "kTh"); b_Vt = bufs(NT, "Vt")
                b_Pb = bufs(NPB, "Pb"); b_Eb = bufs(2, "Eb"); b_SPb = bufs(3, "SPb"); b_Lpb = bufs(3, "Lpb"); b_Lcb = bufs(3, "Lcb")
                b_Yg = bufs(2, "Yg"); b_yTn = Buf("yTn"); b_rden = Buf("rden"); b_woutg = Buf("woutg")
                if is_fox:
                    fl = V(30 * K1, [4, S]); fl2 = V(38 * K1, [4, S])
                    chi = V(46 * K1, [4, S], BF16); clo = V(50 * K1, [4, S], BF16)
                    nchi = V(54 * K1, [4, S], BF16); nclo = V(58 * K1, [4, S], BF16)
                    b_fl, b_fl2, b_chi, b_clo, b_nchi, b_nclo = (Buf("fl"), Buf("fl2"), Buf("chi"), Buf("clo"), Buf("nchi"), Buf("nclo"))
                    fones = V(62 * K1, [4, S]); b_fones = Buf("fones")
                    c.op("dve", lambda e: e.memset(fones, 1.0), writes=[b_fones])
                    load_w(w_in_l, OFF_FOX_F, 4, w_a, b_w_a)
                    for q in range(NQ):
                        sl = slice(q * 512, (q + 1) * 512)
                        projT(w_a, b_w_a, 4, q, q % 4)
                        c.op("dve", lambda e, q=q, sl=sl: e.tensor_scalar(out=fl[:, sl], in0=ps[q % 4][0:4, :], scalar1=foxbf[:, 0:1],
                                                                            scalar2=None, op0=ALU.add),
                             reads=[b_ps[q % 4], b_foxbf], writes=[b_fl])
                    c.op("act", lambda e: e.activation(out=fl, in_=fl, func=AF.Exp, scale=-1.0), reads=[], writes=[b_fl])
                    c.op("act", lambda e: e.activation(out=fl, in_=fl, func=AF.Ln, bias=cst[0:4, 2:3]),
                         reads=[b_cst], writes=[b_fl])
                    c.op("dve", lambda e: e.tensor_tensor_scan(out=fl2, data0=fones, data1=fl, initial=0.0,
                                                               op0=ALU.mult, op1=ALU.add),
                         reads=[b_fl, b_fones], writes=[b_fl2])
                    c.op("dve", lambda e: e.tensor_copy(out=nchi, in_=fl2), reads=[b_fl2], writes=[b_nchi])
                    c.op("dve", lambda e: e.tensor_tensor(out=nclo, in0=fl2, in1=nchi, op=ALU.subtract),
                         reads=[b_fl2, b_nchi], writes=[b_nclo])
                    c.op("dve", lambda e: e.tensor_scalar(out=chi, in0=nchi, scalar1=-1.0, scalar2=None, op0=ALU.mult),
                         reads=[b_nchi], writes=[b_chi])
                    c.op("dve", lambda e: e.tensor_scalar(out=clo, in0=nclo, scalar1=-1.0, scalar2=None, op0=ALU.mult),
                         reads=[b_nclo], writes=[b_clo])
                    for h in range(4):
                        c.op("dve", lambda e, h=h: e.memset(qTh[64:68, h, :], 1.0), writes=[b_qTh[h]])
                        c.op("dve", lambda e, h=h: e.memset(kTh[64:68, h, :], 1.0), writes=[b_kTh[h]])
                        c.dma("sp", [lambda e, h=h: e.dma_start(out=qTh[64:65, h, :], in_=chi[h:h + 1, :]),
                                     lambda e, h=h: e.dma_start(out=qTh[65:66, h, :], in_=clo[h:h + 1, :])],
                              reads=[b_chi, b_clo], writes=[b_qTh[h]], sembuf=b_misc)
                        c.dma("sp", [lambda e, h=h: e.dma_start(out=kTh[66:67, h, :], in_=nchi[h:h + 1, :]),
                                     lambda e, h=h: e.dma_start(out=kTh[67:68, h, :], in_=nclo[h:h + 1, :])],
                              reads=[b_nchi, b_nclo], writes=[b_kTh[h]], sembuf=b_misc)
                    c.barrier()
                KR = 68 if is_fox else 64
                for which, dst, b_dst, coff, scl in ((0, qTh, b_qTh, off, 0.125), (1, kTh, b_kTh, off + GW, None)):
                    for hp in range(2):
                        load_w(w_in_l, coff + hp * 128, 128, w_a, b_w_a)
                        for hh in range(2):
                            h = hp * 2 + hh
                            for q in range(NQ):
                                bk = (h * NQ + q) % 8
                                projT(w_a, b_w_a, 64, q, bk, c0=hh * 64)
                                copy_op(evac_eng(), dst[0:64, h, q * 512:(q + 1) * 512], ps[bk][0:64, :], [b_ps[bk]], [b_dst[h]],
                                        scale=scl)
                load_w(w_in_l, off + 2 * GW, 256, w_b, b_w_b)
                for tt in range(NT):
                    bk = tt % 8
                    for kc in range(KC):
                        mm(ps[bk][:, 0:256], xT[:, kc, tt * 128:(tt + 1) * 128], w_b[:, kc, 0:256], kc == 0, kc == KC - 1,
                           [b_w_b, b_xT[tt]], [b_ps[bk]])
                    copy_op(evac_eng(), Vt[:, tt, :], ps[bk][:, 0:256], [b_ps[bk]], [b_Vt[tt]])
                pairs = []
                for h in range(4):
                    for q in range(NQ):
                        nA = 4 * q + 4
                        order = list(range(nA)) if is_fox else list(range(nA - 1, -1, -1))
                        for idx, A in enumerate(order):
                            pairs.append((h, q, idx, A, nA))

                def geom(m):
                    h, q, idx, A, nA = pairs[m]
                    diag = A >= 4 * q
                    moff = 384 - 128 * (A - 4 * q)
                    par = (h * NQ + q) % 2
                    return dict(h=h, q=q, idx=idx, A=A, nA=nA, first=idx == 0, last=idx == nA - 1, diag=diag, moff=moff,
                                ks=slice(A * 128, (A + 1) * 128), qs=slice(q * 512, (q + 1) * 512),
                                sbk=m % 3, pb=m % 3, eb=m % 2, s3=m % 3, wbk=3 + (m % 2), par=par, lc=idx % 3)

                def st_score(m):
                    g = geom(m)
                    h, sbk = g["h"], g["sbk"]
                    mm(ps[sbk][:, :], kTh[0:KR, h, g["ks"]], qTh[0:KR, h, g["qs"]], True, True, [b_kTh[h], b_qTh[h]], [b_ps[sbk]])
                    if is_fox and g["diag"]:
                        mo = g["moff"]
                        c.op("dve", lambda e: e.tensor_tensor(out=ps[sbk][:, :], in0=ps[sbk][:, :], in1=nbl[:, mo:mo + 512], op=ALU.add),
                             reads=[b_mle], writes=[b_ps[sbk]])

                def st_fox_exp(m):
                    g = geom(m)
                    sbk, pb = g["sbk"], g["pb"]
                    c.op("act", lambda e: e.activation(out=Pb[pb], in_=ps[sbk][:, :], func=AF.Exp), reads=[b_ps[sbk]], writes=[b_Pb[pb]])

                def st_fox_pv(m):
                    g = geom(m)
                    h, q, A, pb, par = g["h"], g["q"], g["A"], g["pb"], g["par"]
                    cc, hp = h // 2, h % 2
                    vsl = slice(cc * 128, (cc + 1) * 128)
                    prt = slice(hp * 64, (hp + 1) * 64)
                    ob, db = 4 + par, 6 + par
                    mm(ps[ob][:, :], Vt[:, A, vsl], Pb[pb], g["first"], g["last"], [b_Vt[A], b_Pb[pb]], [b_ps[ob]])
                    mm(ps[db][:, :], ones_bf[:], Pb[pb], g["first"], g["last"], [b_ones_bf, b_Pb[pb]], [b_ps[db]])
                    if g["last"]:
                        qs = g["qs"]
                        c.op("dve", lambda e: e.reciprocal(out=rden, in_=ps[db][:, :]), reads=[b_ps[db]], writes=[b_rden])
                        c.op("dve", lambda e: e.tensor_tensor(out=Yg[prt, cc, qs], in0=ps[ob][prt, :], in1=rden[prt, :], op=ALU.mult),
                             reads=[b_ps[ob], b_rden], writes=[b_Yg[cc]])

                def st_sb_elem(m):
                    g = geom(m)
                    sbk, eb, s3, lc = g["sbk"], g["eb"], g["s3"], g["lc"]
                    c.op("act", lambda e: e.activation(out=Eb[eb], in_=ps[sbk][:, :], func=AF.Exp, scale=-1.0), reads=[b_ps[sbk]], writes=[b_Eb[eb]])
                    c.op("act", lambda e: e.activation(out=SPb[s3], in_=Eb[eb], func=AF.Ln, bias=ONE), reads=[b_Eb[eb], b_cst], writes=[b_SPb[s3]])
                    c.op("dve", lambda e: e.tensor_tensor(out=Lpb[s3], in0=ps[sbk][:, :], in1=SPb[s3], op=ALU.add),
                         reads=[b_ps[sbk], b_SPb[s3]], writes=[b_Lpb[s3]])
                    if g["diag"]:
                        mo = g["moff"]
                        c.op("pool", lambda e: e.tensor_tensor(out=Lpb[s3], in0=Lpb[s3], in1=mlt[:, mo:mo + 512], op=ALU.mult),
                             reads=[b_mlt], writes=[b_Lpb[s3]])
                    if not g["last"]:
                        nx = (lc + 1) % 3
                        if g["first"]:
                            c.op("dve", lambda e: e.tensor_copy(out=Lcb[nx], in_=Lpb[s3]), reads=[b_Lpb[s3]], writes=[b_Lcb[nx]])
                        else:
                            c.op("dve", lambda e: e.tensor_tensor(out=Lcb[nx], in0=Lcb[lc], in1=Lpb[s3], op=ALU.add),
                                 reads=[b_Lcb[lc], b_Lpb[s3]], writes=[b_Lcb[nx]])

                def st_sb_w(m):
                    g = geom(m)
                    s3, wbk, lc, first = g["s3"], g["wbk"], g["lc"], g["first"]
                    mm(ps[wbk][:, :], negid[:], SPb[s3], True, False, [b_negid, b_SPb[s3]], [b_ps[wbk]])
                    mm(ps[wbk][:, :], negtri[:], Lpb[s3], False, first, [b_negtri, b_Lpb[s3]], [b_ps[wbk]])
                    if not first:
                        mm(ps[wbk][:, :], negones[:], Lcb[lc], False, True, [b_negones, b_Lcb[lc]], [b_ps[wbk]])

                def st_sb_expw(m):
                    g = geom(m)
                    wbk, pb = g["wbk"], g["pb"]
                    c.op("act", lambda e: e.activation(out=Pb[pb], in_=ps[wbk][:, :], func=AF.Exp), reads=[b_ps[wbk]], writes=[b_Pb[pb]])
                    if g["diag"]:
                        mo = g["moff"]
                        c.op("pool", lambda e: e.tensor_tensor(out=Pb[pb], in0=Pb[pb], in1=mlt[:, mo:mo + 512], op=ALU.mult),
                             reads=[b_mlt], writes=[b_Pb[pb]])

                def st_sb_pv(m):
                    g = geom(m)
                    h, q, A, pb, par = g["h"], g["q"], g["A"], g["pb"], g["par"]
                    cc, hp = h // 2, h % 2
                    vsl = slice(cc * 128, (cc + 1) * 128)
                    prt = slice(hp * 64, (hp + 1) * 64)
                    ob = 5 + par
                    mm(ps[ob][:, :], Vt[:, A, vsl], Pb[pb], g["first"], g["last"], [b_Vt[A], b_Pb[pb]], [b_ps[ob]])
                    if g["last"]:
                        qs = g["qs"]
                        c.op("act", lambda e: e.activation(out=Yg[prt, cc, qs], in_=ps[ob][prt, :], func=AF.Copy), reads=[b_ps[ob]], writes=[b_Yg[cc]])

                stages = [st_score, st_fox_exp, st_fox_pv] if is_fox else [st_score, st_sb_elem, st_sb_w, st_sb_expw, st_sb_pv]
                NP_ = len(pairs)
                for n in range(NP_ + len(stages) - 1):
                    for k in range(len(stages) - 1, -1, -1):
                        m = n - k
                        if 0 <= m < NP_:
                            stages[k](m)
                group_rms_and_proj(grp, False, Eb[0], b_Eb[0], Eb[1], b_Eb[1])

            c.barrier()
            b_T1, b_T2, b_T3, b_T4 = Buf("T1"), Buf("T2"), Buf("T3"), Buf("T4")
            b_Yg = bufs(2, "Yg"); b_yTn = Buf("yTn"); b_rden = Buf("rden"); b_woutg = Buf("woutg")
            for j in range(2):
                load_w(w_in_l, OFF_SC + GW + j * 128, 128, w_a, b_w_a)
                load_w(w_in_l, OFF_SC + 2 * GW + j * 128, 128, w_b, b_w_b)
                for q in range(NQ):
                    sl = slice(q * 512, (q + 1) * 512)
                    projT(w_a, b_w_a, 128, q, q % 4)
                    copy_op("act", T3[:, sl], ps[q % 4][:, :], [b_ps[q % 4]], [b_T3])
                c.op("pool", lambda e: e.memset(T1[:, 0:4], 0.0), writes=[b_T1])
                for q in range(NQ):
                    projT(w_b, b_w_b, 128, q, 4 + q % 4)
                    c.op("dve", lambda e, q=q: e.tensor_tensor(out=T1[:, 4 + q * 512:4 + (q + 1) * 512], in0=ps[4 + q % 4][:, :],
                                                               in1=T3[:, q * 512:(q + 1) * 512], op=ALU.mult),
                         reads=[b_ps[4 + q % 4], b_T3], writes=[b_T1])
                c.op("dve", lambda e, j=j: e.tensor_scalar(out=T2[:, 0:S], in0=T1[:, 4:4 + S], scalar1=scp[:, j, 2:3], scalar2=None,
                                                           op0=ALU.mult), reads=[b_T1, b_scp], writes=[b_T2])
                for k in range(2):
                    c.op("dve", lambda e, j=j, k=k: e.scalar_tensor_tensor(
                        out=T2[:, 0:S], in0=T1[:, 2 + k:2 + k + S], scalar=scp[:, j, k:k + 1], in1=T2[:, 0:S],
                        op0=ALU.mult, op1=ALU.add), reads=[b_T1, b_scp], writes=[b_T2])
                load_w(w_in_l, OFF_SC + j * 128, 128, w_a, b_w_a)
                for q in range(NQ):
                    projT(w_a, b_w_a, 128, q, q % 4)
                    c.op("dve", lambda e, q=q, j=j: e.tensor_tensor(out=Yg[:, j, q * 512:(q + 1) * 512], in0=ps[q % 4][:, :],
                                                                    in1=T2[:, q * 512:(q + 1) * 512], op=ALU.mult),
                         reads=[b_ps[q % 4], b_T2], writes=[b_Yg[j]])
            group_rms_and_proj(3, False, T3, b_T3, T4, b_T4)
            c.barrier()

        def xattn(l, s):
            c.barrier()
            K1 = 1024
            wqx = V(0, [128, KC, D], BF16); b_wqx = Buf("wqx")
            woh = V(16 * K1, [128, 2, D], BF16); b_woh = Buf("woh")
            KT = V(20 * K1, [128, 8, NMEM], BF16); b_KT = Buf("KT")
            Vx = V(24 * K1, [128, 2, D], BF16); b_Vx = Buf("Vx")
            memf = V(28 * K1, [128, KC, NMEM]); b_memf = Buf("memf")
            memb = V(36 * K1, [128, KC, NMEM], BF16); b_memb = Buf("memb")
            qTx = V(40 * K1, [128, 2, 512], BF16); b_qTx = Buf("qTx")
            pTx = [V(42 * K1 + i * K1, [128, 512], BF16) for i in range(2)]; b_pTx = bufs(2, "pTx")
            oTn = V(46 * K1, [128, 2, 512], BF16); b_oTn = Buf("oTn")
            rdx = V(48 * K1, [128, 512]); b_rdx = Buf("rdx")
            load_w(xwq_d[l], 0, D, wqx, b_wqx, eng_cast="act")
            c.dma("sp", lambda e: e.dma_start(out=memf, in_=memT_d[s].rearrange("(kc p) m -> p kc m", p=128)), writes=[b_memf], sembuf=b_misc)
            c.op("act", lambda e: e.activation(out=memb, in_=memf, func=AF.Copy), reads=[b_memf], writes=[b_memb])
            for cch in range(8):
                load_w(xwkv_d[l], cch * 128, 128, w_a, b_w_a, eng_cast="act")
                bk = cch % 8
                for kc in range(KC):
                    mm(ps[bk][:, 0:NMEM], w_a[:, kc, 0:128], memb[:, kc, :], kc == 0, kc == KC - 1, [b_w_a, b_memb], [b_ps[bk]])
                copy_op(evac_eng(), KT[:, cch, :], ps[bk][:, 0:NMEM], [b_ps[bk]], [b_KT])
            for cch in range(8):
                load_w(xwkv_d[l], D + cch * 128, 128, w_b, b_w_b, eng_cast="act")
                for mt in range(2):
                    bk = (cch * 2 + mt) % 8
                    for kc in range(KC):
                        mm(ps[bk][:, 0:128], memb[:, kc, mt * 128:(mt + 1) * 128], w_b[:, kc, 0:128], kc == 0, kc == KC - 1,
                           [b_w_b, b_memb], [b_ps[bk]])
                    copy_op(evac_eng(), Vx[:, mt, cch * 128:(cch + 1) * 128], ps[bk][:, 0:128], [b_ps[bk]], [b_Vx])
            woh2 = [woh, V(50 * K1, [128, 2, D], BF16)]; b_woh2 = bufs(2, "woh2")
            qTx2 = [qTx, V(54 * K1, [128, 2, 512], BF16)]; b_qTx2 = bufs(2, "qTx2")
            pTx2 = [[V(56 * K1 + (2 * i + j) * K1, [128, 512], BF16) for j in range(2)] for i in range(2)]
            b_pTx2 = [bufs(2, "pTxa"), bufs(2, "pTxb")]
            oTn2 = [oTn, V(60 * K1, [128, 2, 512], BF16)]; b_oTn2 = bufs(2, "oTn2")
            its = [(h, q) for h in range(4) for q in range(NQ)]

            def stA(i):
                h, q = its[i]
                if q == 0:
                    load_rows(xwo_d[l][h * 256:(h + 1) * 256, :], woh2[h % 2], b_woh2[h % 2], eng_cast="act")
                for c2 in range(2):
                    projT(wqx, b_wqx, 128, q, 0, c0=h * 256 + c2 * 128)
                    copy_op(evac_eng(), qTx2[i % 2][:, c2, :], ps[0][:, :], [b_ps[0]], [b_qTx2[i % 2]], scale=1.0 / 16.0)

            def stB(i):
                h, q = its[i]
                for mt in range(2):
                    for c2 in range(2):
                        mm(ps[1 + mt][:, :], KT[:, 2 * h + c2, mt * 128:(mt + 1) * 128], qTx2[i % 2][:, c2, :], c2 == 0, c2 == 1,
                           [b_KT, b_qTx2[i % 2]], [b_ps[1 + mt]])
                    c.op("act", lambda e, mt=mt: e.activation(out=pTx2[i % 2][mt], in_=ps[1 + mt][:, :], func=AF.Exp),
                         reads=[b_ps[1 + mt]], writes=[b_pTx2[i % 2][mt]])

            def stC(i):
                h, q = its[i]
                pt, bpt = pTx2[i % 2], b_pTx2[i % 2]
                for mt in range(2):
                    mm(ps[3][:, :], ones_bf[:], pt[mt], mt == 0, mt == 1, [b_ones_bf, bpt[mt]], [b_ps[3]])
                for c2 in range(2):
                    for mt in range(2):
                        mm(ps[4 + c2][:, :], Vx[:, mt, h * 256 + c2 * 128:h * 256 + (c2 + 1) * 128], pt[mt], mt == 0, mt == 1,
                           [b_Vx, bpt[mt]], [b_ps[4 + c2]])
                c.op("dve", lambda e: e.reciprocal(out=rdx, in_=ps[3][:, :]), reads=[b_ps[3]], writes=[b_rdx])
                for c2 in range(2):
                    c.op("dve", lambda e, c2=c2: e.tensor_tensor(out=oTn2[i % 2][:, c2, :], in0=ps[4 + c2][:, :], in1=rdx, op=ALU.mult),
                         reads=[b_ps[4 + c2], b_rdx], writes=[b_oTn2[i % 2]])

            def stD(i):
                h, q = its[i]
                for tsub in range(4):
                    tt = 4 * q + tsub
                    for hf in range(2):
                        bk = 6 + (tsub * 2 + hf) % 2
                        for c2 in range(2):
                            mm(ps[bk][:, :], oTn2[i % 2][:, c2, tsub * 128:(tsub + 1) * 128], woh2[h % 2][:, c2, hf * 512:(hf + 1) * 512],
                               c2 == 0, c2 == 1, [b_oTn2[i % 2], b_woh2[h % 2]], [b_ps[bk]])
                        if h == 0:
                            c.op("dve", lambda e, tt=tt, hf=hf, bk=bk: e.scalar_tensor_tensor(
                                out=xs[:, tt, hf * 512:(hf + 1) * 512], in0=xs[:, tt, hf * 512:(hf + 1) * 512], scalar=ALPHA,
                                in1=ps[bk][:, :], op0=ALU.mult, op1=ALU.add), reads=[b_ps[bk]], writes=[b_xs[tt]])
                        else:
                            c.op("dve", lambda e, tt=tt, hf=hf, bk=bk: e.tensor_tensor(
                                out=xs[:, tt, hf * 512:(hf + 1) * 512], in0=xs[:, tt, hf * 512:(hf + 1) * 512],
                                in1=ps[bk][:, :], op=ALU.add), reads=[b_ps[bk]], writes=[b_xs[tt]])

            xst = [stA, stB, stC, stD]
            NI = len(its)
            for n in range(NI + len(xst) - 1):
                for k in range(len(xst) - 1, -1, -1):
                    i = n - k
                    if 0 <= i < NI:
                        xst[k](i)
            c.barrier()

        def peer(l, s, is_last):
            c.barrier()
            K1 = 1024
            s_all = V(0, [128, 2, 16, 128]); b_sall = bufs(2, "sall")
            s_all_u = V(0, [128, 2, 16, 128], U32)
            idx_all = V(16 * K1, [128, 4, 128], U32); b_idx = bufs(4, "idx")
            gate_all = V(18 * K1, [128, 4, 128]); b_gate = bufs(4, "gate")
            NGA = 8
            NG = NGA + 2
            G = [V(20 * K1 + i * 4 * K1, [128, 2 * D], BF16) for i in range(NGA)]
            G.append(w_b[:].rearrange("p a b -> p (a b)"))
            G.append(wstg[1][:].rearrange("p a b -> p (a b)").bitcast(BF16))
            wst_n[0] = 1
            o = 20 * K1 + NGA * 4 * K1
            qTp = V(o, [128, 256], BF16); b_qTp = Buf("qTp"); o += 512
            o_qtp2 = o; o += 512
            kTb = V(o, [128, 2, 128], BF16); b_kTb = Buf("kTb"); o += 512
            kst = V(o, [128, 2, 128]); b_kst = Buf("kst"); o += 1024
            xb = V(o, [128, D], BF16); b_xb = Buf("xb"); o += 2048
            Vt_ = V(o, [128, 8, 2, 16]); Vt_u = V(o, [128, 8, 2, 16], U32); b_V = Buf("V"); o += 1024
            tmp128 = V(o, [128, 128]); b_tmp = Buf("tmp128"); o += 512
            I12u = V(o, [128, 8, 2, 16], U32); b_I12u = Buf("I12u"); o += 1024
            I12f = V(o, [128, 8, 2, 16], BF16); b_I12f = Buf("I12f"); o += 512
            iota16b = V(o, [128, 16], BF16); b_iota16b = Buf("iota16b"); o += 32
            e12b = V(o, [128, 8, 2, 16], BF16); o += 512
            cand2 = V(o, [128, 256]); b_cand2 = Buf("cand2"); o += 1024
            SC = V(o, [128, 8, 16]); SCu = V(o, [128, 8, 16], U32); b_SC = Buf("SC"); o += 512
            abu = V(o, [128, 8, 2, 16], U32); b_abu = Buf("abu"); o += 1024
            abf = V(o, [128, 8, 2, 16], BF16); b_abf = Buf("abf"); o += 512
            oh = V(o, [128, 8, 16, 16], BF16); b_oh = Buf("oh"); o += 4096
            e12 = V(o, [128, 8, 2, 16]); b_e12 = Buf("e12"); o += 1024
            exf = V(o, [128, 8, 16]); b_exf = Buf("exf"); o += 512
            exg = V(o, [128, 8, 16]); b_exg = Buf("exg"); o += 512
            sm = V(o, [128, 2, 8]); b_sm = Buf("sm"); o += 64
            actt = V(o, [128, 128]); b_act = bufs(64, "act"); o += 512
            coef = V(o, [128, 128]); b_coef = bufs(64, "coef"); o += 512
            NDG = 8
            dg = [V(o + i * 256, [128, 128], BF16) for i in range(NDG)]; b_dg = bufs(NDG, "dg"); o += NDG * 256
            identb = V(o, [128, 128], BF16); b_identb = Buf("identb"); o += 256
            io128u = V(o, [128, 128], U32); b_io128 = Buf("io128"); o += 512
            io256u = V(o, [128, 256], U32); b_io256 = Buf("io256"); o += 1024
            assert o <= LN_OFF, o
            c.op("dve", lambda e: e.tensor_copy(out=identb, in_=ident[:]), reads=[b_ident], writes=[b_identb])
            c.op("dve", lambda e: e.tensor_copy(out=iota16b, in_=iota16[:]), reads=[b_iota], writes=[b_iota16b])
            c.dma("sp", lambda e: e.dma_start(out=cand2, in_=ciota256_d[:, :]), writes=[b_cand2], sembuf=b_misc)
            c.op("dve", lambda e: e.tensor_copy(out=io256u, in_=cand2), reads=[b_cand2], writes=[b_io256])
            c.op("dve", lambda e: e.tensor_copy(out=io128u, in_=cand2[:, 0:128]), reads=[b_cand2], writes=[b_io128])
            b_lnp = load_ln(l, 2)
            c.dma("sp", lambda e: e.dma_start(out=kst, in_=k12T_d[l]), writes=[b_kst], sembuf=b_misc)
            c.op("dve", lambda e: e.tensor_copy(out=kTb, in_=kst), reads=[b_kst], writes=[b_kTb])
            out_evs = []

            b_wa2 = bufs(2, "wa2")
            b_qTp2 = bufs(2, "qTp2")
            qTp2 = [qTp, V(o_qtp2, [128, 256], BF16)]

            def score_thunks(blk):
                def t1(ch):
                    wv = w_a[:, :, (ch % 2) * 128:(ch % 2 + 1) * 128]
                    load_w(pwq_d[l], ch * 128, 128, wv, b_wa2[ch % 2], eng_cast="act")

                def t2(ch):
                    wv = w_a[:, :, (ch % 2) * 128:(ch % 2 + 1) * 128]
                    for kc in range(KC):
                        mm(ps[0][:, 0:256], wv[:, kc, :], xT[:, kc, blk * 256:(blk + 1) * 256], kc == 0, kc == KC - 1,
                           [b_wa2[ch % 2]] + b_xT[2 * blk:2 * blk + 2], [b_ps[0]])
                    copy_op("act", qTp2[ch % 2], ps[0][:, 0:256], [b_ps[0]], [b_qTp2[ch % 2]])

                def t3(ch):
                    for ti in range(2):
                        bk2 = 1 + ti
                        mm(ps[bk2][:, 0:128], qTp2[ch % 2][:, ti * 128:(ti + 1) * 128], kTb[:, ch % 2, :], True, True,
                           [b_qTp2[ch % 2], b_kTb], [b_ps[bk2]])
                        copy_op("act", s_all[:, ti, ch, :], ps[bk2][:, 0:128], [b_ps[bk2]], [b_sall[ti]])

                th = []
                for i in range(18):
                    def one(i=i):
                        if i - 2 >= 0:
                            t3(i - 2)
                        if 0 <= i - 1 < 16:
                            t2(i - 1)
                        if i < 16:
                            t1(i)
                    th.append(one)
                    th.append(lambda: None)
                return th

            def routing_thunks(blk):
                th = []
                A = th.append
                for ti in range(2):
                    tt = blk * 2 + ti
                    t4 = tt % 4
                    bs = b_sall[ti]
                    Su = s_all_u[:, ti].rearrange("p a b -> p (a b)")
                    S3u = s_all_u[:, ti]
                    A(lambda Su=Su, bs=bs: c.op("dve", lambda e: e.tensor_single_scalar(out=Su, in_=Su, scalar=0xFFFFFF80, op=ALU.bitwise_and),
                                                reads=[], writes=[bs]))
                    A(lambda S3u=S3u, bs=bs: c.op("dve", lambda e: e.tensor_tensor(out=S3u, in0=S3u, in1=io128u.unsqueeze(1).to_broadcast([128, 16, 128]),
                                                                                     op=ALU.bitwise_or), reads=[b_io128], writes=[bs]))
                    for ch in range(16):
                        h, sd = ch // 2, ch % 2
                        sv = s_all[:, ti, ch, :]
                        A(lambda sv=sv, h=h, sd=sd, bs=bs: c.op("dve", lambda e: e.max(out=Vt_[:, h, sd, 0:8], in_=sv), reads=[bs], writes=[b_V]))
                        A(lambda sv=sv, h=h, sd=sd, bs=bs: c.op("dve", lambda e: e.match_replace(out=tmp128, in_to_replace=Vt_[:, h, sd, 0:8], in_values=sv, imm_value=NEG),
                                                                 reads=[bs, b_V], writes=[b_tmp]))
                        A(lambda h=h, sd=sd: c.op("dve", lambda e: e.max(out=Vt_[:, h, sd, 8:16], in_=tmp128), reads=[b_tmp], writes=[b_V]))
                    A(lambda: c.op("dve", lambda e: e.tensor_single_scalar(out=I12u, in_=Vt_u, scalar=127, op=ALU.bitwise_and), reads=[b_V], writes=[b_I12u]))
                    A(lambda: c.op("dve", lambda e: e.tensor_copy(out=I12f, in_=I12u), reads=[b_I12u], writes=[b_I12f]))
                    cand4 = s_all[:, ti].rearrange("p a b -> p (a b)").rearrange("p (h a b) -> p h a b", h=8, a=16)
                    cand3 = s_all[:, ti].rearrange("p a b -> p (a b)").rearrange("p (h q) -> p h q", h=8)
                    cand3u = s_all_u[:, ti].rearrange("p a b -> p (a b)").rearrange("p (h q) -> p h q", h=8)
                    A(lambda cand4=cand4, bs=bs: c.op("dve", lambda e: e.tensor_tensor(
                        out=cand4, in0=Vt_[:, :, 0, :].unsqueeze(3).to_broadcast([128, 8, 16, 16]),
                        in1=Vt_[:, :, 1, :].unsqueeze(2).to_broadcast([128, 8, 16, 16]), op=ALU.add), reads=[b_V], writes=[bs]))
                    A(lambda cand3u=cand3u, bs=bs: c.op("dve", lambda e: e.tensor_single_scalar(out=cand3u, in_=cand3u, scalar=0xFFFFFF00, op=ALU.bitwise_and),
                                                        reads=[], writes=[bs]))
                    A(lambda cand3u=cand3u, bs=bs: c.op("dve", lambda e: e.tensor_tensor(out=cand3u, in0=cand3u, in1=io256u.unsqueeze(1).to_broadcast([128, 8, 256]),
                                                                                         op=ALU.bitwise_or), reads=[b_io256], writes=[bs]))
                    for h in range(8):
                        cv = cand3[:, h, :]
                        A(lambda cv=cv, h=h, bs=bs: c.op("dve", lambda e: e.max(out=SC[:, h, 0:8], in_=cv), reads=[bs], writes=[b_SC]))
                        A(lambda cv=cv, h=h, bs=bs: c.op("dve", lambda e: e.match_replace(out=cand2, in_to_replace=SC[:, h, 0:8], in_values=cv, imm_value=NEG),
                                                         reads=[bs, b_SC], writes=[b_cand2]))
                        A(lambda h=h: c.op("dve", lambda e: e.max(out=SC[:, h, 8:16], in_=cand2), reads=[b_cand2], writes=[b_SC]))
                    A(lambda: c.op("dve", lambda e: e.tensor_scalar(out=abu[:, :, 0, :], in0=SCu, scalar1=255, scalar2=4, op0=ALU.bitwise_and,
                                                                    op1=ALU.logical_shift_right), reads=[b_SC], writes=[b_abu]))
                    A(lambda: c.op("dve", lambda e: e.tensor_single_scalar(out=abu[:, :, 1, :], in_=SCu, scalar=15, op=ALU.bitwise_and), reads=[b_SC], writes=[b_abu]))
                    A(lambda: c.op("dve", lambda e: e.tensor_copy(out=abf, in_=abu), reads=[b_abu], writes=[b_abf]))
                    for sd in range(2):
                        A(lambda sd=sd: c.op("dve", lambda e: e.tensor_tensor(
                            out=oh, in0=abf[:, :, sd, :].unsqueeze(3).to_broadcast([128, 8, 16, 16]),
                            in1=iota16b.unsqueeze(1).unsqueeze(1).to_broadcast([128, 8, 16, 16]), op=ALU.is_equal), reads=[b_abf, b_iota16b], writes=[b_oh]))
                        A(lambda sd=sd: c.op("dve", lambda e: e.tensor_tensor(
                            out=oh, in0=oh, in1=I12f[:, :, sd, :].unsqueeze(2).to_broadcast([128, 8, 16, 16]), op=ALU.mult), reads=[b_I12f], writes=[b_oh]))
                        A(lambda sd=sd: c.op("dve", lambda e: e.reduce_sum(out=e12[:, :, sd, :], in_=oh, axis=AX.X), reads=[b_oh], writes=[b_e12]))
                    A(lambda: c.op("dve", lambda e: e.scalar_tensor_tensor(out=exf, in0=e12[:, :, 0, :], scalar=128.0, in1=e12[:, :, 1, :],
                                                                           op0=ALU.mult, op1=ALU.add), reads=[b_e12], writes=[b_exf]))
                    A(lambda t4=t4: c.op("dve", lambda e: e.tensor_copy(out=idx_all[:, t4, :].rearrange("p (h k) -> p h k", h=8), in_=exf),
                                         reads=[b_exf], writes=[b_idx[t4]]))
                    A(lambda: c.op("dve", lambda e: e.tensor_tensor(out=exg, in0=SC, in1=SC[:, :, 0:1].to_broadcast([128, 8, 16]), op=ALU.subtract),
                                   reads=[b_SC], writes=[b_exg]))
                    A(lambda: c.op("act", lambda e: e.activation(out=exg, in_=exg, func=AF.Exp), reads=[], writes=[b_exg]))
                    A(lambda: c.op("dve", lambda e: e.reduce_sum(out=sm[:, 0, :], in_=exg, axis=AX.X), reads=[b_exg], writes=[b_sm]))
                    A(lambda: c.op("dve", lambda e: e.reciprocal(out=sm[:, 1, :], in_=sm[:, 0, :]), reads=[], writes=[b_sm]))
                    A(lambda t4=t4: c.op("dve", lambda e: e.tensor_tensor(out=gate_all[:, t4, :].rearrange("p (h k) -> p h k", h=8), in0=exg,
                                                                         in1=sm[:, 1, :].unsqueeze(2).to_broadcast([128, 8, 16]), op=ALU.mult),
                                         reads=[b_exg, b_sm], writes=[b_gate[t4]]))
                return th

            gi = [0]
            cgt = V(o - 0, [128, 1]) if False else None
            b_act1 = bufs(16, "act1"); b_coef1 = bufs(16, "coef1")
            D_DOT, D_ACC, D_GELU, D_CG, D_DG, D_MM = 2, 3, 4, 5, 6, 7

            def gathers(blk, pending):
                stream = [(blk * 2 + ti, sl_) for ti in range(2) for sl_ in range(128)]
                N = len(stream)
                per = (len(pending) + 239) // 240 if pending else 0
                gmap = {}
                for n in range(N + D_MM):
                    m = n
                    if 0 <= m < N:
                        tt, sl_ = stream[m]
                        t4 = tt % 4
                        g = gi[0] % NG; gi[0] += 1
                        gmap[m] = g
                        c.dma("pool", lambda e, g=g, t4=t4, sl_=sl_: e.indirect_dma_start(
                            out=G[g], out_offset=None, in_=uvb_d[l][:, :],
                            in_offset=bass.IndirectOffsetOnAxis(ap=idx_all[:, t4, sl_:sl_ + 1], axis=0)),
                            reads=[b_idx[t4], b_uvb[l]], writes=[b_G[g]])
                    m = n - D_DOT
                    if 0 <= m < N:
                        tt, sl_ = stream[m]
                        g = gmap[m]
                        if sl_ == 0:
                            c.op("act", lambda e, tt=tt: e.activation(out=xb, in_=xs[:, tt, :], func=AF.Copy), reads=[b_xs[tt]], writes=[b_xb])
                        if sl_ % 2 == 0:
                            c.op("dve", lambda e, g=g, tt=tt, sl_=sl_: e.scalar_tensor_tensor(
                                out=G[g][:, 0:D], in0=G[g][:, 0:D], scalar=1.0, in1=xs[:, tt, :], op0=ALU.mult, op1=ALU.mult,
                                accum_out=actt[:, sl_:sl_ + 1]), reads=[b_xs[tt]], writes=[b_G[g], b_act1[m % 16]])
                        else:
                            c.op("dve", lambda e, g=g: e.tensor_tensor(out=G[g][:, 0:D], in0=G[g][:, 0:D], in1=xb, op=ALU.mult),
                                 reads=[b_xb], writes=[b_G[g]])
                    m = n - D_ACC
                    if 0 <= m < N and stream[m][1] % 2 == 1:
                        tt, sl_ = stream[m]
                        g = gmap[m]
                        c.op("act", lambda e, sl_=sl_, g=g: e.activation(out=G[g][:, 0:D], in_=G[g][:, 0:D], func=AF.Copy, accum_out=actt[:, sl_:sl_ + 1]),
                             reads=[], writes=[b_G[g], b_act1[m % 16]])
                    for m in (n - D_GELU, n - D_GELU + 1):
                        if not (0 <= m < N) or (stream[m][1] % 2 == 1) != (m == n - D_GELU):
                            continue
                        tt, sl_ = stream[m]
                        c.op("act", lambda e, sl_=sl_: e.activation(out=coef[:, sl_:sl_ + 1], in_=actt[:, sl_:sl_ + 1], func=AF.Gelu),
                             reads=[b_act1[m % 16]], writes=[b_coef1[m % 16]])
                    for m in (n - D_CG, n - D_CG + 1):
                        if not (0 <= m < N) or (stream[m][1] % 2 == 1) != (m == n - D_CG):
                            continue
                        tt, sl_ = stream[m]
                        t4 = tt % 4
                        c.op("dve", lambda e, sl_=sl_, t4=t4: e.tensor_tensor(out=coef[:, sl_:sl_ + 1], in0=coef[:, sl_:sl_ + 1],
                                                                             in1=gate_all[:, t4, sl_:sl_ + 1], op=ALU.mult),
                             reads=[b_gate[t4]], writes=[b_coef1[m % 16]])
                    for m in (n - D_DG, n - D_DG + 1):
                        if not (0 <= m < N) or (stream[m][1] % 2 == 1) != (m == n - D_DG):
                            continue
                        tt, sl_ = stream[m]
                        d_ = m % NDG
                        c.op("act", lambda e, sl_=sl_, d_=d_: e.activation(out=dg[d_], in_=identb, func=AF.Copy, scale=coef[:, sl_:sl_ + 1]),
                             reads=[b_identb, b_coef1[m % 16]], writes=[b_dg[d_]])
                    for m in (n - D_MM, n - D_MM + 1):
                        if not (0 <= m < N) or (stream[m][1] % 2 == 1) != (m == n - D_MM):
                            continue
                        tt, sl_ = stream[m]
                        g = gmap[m]
                        d_ = m % NDG
                        AB = 4 + 2 * (tt % 2)
                        for hf in range(2):
                            mm(ps[AB + hf][:, :], dg[d_], G[g][:, D + hf * 512:D + (hf + 1) * 512], sl_ == 0, sl_ == 127,
                               [b_dg[d_], b_G[g]], [b_ps[AB + hf]])
                        if sl_ == 127:
                            for hf in range(2):
                                c.op("dve", lambda e, tt=tt, hf=hf, AB=AB: e.scalar_tensor_tensor(
                                    out=xs[:, tt, hf * 512:(hf + 1) * 512], in0=xs[:, tt, hf * 512:(hf + 1) * 512], scalar=ALPHA,
                                    in1=ps[AB + hf][:, :], op0=ALU.mult, op1=ALU.add), reads=[b_ps[AB + hf]], writes=[b_xs[tt]])
                            layer_norm([tt], b_lnp, aff_eng="dve")
                            if is_last:
                                out_evs.append(c.dma("sp", lambda e, tt=tt: e.dma_start(out=out_d[s, tt * 128:(tt + 1) * 128, :], in_=xs[:, tt, :]),
                                                     reads=[b_xs[tt]], writes=[], sembuf=b_outd))
                    for _ in range(per):
                        if pending:
                            pending.pop(0)()
                while pending:
                    pending.pop(0)()

            b_G = bufs(NG, "G")
            for t_ in score_thunks(0) + routing_thunks(0):
                t_()
            for blk in range(NB2):
                pending = []
                if blk + 1 < NB2:
                    pending = score_thunks(blk + 1) + routing_thunks(blk + 1)
                gathers(blk, pending)
            c.barrier()
            wst_n[0] = WST
            return out_evs

        b_uvb = bufs(L, "uvb")
        if "peer" in phases:
            CH = 2048
            for l in range(L):
                fns = []
                for r in range(NE // CH):
                    fns.append(lambda e, l=l, r=r: e.dma_start(out=uvb_d[l][r * CH:(r + 1) * CH, 0:D], in_=pu_d[l][r * CH:(r + 1) * CH, :]))
                    fns.append(lambda e, l=l, r=r: e.dma_start(out=uvb_d[l][r * CH:(r + 1) * CH, D:2 * D], in_=pv_d[l][r * CH:(r + 1) * CH, :]))
                c.dma("pool", fns, writes=[b_uvb[l]])
        out_events = []
        for s in range(NSEQ):
            c.barrier()
            c.dma("sp", [(lambda e, s=s, tt=tt: e.dma_start(out=xs[:, tt, :], in_=x_d[s, tt * 128:(tt + 1) * 128, :]))
                         for tt in range(NT)], writes=list(b_xs), sembuf=b_misc)
            dumped = False
            for l in range(L):
                if "mixer" in phases:
                    make_xT()
                    mixer(l)
                    b_lnp = load_ln(l, 0)
                    layer_norm(range(NT), b_lnp, add_eng="pool")
                if stop_after == (l, "ln1"):
                    break
                if "xattn" in phases:
                    make_xT()
                    xattn(l, s)
                    b_lnp = load_ln(l, 1)
                    layer_norm(range(NT), b_lnp, add_eng="pool")
                if stop_after == (l, "ln2"):
                    break
                if "peer" in phases:
                    make_xT()
                    last = (l == L - 1) or stop_after == (l, "ln3")
                    evs = peer(l, s, last)
                    if last:
                        out_events.extend(evs)
                        dumped = True
                if stop_after == (l, "ln3"):
                    break
            if not dumped:
                for tt in range(NT):
                    out_events.append(c.dma("sp", lambda e, s=s, tt=tt: e.dma_start(out=out_d[s, tt * 128:(tt + 1) * 128, :], in_=xs[:, tt, :]),
                                            reads=[b_xs[tt]], writes=[], sembuf=b_outd))
        c.emit(out_events[-1:])
    return nc


def host_consts():
    kp = np.arange(128)[:, None]
    xx = np.arange(896)[None, :]
    nble = np.where((xx - 384) >= kp, 0.0, -30000.0).astype(np.float32)
    mlt = ((xx - 384) > kp).astype(np.float32)
    jj = np.arange(128)[:, None]
    ss = np.arange(128)[None, :]
    negtri = -(jj > ss).astype(np.float32)
    return {
        "c_ident": np.eye(128, dtype=np.float32),
        "c_nble": nble, "c_mlt": mlt, "c_negtri": negtri,
        "c_iota16": np.tile(np.arange(16, dtype=np.float32)[None, :], (128, 1)),
        "c_iota256": np.tile(np.arange(256, dtype=np.float32)[None, :], (128, 1)),
    }


def host_weights(inp, L):
    f = lambda a: np.ascontiguousarray(np.asarray(a, dtype=np.float32))
    w = {}
    w["w_in"] = f(inp["w_in"][:L]); w["w_out"] = f(inp["w_out"][:L])
    rgp = np.zeros((L, 2, 128, 8), np.float32)
    cw = np.asarray(inp["rg_conv_w"])[:L]
    for k in range(4):
        rgp[:, :, :, k] = cw[:, k, :].reshape(L, 2, 128)
    rgp[:, :, :, 4] = np.asarray(inp["rg_conv_b"])[:L].reshape(L, 2, 128)
    rgp[:, :, :, 5] = np.asarray(inp["rg_ba"])[:L].reshape(L, 2, 128)
    rgp[:, :, :, 6] = np.asarray(inp["rg_bi"])[:L].reshape(L, 2, 128)
    rgp[:, :, :, 7] = np.asarray(inp["rg_lambda"])[:L].reshape(L, 2, 128)
    w["rgp"] = rgp
    for nm, src in (("wa_bd", "rg_wa"), ("wi_bd", "rg_wi")):
        a = np.asarray(inp[src])[:L]
        bd = np.zeros((L, 2, 128, 128), np.float32)
        for h in range(4):
            j, o = h // 2, (h % 2) * 64
            bd[:, j, o:o + 64, o:o + 64] = a[:, h]
        w[nm] = bd
    w["fox_bf"] = f(np.asarray(inp["fox_bf"])[:L].reshape(L, 4, 1))
    scp = np.zeros((L, 2, 128, 4), np.float32)
    sw = np.asarray(inp["sc_conv_w"])[:L]
    for k in range(3):
        scp[:, :, :, k] = sw[:, k, :].reshape(L, 2, 128)
    w["scp"] = scp
    w["mng"] = f(np.asarray(inp["mix_norm_g"])[:L].reshape(L, 8, 128).transpose(0, 2, 1))
    w["ln_g"] = f(np.stack([np.asarray(inp["ln1_g"])[:L], np.asarray(inp["ln2_g"])[:L], np.asarray(inp["ln3_g"])[:L]], axis=1))
    w["ln_b"] = f(np.stack([np.asarray(inp["ln1_b"])[:L], np.asarray(inp["ln2_b"])[:L], np.asarray(inp["ln3_b"])[:L]], axis=1))
    w["xa_wq"] = f(inp["xa_wq"][:L]); w["xa_wkv"] = f(inp["xa_wkv"][:L]); w["xa_wo"] = f(inp["xa_wo"][:L])
    w["peer_wq"] = f(inp["peer_wq"][:L])
    w["k12T"] = f(np.stack([np.asarray(inp["peer_k1"])[:L].transpose(0, 2, 1),
                            np.asarray(inp["peer_k2"])[:L].transpose(0, 2, 1)], axis=2))
    for l in range(L):
        w[f"peer_u{l}"] = f(inp["peer_u"][l]); w[f"peer_v{l}"] = f(inp["peer_v"][l])
    w.update(host_consts())
    return w


def run(inp, n_cores=8, NSEQ=2, S=2048, L=2, stop_after=None, trace=False, phases=("mixer", "xattn", "peer")):
    nc = build_program(NSEQ=NSEQ, S=S, L=L, stop_after=stop_after, phases=phases)
    w = host_weights(inp, L)
    x = np.asarray(inp["x"], dtype=np.float32)
    mem = np.asarray(inp["mem"], dtype=np.float32)
    in_maps = []
    for ci in range(n_cores):
        m = dict(w)
        m["x"] = np.ascontiguousarray(x[ci * NSEQ:(ci + 1) * NSEQ, :S])
        m["memT"] = np.ascontiguousarray(mem[ci * NSEQ:(ci + 1) * NSEQ].transpose(0, 2, 1))
        in_maps.append(m)
    res = run_bass_kernel_spmd(nc, in_maps, core_ids=list(range(n_cores)), trace=trace)
    out = np.concatenate([r["out"] for r in res.results], axis=0)
    return out, res


def kernel(**inputs):
    out, _ = run(inputs)
    return out.astype(np.float32)
```

```python
import numpy as np
from contextlib import ExitStack
import concourse.bass as bass
import concourse.mybir as mybir
from concourse.bass_utils import run_bass_kernel_spmd

F32 = mybir.dt.float32
BF16 = mybir.dt.bfloat16
U32 = mybir.dt.uint32
AF = mybir.ActivationFunctionType
ALU = mybir.AluOpType
AX = mybir.AxisListType

D = 1024
KC = 8
GW = 256
OFF_RG_X = 0
OFF_RG_G = 256
OFF_FOX = 512
OFF_FOX_F = 1280
OFF_SB = 1284
OFF_SC = 2052
N_IN = 2820
NMEM = 256
NE = 16384
DEPTH = 2
ALPHA = (2.0 * DEPTH) ** 0.25
LN_EPS = 1e-5
NEG = -1.0e30

ENGS = ("pe", "act", "dve", "pool", "sp")


class Buf:
    __slots__ = ("name", "w", "r", "sem", "cnt")

    def __init__(self, name=""):
        self.name = name
        self.w = None
        self.r = {}
        self.sem = None
        self.cnt = 0


def bufs(n, name=""):
    return [Buf(f"{name}{i}") for i in range(n)]


class Ctx:
    def __init__(self, nc, stack):
        self.nc = nc
        self.stack = stack
        self.q = {e: [] for e in ENGS}
        self.esem = {}
        self.ecnt = {e: 0 for e in ENGS}
        for e in ("pe", "act", "dve", "pool"):
            self.esem[e] = stack.enter_context(nc.semaphore("s_" + e))
        self.nsem = 4
        self.dsb = []

    def barrier(self):
        evs = [(self.esem[e], self.ecnt[e], "x") for e in self.esem if self.ecnt[e] > 0]
        evs += [(b.sem, b.cnt, "dma") for b in self.dsb if b.cnt > 0]
        for e in ENGS:
            self.q[e].append((list(evs), None, None, 0))

    def newsem(self, name):
        self.nsem += 1
        return self.stack.enter_context(self.nc.semaphore(name))

    def _deps(self, reads, writes):
        deps = []
        for b in reads:
            if b.w is not None:
                deps.append(b.w)
        for b in writes:
            if b.w is not None:
                deps.append(b.w)
            deps.extend(b.r.values())
        return deps

    def _commit(self, ev, reads, writes):
        k = id(ev[0])
        for b in reads:
            o = b.r.get(k)
            if o is None or o[1] < ev[1]:
                b.r[k] = ev
        for b in writes:
            b.w = ev
            b.r = {}

    def op(self, eng, fn, reads=(), writes=()):
        deps = self._deps(reads, writes)
        self.ecnt[eng] += 1
        ev = (self.esem[eng], self.ecnt[eng], eng)
        self.q[eng].append((deps, fn, ev[0], 1))
        self._commit(ev, reads, writes)
        return ev

    def dma(self, eng, fns, reads=(), writes=(), sembuf=None):
        if not isinstance(fns, (list, tuple)):
            fns = [fns]
        sb = sembuf if sembuf is not None else writes[0]
        if sb.sem is None:
            sb.sem = self.newsem("d%d" % self.nsem)
            self.dsb.append(sb)
        deps = self._deps(reads, writes)
        if sb.cnt > 0:
            deps.append((sb.sem, sb.cnt, "dma"))
        for i, fn in enumerate(fns):
            self.q[eng].append((deps if i == 0 else [], fn, sb.sem, 16))
        sb.cnt += 16 * len(fns)
        ev = (sb.sem, sb.cnt, "dma")
        self._commit(ev, reads, writes)
        return ev

    def emit(self, final_events):
        nc = self.nc
        engmap = {"pe": "tensor", "act": "scalar", "dve": "vector", "pool": "gpsimd", "sp": "sync"}
        with nc.Block() as block:
            for e in ENGS:
                ops = self.q[e]
                fin = final_events if e == "sp" else []

                def body(engine, ops=ops, e=e, fin=fin):
                    waited = {}
                    for deps, fn, sem, inc in ops:
                        need = {}
                        for (s, v, pe) in deps:
                            if pe == e and e == "pe":
                                continue
                            k = id(s)
                            if waited.get(k, 0) >= v:
                                continue
                            if k not in need or need[k][1] < v:
                                need[k] = (s, v)
                        for k, (s, v) in need.items():
                            engine.wait_ge(s, v)
                            waited[k] = v
                        if fn is None:
                            continue
                        ins = fn(engine)
                        ins.then_inc(sem, inc)
                    for (s, v, pe) in fin:
                        engine.wait_ge(s, v)

                getattr(block, engmap[e])(body)


ARENA_BYTES = 87 * 1024 + 512
LN_OFF = 79 * 1024 + 512


def build_program(NSEQ=2, S=2048, L=2, stop_after=None, phases=("mixer", "xattn", "peer")):
    nc = bass.Bass("TRN2", target_bir_lowering=False)
    NT = S // 128
    NQ = S // 512
    NB2 = S // 256

    def din(name, shape, dt=F32):
        return nc.dram_tensor(name, list(shape), dt, kind="ExternalInput").ap()

    x_d = din("x", [NSEQ, S, D])
    memT_d = din("memT", [NSEQ, D, NMEM])
    w_in_d = din("w_in", [L, D, N_IN])
    w_out_d = din("w_out", [L, D, D])
    rgp_d = din("rgp", [L, 2, 128, 8])
    wa_d = din("wa_bd", [L, 2, 128, 128])
    wi_d = din("wi_bd", [L, 2, 128, 128])
    foxbf_d = din("fox_bf", [L, 4, 1])
    scp_d = din("scp", [L, 2, 128, 4])
    mng_d = din("mng", [L, 128, 8])
    lng_d = din("ln_g", [L, 3, D])
    lnb_d = din("ln_b", [L, 3, D])
    xwq_d = din("xa_wq", [L, D, D])
    xwkv_d = din("xa_wkv", [L, D, 2 * D])
    xwo_d = din("xa_wo", [L, D, D])
    pwq_d = din("peer_wq", [L, D, 2048])
    k12T_d = din("k12T", [L, 128, 2, 128])
    pu_d = [din(f"peer_u{l}", [NE, D]) for l in range(L)]
    pv_d = [din(f"peer_v{l}", [NE, D]) for l in range(L)]
    cid_d = din("c_ident", [128, 128])
    cmle_d = din("c_nble", [128, 896])
    cmlt_d = din("c_mlt", [128, 896])
    cntri_d = din("c_negtri", [128, 128])
    ciota_d = din("c_iota16", [128, 16])
    ciota256_d = din("c_iota256", [128, 256])
    out_d = nc.dram_tensor("out", [NSEQ, S, D], F32, kind="ExternalOutput").ap()
    uvb_d = [nc.dram_tensor(f"uvb{l}", [NE, 2 * D], BF16, kind="Internal").ap() for l in range(L)]

    with ExitStack() as st:
        c = Ctx(nc, st)

        def sb(name, shape, dt=F32):
            return st.enter_context(nc.sbuf_tensor("sb_" + name, list(shape), dt))

        b_lnp_shared = Buf("lnp")
        b_misc = Buf("misc")
        b_outd = Buf("outd")
        xs = sb("xs", [128, NT, D]); b_xs = bufs(NT, "xs")
        xT = sb("xT", [128, KC, S], BF16); b_xT = bufs(NT, "xT")
        ps = [st.enter_context(nc.psum_tensor(f"ps{i}", [128, 512], F32)) for i in range(8)]
        b_ps = bufs(8, "ps")
        ident = sb("ident", [128, 128]); b_ident = Buf("ident")
        nbl = sb("nbl", [128, 896], BF16); b_mle = Buf("nbl")
        mlt = sb("mlt", [128, 896], BF16); b_mlt = Buf("mlt")
        negtri = sb("negtri", [128, 128], BF16); b_negtri = Buf("negtri")
        negid = sb("negid", [128, 128], BF16); b_negid = Buf("negid")
        negones = sb("negones", [128, 128], BF16); b_negones = Buf("negones")
        ones_bf = sb("ones_bf", [128, 128], BF16); b_ones_bf = Buf("ones_bf")
        ones_f = sb("ones_f", [128, 128]); b_ones_f = Buf("ones_f")
        iota16 = sb("iota16", [128, 16]); b_iota = Buf("iota")
        WST = 2
        wstg = [sb(f"wstg{i}", [128, KC, 128]) for i in range(WST)]; b_wstg = bufs(WST, "wstg")
        w_a = sb("w_a", [128, KC, 256], BF16); b_w_a = Buf("w_a")
        w_b = sb("w_b", [128, KC, 256], BF16); b_w_b = Buf("w_b")
        rgp = sb("rgp", [128, 2, 8]); b_rgp = Buf("rgp")
        rgc = sb("rgc", [128, 2, 4]); b_rgc = Buf("rgc")
        wabd = sb("wabd", [128, 2, 128]); wibd = sb("wibd", [128, 2, 128]); b_wbd = Buf("wbd")
        scp = sb("scp", [128, 2, 4]); b_scp = Buf("scp")
        mng = sb("mng", [128, 8]); b_mng = Buf("mng")
        foxbf = sb("foxbf", [4, 1]); b_foxbf = Buf("foxbf")
        lnst = sb("lnst", [128, 2, 6]); b_lnst = Buf("lnst")
        lnmv = sb("lnmv", [128, 8]); b_lnmv = Buf("lnmv")
        cst = sb("cst", [128, 4]); b_cst = Buf("cst")
        scr = sb("scr", [128, ARENA_BYTES // 4])

        def V(off, shape, dt=F32):
            esz = 4 if dt in (F32, U32) else 2
            n = 1
            for d_ in shape[1:]:
                n *= d_
            nb = n * esz
            assert off % 4 == 0 and nb % 4 == 0 and off + nb <= ARENA_BYTES, (off, nb)
            a = scr[:, off // 4:(off + nb) // 4]
            if dt != F32:
                a = a.bitcast(dt)
            if len(shape) == 3:
                a = a.rearrange("p (a b) -> p a b", a=shape[1])
            elif len(shape) == 4:
                a = a.rearrange("p (a b c) -> p a b c", a=shape[1], b=shape[2])
            if shape[0] != 128:
                a = a[0:shape[0]]
            return a

        wst_i = [0]
        wst_n = [WST]

        def load_w(dram2d, col0, ncols, dst, b_dst, eng_cast="pool"):
            done = 0
            while done < ncols:
                n = min(128, ncols - done)
                i = wst_i[0] % wst_n[0]
                wst_i[0] += 1
                src = dram2d[:, col0 + done:col0 + done + n].rearrange("(kc p) c -> p kc c", p=128)
                c.dma("sp", lambda e, i=i, n=n, src=src: e.dma_start(out=wstg[i][:, :, 0:n], in_=src), writes=[b_wstg[i]])
                if eng_cast == "act":
                    c.op("act", lambda e, i=i, n=n, done=done: e.activation(out=dst[:, :, done:done + n], in_=wstg[i][:, :, 0:n], func=AF.Copy),
                         reads=[b_wstg[i]], writes=[b_dst])
                else:
                    c.op(eng_cast, lambda e, i=i, n=n, done=done: e.tensor_copy(out=dst[:, :, done:done + n], in_=wstg[i][:, :, 0:n]),
                         reads=[b_wstg[i]], writes=[b_dst])
                done += n

        def load_rows(dram_rows, dst, b_dst, eng_cast="pool"):
            for hf in range(2):
                i = wst_i[0] % wst_n[0]
                wst_i[0] += 1
                stg = wstg[i][:].rearrange("p a b -> p (a b)").rearrange("p (a b) -> p a b", a=2)
                src = dram_rows[:, hf * 512:(hf + 1) * 512].rearrange("(c p) d -> p c d", p=128)
                c.dma("sp", lambda e, stg=stg, src=src: e.dma_start(out=stg, in_=src), writes=[b_wstg[i]])
                if eng_cast == "act":
                    c.op("act", lambda e, stg=stg, hf=hf: e.activation(out=dst[:, :, hf * 512:(hf + 1) * 512], in_=stg, func=AF.Copy),
                         reads=[b_wstg[i]], writes=[b_dst])
                else:
                    c.op(eng_cast, lambda e, stg=stg, hf=hf: e.tensor_copy(out=dst[:, :, hf * 512:(hf + 1) * 512], in_=stg),
                         reads=[b_wstg[i]], writes=[b_dst])

        cstage = V(0, [128, 896]); b_cstage = Buf("cstage")
        c.dma("sp", lambda e: e.dma_start(out=ident[:], in_=cid_d[:, :]), writes=[b_ident], sembuf=b_misc)
        c.dma("sp", lambda e: e.dma_start(out=iota16[:], in_=ciota_d[:, :]), writes=[b_iota], sembuf=b_misc)
        c.dma("sp", lambda e: e.dma_start(out=cstage, in_=cmle_d[:, :]), writes=[b_cstage], sembuf=b_misc)
        c.op("dve", lambda e: e.tensor_copy(out=nbl[:], in_=cstage), reads=[b_cstage], writes=[b_mle])
        c.dma("sp", lambda e: e.dma_start(out=cstage, in_=cmlt_d[:, :]), writes=[b_cstage], sembuf=b_misc)
        c.op("dve", lambda e: e.tensor_copy(out=mlt[:], in_=cstage), reads=[b_cstage], writes=[b_mlt])
        c.dma("sp", lambda e: e.dma_start(out=cstage[:, 0:128], in_=cntri_d[:, :]), writes=[b_cstage], sembuf=b_misc)
        c.op("dve", lambda e: e.tensor_copy(out=negtri[:], in_=cstage[:, 0:128]), reads=[b_cstage], writes=[b_negtri])
        c.op("dve", lambda e: e.tensor_scalar(out=negid[:], in0=ident[:], scalar1=-1.0, scalar2=None, op0=ALU.mult),
             reads=[b_ident], writes=[b_negid])
        c.op("dve", lambda e: e.memset(negones[:], -1.0), writes=[b_negones])
        c.op("dve", lambda e: e.memset(ones_bf[:], 1.0), writes=[b_ones_bf])
        c.op("dve", lambda e: e.memset(ones_f[:], 1.0), writes=[b_ones_f])
        c.op("dve", lambda e: e.memset(cst[:, 0:1], LN_EPS), writes=[b_cst])
        c.op("dve", lambda e: e.memset(cst[:, 1:2], 1e-6), writes=[b_cst])
        c.op("dve", lambda e: e.memset(cst[:, 2:3], 1.0), writes=[b_cst])
        EPS, EPS6, ONE = cst[:, 0:1], cst[:, 1:2], cst[:, 2:3]

        rr = [0]

        def evac_eng():
            rr[0] += 1
            return "act" if rr[0] % 2 == 0 else "dve"

        def copy_op(eng, out_ap, in_ap, reads, writes, scale=None):
            if eng == "act":
                if scale is None:
                    c.op("act", lambda e: e.activation(out=out_ap, in_=in_ap, func=AF.Copy), reads, writes)
                else:
                    c.op("act", lambda e: e.activation(out=out_ap, in_=in_ap, func=AF.Copy, scale=scale), reads, writes)
            else:
                if scale is None:
                    c.op(eng, lambda e: e.tensor_copy(out=out_ap, in_=in_ap), reads, writes)
                else:
                    c.op(eng, lambda e: e.tensor_scalar(out=out_ap, in0=in_ap, scalar1=scale, scalar2=None, op0=ALU.mult),
                         reads, writes)

        def mm(out_ap, lhsT, rhs, start, stop, reads, writes):
            c.op("pe", lambda e: e.matmul(out_ap, lhsT, rhs, start=start, stop=stop), reads, writes)

        def make_xT():
            for tt in range(NT):
                for g in range(2):
                    bk = (tt * 2 + g) % 8
                    for j in range(4):
                        kc = 4 * g + j
                        c.op("pe", lambda e, tt=tt, kc=kc, j=j, bk=bk: e.transpose(
                            ps[bk][:, j * 128:(j + 1) * 128], xs[:, tt, kc * 128:(kc + 1) * 128], ident[:]),
                            reads=[b_xs[tt], b_ident], writes=[b_ps[bk]])
                    copy_op(evac_eng(), xT[:, 4 * g:4 * g + 4, tt * 128:(tt + 1) * 128],
                            ps[bk][:].rearrange("p (j q) -> p j q", j=4), [b_ps[bk]], [b_xT[tt]])

        def projT(w_ap, b_w, M, q, bk, c0=0):
            for kc in range(KC):
                mm(ps[bk][0:M, :], w_ap[:, kc, c0:c0 + M], xT[:, kc, q * 512:(q + 1) * 512], kc == 0, kc == KC - 1,
                   [b_w] + b_xT[4 * q:4 * q + 4], [b_ps[bk]])

        lng = V(LN_OFF, [128, D]); lnb = V(LN_OFF + 4096, [128, D])

        def load_ln(l, which):
            b = b_lnp_shared
            c.dma("sp", [lambda e: e.dma_start(out=lng, in_=lng_d[l, which].partition_broadcast(128)),
                         lambda e: e.dma_start(out=lnb, in_=lnb_d[l, which].partition_broadcast(128))], writes=[b])
            return b

        def layer_norm(tts, b_lnp, aff_eng="dve", add_eng="dve"):
            for tt in tts:
                c.op("dve", lambda e, tt=tt: e.bn_stats(out=lnst[:, 0, :], in_=xs[:, tt, 0:512]),
                     reads=[b_xs[tt]], writes=[b_lnst])
                c.op("dve", lambda e, tt=tt: e.bn_stats(out=lnst[:, 1, :], in_=xs[:, tt, 512:1024]),
                     reads=[b_xs[tt]], writes=[b_lnst])
                c.op("dve", lambda e: e.bn_aggr(out=lnmv[:, 0:2], in_=lnst[:].rearrange("p a b -> p (a b)")),
                     reads=[b_lnst], writes=[b_lnmv])
                c.op("act", lambda e: e.activation(out=lnmv[:, 2:3], in_=lnmv[:, 1:2], func=AF.Sqrt, bias=EPS),
                     reads=[b_cst], writes=[b_lnmv])
                c.op("dve", lambda e: e.reciprocal(out=lnmv[:, 3:4], in_=lnmv[:, 2:3]), reads=[], writes=[b_lnmv])
                c.op("dve", lambda e: e.scalar_tensor_tensor(out=lnmv[:, 4:5], in0=lnmv[:, 0:1], scalar=-1.0,
                                                              in1=lnmv[:, 3:4], op0=ALU.mult, op1=ALU.mult),
                     reads=[], writes=[b_lnmv])
                c.op("act", lambda e, tt=tt: e.activation(out=xs[:, tt, :], in_=xs[:, tt, :], func=AF.Identity,
                                                          scale=lnmv[:, 3:4], bias=lnmv[:, 4:5]),
                     reads=[b_lnmv], writes=[b_xs[tt]])
                c.op(aff_eng, lambda e, tt=tt: e.tensor_tensor(out=xs[:, tt, :], in0=xs[:, tt, :], in1=lng, op=ALU.mult),
                     reads=[b_lnp], writes=[b_xs[tt]])
                c.op(add_eng, lambda e, tt=tt: e.tensor_tensor(out=xs[:, tt, :], in0=xs[:, tt, :], in1=lnb, op=ALU.add),
                     reads=[b_lnp], writes=[b_xs[tt]])

        def mixer(l):
            c.barrier()
            w_in_l = w_in_d[l]
            K1 = 1024
            Yg = V(0, [128, 2, S]); b_Yg = bufs(2, "Yg")
            yTn = V(16 * K1, [128, 2, S], BF16); b_yTn = Buf("yTn")
            woutg = V(24 * K1, [128, 2, D], BF16); b_woutg = Buf("woutg")
            rden = V(28 * K1, [128, 512]); b_rden = Buf("rden")
            TB = 30 * K1
            TW = 8256

            def out_proj_group(g, first):
                load_rows(w_out_d[l][g * 256:(g + 1) * 256, :], woutg, b_woutg)
                for tt in range(NT):
                    for h in range(2):
                        bk = (tt * 2 + h) % 8
                        for cc in range(2):
                            mm(ps[bk][:, :], yTn[:, cc, tt * 128:(tt + 1) * 128], woutg[:, cc, h * 512:(h + 1) * 512],
                               cc == 0, cc == 1, [b_yTn, b_woutg], [b_ps[bk]])
                        if first:
                            c.op("dve", lambda e, tt=tt, h=h, bk=bk: e.scalar_tensor_tensor(
                                out=xs[:, tt, h * 512:(h + 1) * 512], in0=xs[:, tt, h * 512:(h + 1) * 512], scalar=ALPHA,
                                in1=ps[bk][:, :], op0=ALU.mult, op1=ALU.add), reads=[b_ps[bk]], writes=[b_xs[tt]])
                        else:
                            c.op("dve", lambda e, tt=tt, h=h, bk=bk: e.tensor_tensor(
                                out=xs[:, tt, h * 512:(h + 1) * 512], in0=xs[:, tt, h * 512:(h + 1) * 512],
                                in1=ps[bk][:, :], op=ALU.add), reads=[b_ps[bk]], writes=[b_xs[tt]])

            def group_rms_and_proj(g, first, sqA, b_sqA, sqB, b_sqB):
                for q in range(NQ):
                    sl = slice(q * 512, (q + 1) * 512)
                    bk = q % 4
                    c.op("act", lambda e, sl=sl: e.activation(out=sqA[:, 0:512], in_=Yg[:, 0, sl], func=AF.Square),
                         reads=[b_Yg[0]], writes=[b_sqA])
                    c.op("act", lambda e, sl=sl: e.activation(out=sqB[:, 0:512], in_=Yg[:, 1, sl], func=AF.Square),
                         reads=[b_Yg[1]], writes=[b_sqB])
                    mm(ps[bk][:, :], ones_f[:], sqA[:, 0:512], True, False, [b_ones_f, b_sqA], [b_ps[bk]])
                    mm(ps[bk][:, :], ones_f[:], sqB[:, 0:512], False, True, [b_ones_f, b_sqB], [b_ps[bk]])
                    c.op("act", lambda e, bk=bk: e.activation(out=rden, in_=ps[bk][:, :], func=AF.Sqrt,
                                                              scale=1.0 / 256.0, bias=EPS6),
                         reads=[b_ps[bk], b_cst], writes=[b_rden])
                    c.op("dve", lambda e: e.reciprocal(out=rden, in_=rden), reads=[], writes=[b_rden])
                    for cc in range(2):
                        c.op("dve", lambda e, cc=cc, sl=sl: e.scalar_tensor_tensor(
                            out=yTn[:, cc, sl], in0=Yg[:, cc, sl], scalar=mng[:, 2 * g + cc:2 * g + cc + 1], in1=rden,
                            op0=ALU.mult, op1=ALU.mult), reads=[b_Yg[cc], b_mng, b_rden], writes=[b_yTn])
                out_proj_group(g, first)

            c.dma("sp", lambda e: e.dma_start(out=rgp[:], in_=rgp_d[l].rearrange("j p k -> p j k")), writes=[b_rgp], sembuf=b_misc)
            c.dma("sp", [lambda e: e.dma_start(out=wabd[:], in_=wa_d[l].rearrange("j p k -> p j k")),
                         lambda e: e.dma_start(out=wibd[:], in_=wi_d[l].rearrange("j p k -> p j k"))], writes=[b_wbd], sembuf=b_misc)
            c.dma("sp", lambda e: e.dma_start(out=scp[:], in_=scp_d[l].rearrange("j p k -> p j k")), writes=[b_scp], sembuf=b_misc)
            c.dma("sp", lambda e: e.dma_start(out=mng[:], in_=mng_d[l]), writes=[b_mng], sembuf=b_misc)
            c.dma("sp", lambda e: e.dma_start(out=foxbf[:], in_=foxbf_d[l]), writes=[b_foxbf], sembuf=b_misc)
            c.op("act", lambda e: e.activation(out=rgc[:, :, 2], in_=rgp[:, :, 7], func=AF.Exp, scale=-1.0),
                 reads=[b_rgp], writes=[b_rgc])
            c.op("act", lambda e: e.activation(out=rgc[:, :, 3], in_=rgc[:, :, 2], func=AF.Ln, bias=ONE),
                 reads=[b_cst], writes=[b_rgc])
            c.op("dve", lambda e: e.tensor_scalar(out=rgc[:, :, 0], in0=rgc[:, :, 3], scalar1=-8.0, scalar2=None, op0=ALU.mult),
                 reads=[], writes=[b_rgc])
            c.op("dve", lambda e: e.tensor_scalar(out=rgc[:, :, 1], in0=rgc[:, :, 3], scalar1=-16.0, scalar2=None, op0=ALU.mult),
                 reads=[], writes=[b_rgc])

            T1 = V(TB, [128, S + 4]); b_T1 = Buf("T1")
            T2 = V(TB + TW, [128, S + 4]); b_T2 = Buf("T2")
            T3 = V(TB + 2 * TW, [128, S + 4]); b_T3 = Buf("T3")
            T4 = V(TB + 3 * TW, [128, S + 4]); b_T4 = Buf("T4")

            conv_tick()
            for j in range(2):
                T5 = Yg[:, j, :]; b_T5 = b_Yg[j]
                load_w(w_in_l, OFF_RG_X + j * 128, 128, w_a, b_w_a)
                load_w(w_in_l, OFF_RG_G + j * 128, 128, w_b, b_w_b)
                c.op("pool", lambda e: e.memset(T1[:, 0:4], 0.0), writes=[b_T1])
                for q in range(NQ):
                    projT(w_a, b_w_a, 128, q, q)
                    copy_op("act", T1[:, 4 + q * 512:4 + (q + 1) * 512], ps[q][:, :], [b_ps[q]], [b_T1])
                c.op("dve", lambda e, j=j: e.tensor_scalar(out=T2[:, 0:S], in0=T1[:, 4:4 + S], scalar1=rgp[:, j, 3:4],
                                                           scalar2=rgp[:, j, 4:5], op0=ALU.mult, op1=ALU.add),
                     reads=[b_T1, b_rgp], writes=[b_T2])
                for k in range(3):
                    c.op("dve", lambda e, j=j, k=k: e.scalar_tensor_tensor(
                        out=T2[:, 0:S], in0=T1[:, 1 + k:1 + k + S], scalar=rgp[:, j, k:k + 1], in1=T2[:, 0:S],
                        op0=ALU.mult, op1=ALU.add), reads=[b_T1, b_rgp], writes=[b_T2])
                for q in range(NQ):
                    sl = slice(q * 512, (q + 1) * 512)
                    mm(ps[4 + q % 4][:, :], wabd[:, j, :], T2[:, sl], True, True, [b_wbd, b_T2], [b_ps[4 + q % 4]])
                    c.op("act", lambda e, j=j, q=q, sl=sl: e.activation(out=T3[:, sl], in_=ps[4 + q % 4][:, :], func=AF.Sigmoid,
                                                                       bias=rgp[:, j, 5:6]),
                         reads=[b_ps[4 + q % 4], b_rgp], writes=[b_T3])
                for q in range(NQ):
                    sl = slice(q * 512, (q + 1) * 512)
                    mm(ps[q % 4][:, :], wibd[:, j, :], T2[:, sl], True, True, [b_wbd, b_T2], [b_ps[q % 4]])
                    c.op("act", lambda e, j=j, q=q, sl=sl: e.activation(out=T4[:, sl], in_=ps[q % 4][:, :], func=AF.Sigmoid,
                                                                       bias=rgp[:, j, 6:7]),
                         reads=[b_ps[q % 4], b_rgp], writes=[b_T4])
                c.op("act", lambda e, j=j, T5=T5: e.activation(out=T5, in_=T3[:, 0:S], func=AF.Exp, scale=rgc[:, j, 0:1]),
                     reads=[b_T3, b_rgc], writes=[b_T5])
                c.op("act", lambda e, j=j: e.activation(out=T3[:, 0:S], in_=T3[:, 0:S], func=AF.Exp, scale=rgc[:, j, 1:2]),
                     reads=[b_rgc], writes=[b_T3])
                c.op("act", lambda e: e.activation(out=T3[:, 0:S], in_=T3[:, 0:S], func=AF.Sqrt, scale=-1.0, bias=ONE),
                     reads=[b_cst], writes=[b_T3])
                c.op("dve", lambda e: e.tensor_tensor(out=T4[:, 0:S], in0=T4[:, 0:S], in1=T3[:, 0:S], op=ALU.mult),
                     reads=[b_T3], writes=[b_T4])
                c.op("dve", lambda e: e.tensor_tensor(out=T4[:, 0:S], in0=T4[:, 0:S], in1=T2[:, 0:S], op=ALU.mult),
                     reads=[b_T2], writes=[b_T4])
                c.op("dve", lambda e, T5=T5: e.tensor_tensor_scan(out=T3[:, 0:S], data0=T5, data1=T4[:, 0:S], initial=0.0,
                                                                  op0=ALU.mult, op1=ALU.add),
                     reads=[b_T5, b_T4], writes=[b_T3])
                for q in range(NQ):
                    sl = slice(q * 512, (q + 1) * 512)
                    projT(w_b, b_w_b, 128, q, 4 + q % 4)
                    c.op("act", lambda e, q=q, sl=sl: e.activation(out=T2[:, sl], in_=ps[4 + q % 4][:, :], func=AF.Gelu_apprx_tanh),
                         reads=[b_ps[4 + q % 4]], writes=[b_T2])
                c.op("dve", lambda e, j=j: e.tensor_tensor(out=Yg[:, j, :], in0=T3[:, 0:S], in1=T2[:, 0:S], op=ALU.mult),
                     reads=[b_T3, b_T2], writes=[b_Yg[j]])
            group_rms_and_proj(0, True, T1, b_T1, T4, b_T4)

            qTh = V(30 * K1, [68, 4, S], BF16)
            kTh = V(46 * K1, [68, 4, S], BF16)
            Vt = V(62 * K1, [128, NT, 256], BF16)
            PB0 = 70 * K1
            NPB = 3
            Pb = [V(PB0 + i * K1, [128, 512], BF16) for i in range(NPB)]
            Eb = [V(PB0 + 3 * K1 + i * 2 * K1, [128, 512]) for i in range(2)]
            SPb = [V(PB0 + 7 * K1 + i * K1, [128, 512], BF16) for i in range(3)]
            Lpb = [V(PB0 + 10 * K1 + i * K1, [128, 512], BF16) for i in range(3)]
            Lcb = [V(PB0 + 13 * K1 + i * K1, [128, 512], BF16) for i in range(3)]
            for grp, off in ((1, OFF_FOX), (2, OFF_SB)):
                is_fox = grp == 1
                c.barrier()
                conv_tick()
                b_qTh = bufs(4, "qTh"); b_kTh = bufs(4, "kTh"); b_Vt = bufs(NT, "Vt")
                b_Pb = bufs(NPB, "Pb"); b_Eb = bufs(2, "Eb"); b_SPb = bufs(3, "SPb"); b_Lpb = bufs(3, "Lpb"); b_Lcb = bufs(3, "Lcb")
                b_Yg = bufs(2, "Yg"); b_yTn = Buf("yTn"); b_rden = Buf("rden"); b_woutg = Buf("woutg")
                if is_fox:
                    fl = V(30 * K1, [4, S]); fl2 = V(38 * K1, [4, S])
                    chi = V(46 * K1, [4, S], BF16); clo = V(50 * K1, [4, S], BF16)
                    nchi = V(54 * K1, [4, S], BF16); nclo = V(58 * K1, [4, S], BF16)
                    b_fl, b_fl2, b_chi, b_clo, b_nchi, b_nclo = (Buf("fl"), Buf("fl2"), Buf("chi"), Buf("clo"), Buf("nchi"), Buf("nclo"))
                    fones = V(62 * K1, [4, S]); b_fones = Buf("fones")
                    c.op("dve", lambda e: e.memset(fones, 1.0), writes=[b_fones])
                    load_w(w_in_l, OFF_FOX_F, 4, w_a, b_w_a)
                    for q in range(NQ):
                        sl = slice(q * 512, (q + 1) * 512)
                        projT(w_a, b_w_a, 4, q, q % 4)
                        c.op("dve", lambda e, q=q, sl=sl: e.tensor_scalar(out=fl[:, sl], in0=ps[q % 4][0:4, :], scalar1=foxbf[:, 0:1],
                                                                            scalar2=None, op0=ALU.add),
                             reads=[b_ps[q % 4], b_foxbf], writes=[b_fl])
                    c.op("act", lambda e: e.activation(out=fl, in_=fl, func=AF.Exp, scale=-1.0), reads=[], writes=[b_fl])
                    c.op("act", lambda e: e.activation(out=fl, in_=fl, func=AF.Ln, bias=cst[0:4, 2:3]),
                         reads=[b_cst], writes=[b_fl])
                    c.op("dve", lambda e: e.tensor_tensor_scan(out=fl2, data0=fones, data1=fl, initial=0.0,
                                                               op0=ALU.mult, op1=ALU.add),
                         reads=[b_fl, b_fones], writes=[b_fl2])
                    c.op("dve", lambda e: e.tensor_copy(out=nchi, in_=fl2), reads=[b_fl2], writes=[b_nchi])
                    c.op("dve", lambda e: e.tensor_tensor(out=nclo, in0=fl2, in1=nchi, op=ALU.subtract),
                         reads=[b_fl2, b_nchi], writes=[b_nclo])
                    c.op("dve", lambda e: e.tensor_scalar(out=chi, in0=nchi, scalar1=-1.0, scalar2=None, op0=ALU.mult),
                         reads=[b_nchi], writes=[b_chi])
                    c.op("dve", lambda e: e.tensor_scalar(out=clo, in0=nclo, scalar1=-1.0, scalar2=None, op0=ALU.mult),
                         reads=[b_nclo], writes=[b_clo])
                    for h in range(4):
                        c.op("dve", lambda e, h=h: e.memset(qTh[64:68, h, :], 1.0), writes=[b_qTh[h]])
                        c.op("dve", lambda e, h=h: e.memset(kTh[64:68, h, :], 1.0), writes=[b_kTh[h]])
                        c.dma("sp", [lambda e, h=h: e.dma_start(out=qTh[64:65, h, :], in_=chi[h:h + 1, :]),
                                     lambda e, h=h: e.dma_start(out=qTh[65:66, h, :], in_=clo[h:h + 1, :])],
                              reads=[b_chi, b_clo], writes=[b_qTh[h]], sembuf=b_misc)
                        c.dma("sp", [lambda e, h=h: e.dma_start(out=kTh[66:67, h, :], in_=nchi[h:h + 1, :]),
                                     lambda e, h=h: e.dma_start(out=kTh[67:68, h, :], in_=nclo[h:h + 1, :])],
                              reads=[b_nchi, b_nclo], writes=[b_kTh[h]], sembuf=b_misc)
                    c.barrier()
                KR = 68 if is_fox else 64
                for which, dst, b_dst, coff, scl in ((0, qTh, b_qTh, off, 0.125), (1, kTh, b_kTh, off + GW, None)):
                    for hp in range(2):
                        load_w(w_in_l, coff + hp * 128, 128, w_a, b_w_a)
                        for hh in range(2):
                            h = hp * 2 + hh
                            for q in range(NQ):
                                bk = (h * NQ + q) % 8
                                projT(w_a, b_w_a, 64, q, bk, c0=hh * 64)
                                copy_op(evac_eng(), dst[0:64, h, q * 512:(q + 1) * 512], ps[bk][0:64, :], [b_ps[bk]], [b_dst[h]],
                                        scale=scl)
                load_w(w_in_l, off + 2 * GW, 256, w_b, b_w_b)
                for tt in range(NT):
                    bk = tt % 8
                    for kc in range(KC):
                        mm(ps[bk][:, 0:256], xT[:, kc, tt * 128:(tt + 1) * 128], w_b[:, kc, 0:256], kc == 0, kc == KC - 1,
                           [b_w_b, b_xT[tt]], [b_ps[bk]])
                    copy_op(evac_eng(), Vt[:, tt, :], ps[bk][:, 0:256], [b_ps[bk]], [b_Vt[tt]])
                pairs = []
                for h in range(4):
                    for q in range(NQ):
                        nA = 4 * q + 4
                        order = list(range(nA)) if is_fox else list(range(nA - 1, -1, -1))
                        for idx, A in enumerate(order):
                            pairs.append((h, q, idx, A, nA))

                def geom(m):
                    h, q, idx, A, nA = pairs[m]
                    diag = A >= 4 * q
                    moff = 384 - 128 * (A - 4 * q)
                    par = (h * NQ + q) % 2
                    return dict(h=h, q=q, idx=idx, A=A, nA=nA, first=idx == 0, last=idx == nA - 1, diag=diag, moff=moff,
                                ks=slice(A * 128, (A + 1) * 128), qs=slice(q * 512, (q + 1) * 512),
                                sbk=m % 3, pb=m % 3, eb=m % 2, s3=m % 3, wbk=3 + (m % 2), par=par, lc=idx % 3)

                def st_score(m):
                    g = geom(m)
                    h, sbk = g["h"], g["sbk"]
                    mm(ps[sbk][:, :], kTh[0:KR, h, g["ks"]], qTh[0:KR, h, g["qs"]], True, True, [b_kTh[h], b_qTh[h]], [b_ps[sbk]])
                    if is_fox and g["diag"]:
                        mo = g["moff"]
                        c.op("dve", lambda e: e.tensor_tensor(out=ps[sbk][:, :], in0=ps[sbk][:, :], in1=nbl[:, mo:mo + 512], op=ALU.add),
                             reads=[b_mle], writes=[b_ps[sbk]])

                def st_fox_exp(m):
                    g = geom(m)
                    sbk, pb = g["sbk"], g["pb"]
                    c.op("act", lambda e: e.activation(out=Pb[pb], in_=ps[sbk][:, :], func=AF.Exp), reads=[b_ps[sbk]], writes=[b_Pb[pb]])

                def st_fox_pv(m):
                    g = geom(m)
                    h, q, A, pb, par = g["h"], g["q"], g["A"], g["pb"], g["par"]
                    cc, hp = h // 2, h % 2
                    vsl = slice(cc * 128, (cc + 1) * 128)
                    prt = slice(hp * 64, (hp + 1) * 64)
                    ob, db = 4 + par, 6 + par
                    mm(ps[ob][:, :], Vt[:, A, vsl], Pb[pb], g["first"], g["last"], [b_Vt[A], b_Pb[pb]], [b_ps[ob]])
                    mm(ps[db][:, :], ones_bf[:], Pb[pb], g["first"], g["last"], [b_ones_bf, b_Pb[pb]], [b_ps[db]])
                    if g["last"]:
                        qs = g["qs"]
                        c.op("dve", lambda e: e.reciprocal(out=rden, in_=ps[db][:, :]), reads=[b_ps[db]], writes=[b_rden])
                        c.op("dve", lambda e: e.tensor_tensor(out=Yg[prt, cc, qs], in0=ps[ob][prt, :], in1=rden[prt, :], op=ALU.mult),
                             reads=[b_ps[ob], b_rden], writes=[b_Yg[cc]])

                def st_sb_elem(m):
                    g = geom(m)
                    sbk, eb, s3, lc = g["sbk"], g["eb"], g["s3"], g["lc"]
                    c.op("act", lambda e: e.activation(out=Eb[eb], in_=ps[sbk][:, :], func=AF.Exp, scale=-1.0), reads=[b_ps[sbk]], writes=[b_Eb[eb]])
                    c.op("act", lambda e: e.activation(out=SPb[s3], in_=Eb[eb], func=AF.Ln, bias=ONE), reads=[b_Eb[eb], b_cst], writes=[b_SPb[s3]])
                    c.op("dve", lambda e: e.tensor_tensor(out=Lpb[s3], in0=ps[sbk][:, :], in1=SPb[s3], op=ALU.add),
                         reads=[b_ps[sbk], b_SPb[s3]], writes=[b_Lpb[s3]])
                    if g["diag"]:
                        mo = g["moff"]
                        c.op("pool", lambda e: e.tensor_tensor(out=Lpb[s3], in0=Lpb[s3], in1=mlt[:, mo:mo + 512], op=ALU.mult),
                             reads=[b_mlt], writes=[b_Lpb[s3]])
                    if not g["last"]:
                        nx = (lc + 1) % 3
                        if g["first"]:
                            c.op("dve", lambda e: e.tensor_copy(out=Lcb[nx], in_=Lpb[s3]), reads=[b_Lpb[s3]], writes=[b_Lcb[nx]])
                        else:
                            c.op("dve", lambda e: e.tensor_tensor(out=Lcb[nx], in0=Lcb[lc], in1=Lpb[s3], op=ALU.add),
                                 reads=[b_Lcb[lc], b_Lpb[s3]], writes=[b_Lcb[nx]])

                def st_sb_w(m):
                    g = geom(m)
                    s3, wbk, lc, first = g["s3"], g["wbk"], g["lc"], g["first"]
                    mm(ps[wbk][:, :], negid[:], SPb[s3], True, False, [b_negid, b_SPb[s3]], [b_ps[wbk]])
                    mm(ps[wbk][:, :], negtri[:], Lpb[s3], False, first, [b_negtri, b_Lpb[s3]], [b_ps[wbk]])
                    if not first:
                        mm(ps[wbk][:, :], negones[:], Lcb[lc], False, True, [b_negones, b_Lcb[lc]], [b_ps[wbk]])

                def st_sb_expw(m):
                    g = geom(m)
                    wbk, pb = g["wbk"], g["pb"]
                    c.op("act", lambda e: e.activation(out=Pb[pb], in_=ps[wbk][:, :], func=AF.Exp), reads=[b_ps[wbk]], writes=[b_Pb[pb]])
                    if g["diag"]:
                        mo = g["moff"]
                        c.op("pool", lambda e: e.tensor_tensor(out=Pb[pb], in0=Pb[pb], in1=mlt[:, mo:mo + 512], op=ALU.mult),
                             reads=[b_mlt], writes=[b_Pb[pb]])

                def st_sb_pv(m):
                    g = geom(m)
                    h, q, A, pb, par = g["h"], g["q"], g["A"], g["pb"], g["par"]
                    cc, hp = h // 2, h % 2
                    vsl = slice(cc * 128, (cc + 1) * 128)
                    prt = slice(hp * 64, (hp + 1) * 64)
                    ob = 5 + par
                    mm(ps[ob][:, :], Vt[:, A, vsl], Pb[pb], g["first"], g["last"], [b_Vt[A], b_Pb[pb]], [b_ps[ob]])
                    if g["last"]:
                        qs = g["qs"]
                        c.op("act", lambda e: e.activation(out=Yg[prt, cc, qs], in_=ps[ob][prt, :], func=AF.Copy), reads=[b_ps[ob]], writes=[b_Yg[cc]])

                stages = [st_score, st_fox_exp, st_fox_pv] if is_fox else [st_score, st_sb_elem, st_sb_w, st_sb_expw, st_sb_pv]
                NP_ = len(pairs)
                for n in range(NP_ + len(stages) - 1):
                    for k in range(len(stages) - 1, -1, -1):
                        m = n - k
                        if 0 <= m < NP_:
                            stages[k](m)
                group_rms_and_proj(grp, False, Eb[0], b_Eb[0], Eb[1], b_Eb[1])

            c.barrier()
            conv_tick()
            b_T1, b_T2, b_T3, b_T4 = Buf("T1"), Buf("T2"), Buf("T3"), Buf("T4")
            b_Yg = bufs(2, "Yg"); b_yTn = Buf("yTn"); b_rden = Buf("rden"); b_woutg = Buf("woutg")
            for j in range(2):
                load_w(w_in_l, OFF_SC + GW + j * 128, 128, w_a, b_w_a)
                load_w(w_in_l, OFF_SC + 2 * GW + j * 128, 128, w_b, b_w_b)
                for q in range(NQ):
                    sl = slice(q * 512, (q + 1) * 512)
                    projT(w_a, b_w_a, 128, q, q % 4)
                    copy_op("act", T3[:, sl], ps[q % 4][:, :], [b_ps[q % 4]], [b_T3])
                c.op("pool", lambda e: e.memset(T1[:, 0:4], 0.0), writes=[b_T1])
                for q in range(NQ):
                    projT(w_b, b_w_b, 128, q, 4 + q % 4)
                    c.op("dve", lambda e, q=q: e.tensor_tensor(out=T1[:, 4 + q * 512:4 + (q + 1) * 512], in0=ps[4 + q % 4][:, :],
                                                               in1=T3[:, q * 512:(q + 1) * 512], op=ALU.mult),
                         reads=[b_ps[4 + q % 4], b_T3], writes=[b_T1])
                c.op("dve", lambda e, j=j: e.tensor_scalar(out=T2[:, 0:S], in0=T1[:, 4:4 + S], scalar1=scp[:, j, 2:3], scalar2=None,
                                                           op0=ALU.mult), reads=[b_T1, b_scp], writes=[b_T2])
                for k in range(2):
                    c.op("dve", lambda e, j=j, k=k: e.scalar_tensor_tensor(
                        out=T2[:, 0:S], in0=T1[:, 2 + k:2 + k + S], scalar=scp[:, j, k:k + 1], in1=T2[:, 0:S],
                        op0=ALU.mult, op1=ALU.add), reads=[b_T1, b_scp], writes=[b_T2])
                load_w(w_in_l, OFF_SC + j * 128, 128, w_a, b_w_a)
                for q in range(NQ):
                    projT(w_a, b_w_a, 128, q, q % 4)
                    c.op("dve", lambda e, q=q, j=j: e.tensor_tensor(out=Yg[:, j, q * 512:(q + 1) * 512], in0=ps[q % 4][:, :],
                                                                    in1=T2[:, q * 512:(q + 1) * 512], op=ALU.mult),
                         reads=[b_ps[q % 4], b_T2], writes=[b_Yg[j]])
            group_rms_and_proj(3, False, T3, b_T3, T4, b_T4)
            c.barrier()

        def xattn(l, s):
            c.barrier()
            K1 = 1024
            wqx = V(0, [128, KC, D], BF16); b_wqx = Buf("wqx")
            woh = V(16 * K1, [128, 2, D], BF16); b_woh = Buf("woh")
            KT = V(20 * K1, [128, 8, NMEM], BF16); b_KT = Buf("KT")
            Vx = V(24 * K1, [128, 2, D], BF16); b_Vx = Buf("Vx")
            memf = V(28 * K1, [128, KC, NMEM]); b_memf = Buf("memf")
            memb = V(36 * K1, [128, KC, NMEM], BF16); b_memb = Buf("memb")
            qTx = V(40 * K1, [128, 2, 512], BF16); b_qTx = Buf("qTx")
            pTx = [V(42 * K1 + i * K1, [128, 512], BF16) for i in range(2)]; b_pTx = bufs(2, "pTx")
            oTn = V(46 * K1, [128, 2, 512], BF16); b_oTn = Buf("oTn")
            rdx = V(48 * K1, [128, 512]); b_rdx = Buf("rdx")
            load_w(xwq_d[l], 0, D, wqx, b_wqx, eng_cast="act")
            c.dma("sp", lambda e: e.dma_start(out=memf, in_=memT_d[s].rearrange("(kc p) m -> p kc m", p=128)), writes=[b_memf], sembuf=b_misc)
            c.op("act", lambda e: e.activation(out=memb, in_=memf, func=AF.Copy), reads=[b_memf], writes=[b_memb])
            for cch in range(8):
                load_w(xwkv_d[l], cch * 128, 128, w_a, b_w_a, eng_cast="act")
                bk = cch % 8
                for kc in range(KC):
                    mm(ps[bk][:, 0:NMEM], w_a[:, kc, 0:128], memb[:, kc, :], kc == 0, kc == KC - 1, [b_w_a, b_memb], [b_ps[bk]])
                copy_op(evac_eng(), KT[:, cch, :], ps[bk][:, 0:NMEM], [b_ps[bk]], [b_KT])
            for cch in range(8):
                load_w(xwkv_d[l], D + cch * 128, 128, w_b, b_w_b, eng_cast="act")
                for mt in range(2):
                    bk = (cch * 2 + mt) % 8
                    for kc in range(KC):
                        mm(ps[bk][:, 0:128], memb[:, kc, mt * 128:(mt + 1) * 128], w_b[:, kc, 0:128], kc == 0, kc == KC - 1,
                           [b_w_b, b_memb], [b_ps[bk]])
                    copy_op(evac_eng(), Vx[:, mt, cch * 128:(cch + 1) * 128], ps[bk][:, 0:128], [b_ps[bk]], [b_Vx])
            woh2 = [woh, V(50 * K1, [128, 2, D], BF16)]; b_woh2 = bufs(2, "woh2")
            qTx2 = [qTx, V(54 * K1, [128, 2, 512], BF16)]; b_qTx2 = bufs(2, "qTx2")
            pTx2 = [[V(56 * K1 + (2 * i + j) * K1, [128, 512], BF16) for j in range(2)] for i in range(2)]
            b_pTx2 = [bufs(2, "pTxa"), bufs(2, "pTxb")]
            oTn2 = [oTn, V(60 * K1, [128, 2, 512], BF16)]; b_oTn2 = bufs(2, "oTn2")
            its = [(h, q) for h in range(4) for q in range(NQ)]

            def stA(i):
                h, q = its[i]
                if q == 0:
                    load_rows(xwo_d[l][h * 256:(h + 1) * 256, :], woh2[h % 2], b_woh2[h % 2], eng_cast="act")
                for c2 in range(2):
                    projT(wqx, b_wqx, 128, q, 0, c0=h * 256 + c2 * 128)
                    copy_op(evac_eng(), qTx2[i % 2][:, c2, :], ps[0][:, :], [b_ps[0]], [b_qTx2[i % 2]], scale=1.0 / 16.0)

            def stB(i):
                h, q = its[i]
                for mt in range(2):
                    for c2 in range(2):
                        mm(ps[1 + mt][:, :], KT[:, 2 * h + c2, mt * 128:(mt + 1) * 128], qTx2[i % 2][:, c2, :], c2 == 0, c2 == 1,
                           [b_KT, b_qTx2[i % 2]], [b_ps[1 + mt]])
                    c.op("act", lambda e, mt=mt: e.activation(out=pTx2[i % 2][mt], in_=ps[1 + mt][:, :], func=AF.Exp),
                         reads=[b_ps[1 + mt]], writes=[b_pTx2[i % 2][mt]])

            def stC(i):
                h, q = its[i]
                pt, bpt = pTx2[i % 2], b_pTx2[i % 2]
                for mt in range(2):
                    mm(ps[3][:, :], ones_bf[:], pt[mt], mt == 0, mt == 1, [b_ones_bf, bpt[mt]], [b_ps[3]])
                for c2 in range(2):
                    for mt in range(2):
                        mm(ps[4 + c2][:, :], Vx[:, mt, h * 256 + c2 * 128:h * 256 + (c2 + 1) * 128], pt[mt], mt == 0, mt == 1,
                           [b_Vx, bpt[mt]], [b_ps[4 + c2]])
                c.op("dve", lambda e: e.reciprocal(out=rdx, in_=ps[3][:, :]), reads=[b_ps[3]], writes=[b_rdx])
                for c2 in range(2):
                    c.op("dve", lambda e, c2=c2: e.tensor_tensor(out=oTn2[i % 2][:, c2, :], in0=ps[4 + c2][:, :], in1=rdx, op=ALU.mult),
                         reads=[b_ps[4 + c2], b_rdx], writes=[b_oTn2[i % 2]])

            def stD(i):
                h, q = its[i]
                for tsub in range(4):
                    tt = 4 * q + tsub
                    for hf in range(2):
                        bk = 6 + (tsub * 2 + hf) % 2
                        for c2 in range(2):
                            mm(ps[bk][:, :], oTn2[i % 2][:, c2, tsub * 128:(tsub + 1) * 128], woh2[h % 2][:, c2, hf * 512:(hf + 1) * 512],
                               c2 == 0, c2 == 1, [b_oTn2[i % 2], b_woh2[h % 2]], [b_ps[bk]])
                        if h == 0:
                            c.op("dve", lambda e, tt=tt, hf=hf, bk=bk: e.scalar_tensor_tensor(
                                out=xs[:, tt, hf * 512:(hf + 1) * 512], in0=xs[:, tt, hf * 512:(hf + 1) * 512], scalar=ALPHA,
                                in1=ps[bk][:, :], op0=ALU.mult, op1=ALU.add), reads=[b_ps[bk]], writes=[b_xs[tt]])
                        else:
                            c.op("dve", lambda e, tt=tt, hf=hf, bk=bk: e.tensor_tensor(
                                out=xs[:, tt, hf * 512:(hf + 1) * 512], in0=xs[:, tt, hf * 512:(hf + 1) * 512],
                                in1=ps[bk][:, :], op=ALU.add), reads=[b_ps[bk]], writes=[b_xs[tt]])

            xst = [stA, stB, stC, stD]
            NI = len(its)
            for n in range(NI + len(xst) - 1):
                for k in range(len(xst) - 1, -1, -1):
                    i = n - k
                    if 0 <= i < NI:
                        xst[k](i)
            c.barrier()

        def peer(l, s, is_last):
            c.barrier()
            K1 = 1024
            s_all = V(0, [128, 2, 16, 128]); b_sall = bufs(2, "sall")
            s_all_u = V(0, [128, 2, 16, 128], U32)
            idx_all = V(16 * K1, [128, 4, 128], U32); b_idx = bufs(4, "idx")
            gate_all = V(18 * K1, [128, 4, 128]); b_gate = bufs(4, "gate")
            NGA = 8
            NG = NGA + 2
            G = [V(20 * K1 + i * 4 * K1, [128, 2 * D], BF16) for i in range(NGA)]
            G.append(w_b[:].rearrange("p a b -> p (a b)"))
            G.append(wstg[1][:].rearrange("p a b -> p (a b)").bitcast(BF16))
            wst_n[0] = 1
            o = 20 * K1 + NGA * 4 * K1
            qTp = V(o, [128, 256], BF16); b_qTp = Buf("qTp"); o += 512
            o_qtp2 = o; o += 512
            kTb = V(o, [128, 2, 128], BF16); b_kTb = Buf("kTb"); o += 512
            kst = V(o, [128, 2, 128]); b_kst = Buf("kst"); o += 1024
            xb = V(o, [128, D], BF16); b_xb = Buf("xb"); o += 2048
            Vt_ = V(o, [128, 8, 2, 16]); Vt_u = V(o, [128, 8, 2, 16], U32); b_V = Buf("V"); o += 1024
            tmp128 = V(o, [128, 128]); b_tmp = Buf("tmp128"); o += 512
            I12u = V(o, [128, 8, 2, 16], U32); b_I12u = Buf("I12u"); o += 1024
            I12f = V(o, [128, 8, 2, 16], BF16); b_I12f = Buf("I12f"); o += 512
            iota16b = V(o, [128, 16], BF16); b_iota16b = Buf("iota16b"); o += 32
            e12b = V(o, [128, 8, 2, 16], BF16); o += 512
            cand2 = V(o, [128, 256]); b_cand2 = Buf("cand2"); o += 1024
            SC = V(o, [128, 8, 16]); SCu = V(o, [128, 8, 16], U32); b_SC = Buf("SC"); o += 512
            abu = V(o, [128, 8, 2, 16], U32); b_abu = Buf("abu"); o += 1024
            abf = V(o, [128, 8, 2, 16], BF16); b_abf = Buf("abf"); o += 512
            oh = V(o, [128, 8, 16, 16], BF16); b_oh = Buf("oh"); o += 4096
            e12 = V(o, [128, 8, 2, 16]); b_e12 = Buf("e12"); o += 1024
            exf = V(o, [128, 8, 16]); b_exf = Buf("exf"); o += 512
            exg = V(o, [128, 8, 16]); b_exg = Buf("exg"); o += 512
            sm = V(o, [128, 2, 8]); b_sm = Buf("sm"); o += 64
            actt = V(o, [128, 128]); b_act = bufs(64, "act"); o += 512
            coef = V(o, [128, 128]); b_coef = bufs(64, "coef"); o += 512
            NDG = 8
            dg = [V(o + i * 256, [128, 128], BF16) for i in range(NDG)]; b_dg = bufs(NDG, "dg"); o += NDG * 256
            identb = V(o, [128, 128], BF16); b_identb = Buf("identb"); o += 256
            io128u = V(o, [128, 128], U32); b_io128 = Buf("io128"); o += 512
            io256u = V(o, [128, 256], U32); b_io256 = Buf("io256"); o += 1024
            assert o <= LN_OFF, o
            c.op("dve", lambda e: e.tensor_copy(out=identb, in_=ident[:]), reads=[b_ident], writes=[b_identb])
            c.op("dve", lambda e: e.tensor_copy(out=iota16b, in_=iota16[:]), reads=[b_iota], writes=[b_iota16b])
            c.dma("sp", lambda e: e.dma_start(out=cand2, in_=ciota256_d[:, :]), writes=[b_cand2], sembuf=b_misc)
            c.op("dve", lambda e: e.tensor_copy(out=io256u, in_=cand2), reads=[b_cand2], writes=[b_io256])
            c.op("dve", lambda e: e.tensor_copy(out=io128u, in_=cand2[:, 0:128]), reads=[b_cand2], writes=[b_io128])
            b_lnp = load_ln(l, 2)
            c.dma("sp", lambda e: e.dma_start(out=kst, in_=k12T_d[l]), writes=[b_kst], sembuf=b_misc)
            c.op("dve", lambda e: e.tensor_copy(out=kTb, in_=kst), reads=[b_kst], writes=[b_kTb])
            out_evs = []

            b_wa2 = bufs(2, "wa2")
            b_qTp2 = bufs(2, "qTp2")
            qTp2 = [qTp, V(o_qtp2, [128, 256], BF16)]

            def score_thunks(blk):
                def t1(ch):
                    wv = w_a[:, :, (ch % 2) * 128:(ch % 2 + 1) * 128]
                    load_w(pwq_d[l], ch * 128, 128, wv, b_wa2[ch % 2], eng_cast="act")

                def t2(ch):
                    wv = w_a[:, :, (ch % 2) * 128:(ch % 2 + 1) * 128]
                    for kc in range(KC):
                        mm(ps[0][:, 0:256], wv[:, kc, :], xT[:, kc, blk * 256:(blk + 1) * 256], kc == 0, kc == KC - 1,
                           [b_wa2[ch % 2]] + b_xT[2 * blk:2 * blk + 2], [b_ps[0]])
                    copy_op("act", qTp2[ch % 2], ps[0][:, 0:256], [b_ps[0]], [b_qTp2[ch % 2]])

                def t3(ch):
                    for ti in range(2):
                        bk2 = 1 + ti
                        mm(ps[bk2][:, 0:128], qTp2[ch % 2][:, ti * 128:(ti + 1) * 128], kTb[:, ch % 2, :], True, True,
                           [b_qTp2[ch % 2], b_kTb], [b_ps[bk2]])
                        copy_op("act", s_all[:, ti, ch, :], ps[bk2][:, 0:128], [b_ps[bk2]], [b_sall[ti]])

                th = []
                for i in range(18):
                    def one(i=i):
                        if i - 2 >= 0:
                            t3(i - 2)
                        if 0 <= i - 1 < 16:
                            t2(i - 1)
                        if i < 16:
                            t1(i)
                    th.append(one)
                    th.append(lambda: None)
                return th

            def routing_thunks(blk):
                th = []
                A = th.append
                for ti in range(2):
                    tt = blk * 2 + ti
                    t4 = tt % 4
                    bs = b_sall[ti]
                    Su = s_all_u[:, ti].rearrange("p a b -> p (a b)")
                    S3u = s_all_u[:, ti]
                    A(lambda Su=Su, bs=bs: c.op("dve", lambda e: e.tensor_single_scalar(out=Su, in_=Su, scalar=0xFFFFFF80, op=ALU.bitwise_and),
                                                reads=[], writes=[bs]))
                    A(lambda S3u=S3u, bs=bs: c.op("dve", lambda e: e.tensor_tensor(out=S3u, in0=S3u, in1=io128u.unsqueeze(1).to_broadcast([128, 16, 128]),
                                                                                     op=ALU.bitwise_or), reads=[b_io128], writes=[bs]))
                    for ch in range(16):
                        h, sd = ch // 2, ch % 2
                        sv = s_all[:, ti, ch, :]
                        A(lambda sv=sv, h=h, sd=sd, bs=bs: c.op("dve", lambda e: e.max(out=Vt_[:, h, sd, 0:8], in_=sv), reads=[bs], writes=[b_V]))
                        A(lambda sv=sv, h=h, sd=sd, bs=bs: c.op("dve", lambda e: e.match_replace(out=tmp128, in_to_replace=Vt_[:, h, sd, 0:8], in_values=sv, imm_value=NEG),
                                                                 reads=[bs, b_V], writes=[b_tmp]))
                        A(lambda h=h, sd=sd: c.op("dve", lambda e: e.max(out=Vt_[:, h, sd, 8:16], in_=tmp128), reads=[b_tmp], writes=[b_V]))
                    A(lambda: c.op("dve", lambda e: e.tensor_single_scalar(out=I12u, in_=Vt_u, scalar=127, op=ALU.bitwise_and), reads=[b_V], writes=[b_I12u]))
                    A(lambda: c.op("dve", lambda e: e.tensor_copy(out=I12f, in_=I12u), reads=[b_I12u], writes=[b_I12f]))
                    cand4 = s_all[:, ti].rearrange("p a b -> p (a b)").rearrange("p (h a b) -> p h a b", h=8, a=16)
                    cand3 = s_all[:, ti].rearrange("p a b -> p (a b)").rearrange("p (h q) -> p h q", h=8)
                    cand3u = s_all_u[:, ti].rearrange("p a b -> p (a b)").rearrange("p (h q) -> p h q", h=8)
                    A(lambda cand4=cand4, bs=bs: c.op("dve", lambda e: e.tensor_tensor(
                        out=cand4, in0=Vt_[:, :, 0, :].unsqueeze(3).to_broadcast([128, 8, 16, 16]),
                        in1=Vt_[:, :, 1, :].unsqueeze(2).to_broadcast([128, 8, 16, 16]), op=ALU.add), reads=[b_V], writes=[bs]))
                    A(lambda cand3u=cand3u, bs=bs: c.op("dve", lambda e: e.tensor_single_scalar(out=cand3u, in_=cand3u, scalar=0xFFFFFF00, op=ALU.bitwise_and),
                                                        reads=[], writes=[bs]))
                    A(lambda cand3u=cand3u, bs=bs: c.op("dve", lambda e: e.tensor_tensor(out=cand3u, in0=cand3u, in1=io256u.unsqueeze(1).to_broadcast([128, 8, 256]),
                                                                                         op=ALU.bitwise_or), reads=[b_io256], writes=[bs]))
                    for h in range(8):
                        cv = cand3[:, h, :]
                        A(lambda cv=cv, h=h, bs=bs: c.op("dve", lambda e: e.max(out=SC[:, h, 0:8], in_=cv), reads=[bs], writes=[b_SC]))
                        A(lambda cv=cv, h=h, bs=bs: c.op("dve", lambda e: e.match_replace(out=cand2, in_to_replace=SC[:, h, 0:8], in_values=cv, imm_value=NEG),
                                                         reads=[bs, b_SC], writes=[b_cand2]))
                        A(lambda h=h: c.op("dve", lambda e: e.max(out=SC[:, h, 8:16], in_=cand2), reads=[b_cand2], writes=[b_SC]))
                    A(lambda: c.op("dve", lambda e: e.tensor_scalar(out=abu[:, :, 0, :], in0=SCu, scalar1=255, scalar2=4, op0=ALU.bitwise_and,
                                                                    op1=ALU.logical_shift_right), reads=[b_SC], writes=[b_abu]))
                    A(lambda: c.op("dve", lambda e: e.tensor_single_scalar(out=abu[:, :, 1, :], in_=SCu, scalar=15, op=ALU.bitwise_and), reads=[b_SC], writes=[b_abu]))
                    A(lambda: c.op("dve", lambda e: e.tensor_copy(out=abf, in_=abu), reads=[b_abu], writes=[b_abf]))
                    for sd in range(2):
                        A(lambda sd=sd: c.op("dve", lambda e: e.tensor_tensor(
                            out=oh, in0=abf[:, :, sd, :].unsqueeze(3).to_broadcast([128, 8, 16, 16]),
                            in1=iota16b.unsqueeze(1).unsqueeze(1).to_broadcast([128, 8, 16, 16]), op=ALU.is_equal), reads=[b_abf, b_iota16b], writes=[b_oh]))
                        A(lambda sd=sd: c.op("dve", lambda e: e.tensor_tensor(
                            out=oh, in0=oh, in1=I12f[:, :, sd, :].unsqueeze(2).to_broadcast([128, 8, 16, 16]), op=ALU.mult), reads=[b_I12f], writes=[b_oh]))
                        A(lambda sd=sd: c.op("dve", lambda e: e.reduce_sum(out=e12[:, :, sd, :], in_=oh, axis=AX.X), reads=[b_oh], writes=[b_e12]))
                    A(lambda: c.op("dve", lambda e: e.scalar_tensor_tensor(out=exf, in0=e12[:, :, 0, :], scalar=128.0, in1=e12[:, :, 1, :],
                                                                           op0=ALU.mult, op1=ALU.add), reads=[b_e12], writes=[b_exf]))
                    A(lambda t4=t4: c.op("dve", lambda e: e.tensor_copy(out=idx_all[:, t4, :].rearrange("p (h k) -> p h k", h=8), in_=exf),
                                         reads=[b_exf], writes=[b_idx[t4]]))
                    A(lambda: c.op("dve", lambda e: e.tensor_tensor(out=exg, in0=SC, in1=SC[:, :, 0:1].to_broadcast([128, 8, 16]), op=ALU.subtract),
                                   reads=[b_SC], writes=[b_exg]))
                    A(lambda: c.op("act", lambda e: e.activation(out=exg, in_=exg, func=AF.Exp), reads=[], writes=[b_exg]))
                    A(lambda: c.op("dve", lambda e: e.reduce_sum(out=sm[:, 0, :], in_=exg, axis=AX.X), reads=[b_exg], writes=[b_sm]))
                    A(lambda: c.op("dve", lambda e: e.reciprocal(out=sm[:, 1, :], in_=sm[:, 0, :]), reads=[], writes=[b_sm]))
                    A(lambda t4=t4: c.op("dve", lambda e: e.tensor_tensor(out=gate_all[:, t4, :].rearrange("p (h k) -> p h k", h=8), in0=exg,
                                                                         in1=sm[:, 1, :].unsqueeze(2).to_broadcast([128, 8, 16]), op=ALU.mult),
                                         reads=[b_exg, b_sm], writes=[b_gate[t4]]))
                return th

            gi = [0]
            cgt = V(o - 0, [128, 1]) if False else None
            b_act1 = bufs(16, "act1"); b_coef1 = bufs(16, "coef1")
            D_DOT, D_ACC, D_GELU, D_CG, D_DG, D_MM = 2, 3, 4, 5, 6, 7

            def gathers(blk, pending):
                stream = [(blk * 2 + ti, sl_) for ti in range(2) for sl_ in range(128)]
                N = len(stream)
                per = (len(pending) + 239) // 240 if pending else 0
                gmap = {}
                for n in range(N + D_MM):
                    m = n
                    if 0 <= m < N:
                        tt, sl_ = stream[m]
                        t4 = tt % 4
                        g = gi[0] % NG; gi[0] += 1
                        gmap[m] = g
                        c.dma("pool", lambda e, g=g, t4=t4, sl_=sl_: e.indirect_dma_start(
                            out=G[g], out_offset=None, in_=uvb_d[l][:, :],
                            in_offset=bass.IndirectOffsetOnAxis(ap=idx_all[:, t4, sl_:sl_ + 1], axis=0)),
                            reads=[b_idx[t4]] + b_uvb[l], writes=[b_G[g]])
                    m = n - D_DOT
                    if 0 <= m < N:
                        tt, sl_ = stream[m]
                        g = gmap[m]
                        if sl_ == 0:
                            c.op("act", lambda e, tt=tt: e.activation(out=xb, in_=xs[:, tt, :], func=AF.Copy), reads=[b_xs[tt]], writes=[b_xb])
                        if sl_ % 2 == 0:
                            c.op("dve", lambda e, g=g, tt=tt, sl_=sl_: e.scalar_tensor_tensor(
                                out=G[g][:, 0:D], in0=G[g][:, 0:D], scalar=1.0, in1=xs[:, tt, :], op0=ALU.mult, op1=ALU.mult,
                                accum_out=actt[:, sl_:sl_ + 1]), reads=[b_xs[tt]], writes=[b_G[g], b_act1[m % 16]])
                        else:
                            c.op("dve", lambda e, g=g: e.tensor_tensor(out=G[g][:, 0:D], in0=G[g][:, 0:D], in1=xb, op=ALU.mult),
                                 reads=[b_xb], writes=[b_G[g]])
                    m = n - D_ACC
                    if 0 <= m < N and stream[m][1] % 2 == 1:
                        tt, sl_ = stream[m]
                        g = gmap[m]
                        c.op("act", lambda e, sl_=sl_, g=g: e.activation(out=G[g][:, 0:D], in_=G[g][:, 0:D], func=AF.Copy, accum_out=actt[:, sl_:sl_ + 1]),
                             reads=[], writes=[b_G[g], b_act1[m % 16]])
                    for m in (n - D_GELU, n - D_GELU + 1):
                        if not (0 <= m < N) or (stream[m][1] % 2 == 1) != (m == n - D_GELU):
                            continue
                        tt, sl_ = stream[m]
                        c.op("act", lambda e, sl_=sl_: e.activation(out=coef[:, sl_:sl_ + 1], in_=actt[:, sl_:sl_ + 1], func=AF.Gelu),
                             reads=[b_act1[m % 16]], writes=[b_coef1[m % 16]])
                    for m in (n - D_CG, n - D_CG + 1):
                        if not (0 <= m < N) or (stream[m][1] % 2 == 1) != (m == n - D_CG):
                            continue
                        tt, sl_ = stream[m]
                        t4 = tt % 4
                        c.op("dve", lambda e, sl_=sl_, t4=t4: e.tensor_tensor(out=coef[:, sl_:sl_ + 1], in0=coef[:, sl_:sl_ + 1],
                                                                             in1=gate_all[:, t4, sl_:sl_ + 1], op=ALU.mult),
                             reads=[b_gate[t4]], writes=[b_coef1[m % 16]])
                    for m in (n - D_DG, n - D_DG + 1):
                        if not (0 <= m < N) or (stream[m][1] % 2 == 1) != (m == n - D_DG):
                            continue
                        tt, sl_ = stream[m]
                        d_ = m % NDG
                        c.op("act", lambda e, sl_=sl_, d_=d_: e.activation(out=dg[d_], in_=identb, func=AF.Copy, scale=coef[:, sl_:sl_ + 1]),
                             reads=[b_identb, b_coef1[m % 16]], writes=[b_dg[d_]])
                    for m in (n - D_MM, n - D_MM + 1):
                        if not (0 <= m < N) or (stream[m][1] % 2 == 1) != (m == n - D_MM):
                            continue
                        tt, sl_ = stream[m]
                        g = gmap[m]
                        d_ = m % NDG
                        AB = 4 + 2 * (tt % 2)
                        for hf in range(2):
                            mm(ps[AB + hf][:, :], dg[d_], G[g][:, D + hf * 512:D + (hf + 1) * 512], sl_ == 0, sl_ == 127,
                               [b_dg[d_], b_G[g]], [b_ps[AB + hf]])
                        if sl_ == 127:
                            for hf in range(2):
                                c.op("dve", lambda e, tt=tt, hf=hf, AB=AB: e.scalar_tensor_tensor(
                                    out=xs[:, tt, hf * 512:(hf + 1) * 512], in0=xs[:, tt, hf * 512:(hf + 1) * 512], scalar=ALPHA,
                                    in1=ps[AB + hf][:, :], op0=ALU.mult, op1=ALU.add), reads=[b_ps[AB + hf]], writes=[b_xs[tt]])
                            layer_norm([tt], b_lnp, aff_eng="dve")
                            if is_last:
                                out_evs.append(c.dma("sp", lambda e, tt=tt: e.dma_start(out=out_d[s, tt * 128:(tt + 1) * 128, :], in_=xs[:, tt, :]),
                                                     reads=[b_xs[tt]], writes=[], sembuf=b_outd))
                    for _ in range(per):
                        if pending:
                            pending.pop(0)()
                while pending:
                    pending.pop(0)()

            b_G = bufs(NG, "G")
            for t_ in score_thunks(0) + routing_thunks(0):
                t_()
            for blk in range(NB2):
                pending = []
                if blk + 1 < NB2:
                    pending = score_thunks(blk + 1) + routing_thunks(blk + 1)
                gathers(blk, pending)
            c.barrier()
            wst_n[0] = WST
            return out_evs

        b_uvb = [bufs(4, f"uvb{l}") for l in range(L)]
        CH = 2048

        def emit_conv(l, part):
            if "peer" not in phases:
                return
            fns = []
            for r in range(part * 2, part * 2 + 2):
                fns.append(lambda e, r=r: e.dma_start(out=uvb_d[l][r * CH:(r + 1) * CH, 0:D], in_=pu_d[l][r * CH:(r + 1) * CH, :]))
                fns.append(lambda e, r=r: e.dma_start(out=uvb_d[l][r * CH:(r + 1) * CH, D:2 * D], in_=pv_d[l][r * CH:(r + 1) * CH, :]))
            c.dma("pool", fns, writes=[b_uvb[l][part]])

        conv_state = {"todo": []}

        def conv_tick():
            if conv_state["todo"]:
                l_, p_ = conv_state["todo"].pop(0)
                emit_conv(l_, p_)

        out_events = []
        for s in range(NSEQ):
            c.barrier()
            c.dma("sp", [(lambda e, s=s, tt=tt: e.dma_start(out=xs[:, tt, :], in_=x_d[s, tt * 128:(tt + 1) * 128, :]))
                         for tt in range(NT)], writes=list(b_xs), sembuf=b_misc)
            dumped = False
            for l in range(L):
                if "mixer" in phases:
                    if s == 0:
                        conv_state["todo"] = [(l, p_) for p_ in range(4)]
                    make_xT()
                    mixer(l)
                    while conv_state["todo"]:
                        conv_tick()
                    b_lnp = load_ln(l, 0)
                    layer_norm(range(NT), b_lnp, add_eng="pool")
                if stop_after == (l, "ln1"):
                    break
                if "xattn" in phases:
                    make_xT()
                    xattn(l, s)
                    b_lnp = load_ln(l, 1)
                    layer_norm(range(NT), b_lnp, add_eng="pool")
                if stop_after == (l, "ln2"):
                    break
                if "peer" in phases:
                    make_xT()
                    last = (l == L - 1) or stop_after == (l, "ln3")
                    evs = peer(l, s, last)
                    if last:
                        out_events.extend(evs)
                        dumped = True
                if stop_after == (l, "ln3"):
                    break
            if not dumped:
                for tt in range(NT):
                    out_events.append(c.dma("sp", lambda e, s=s, tt=tt: e.dma_start(out=out_d[s, tt * 128:(tt + 1) * 128, :], in_=xs[:, tt, :]),
                                            reads=[b_xs[tt]], writes=[], sembuf=b_outd))
        c.emit(out_events[-1:])
    return nc


def host_consts():
    kp = np.arange(128)[:, None]
    xx = np.arange(896)[None, :]
    nble = np.where((xx - 384) >= kp, 0.0, -30000.0).astype(np.float32)
    mlt = ((xx - 384) > kp).astype(np.float32)
    jj = np.arange(128)[:, None]
    ss = np.arange(128)[None, :]
    negtri = -(jj > ss).astype(np.float32)
    return {
        "c_ident": np.eye(128, dtype=np.float32),
        "c_nble": nble, "c_mlt": mlt, "c_negtri": negtri,
        "c_iota16": np.tile(np.arange(16, dtype=np.float32)[None, :], (128, 1)),
        "c_iota256": np.tile(np.arange(256, dtype=np.float32)[None, :], (128, 1)),
    }


def host_weights(inp, L):
    f = lambda a: np.ascontiguousarray(np.asarray(a, dtype=np.float32))
    w = {}
    w["w_in"] = f(inp["w_in"][:L]); w["w_out"] = f(inp["w_out"][:L])
    rgp = np.zeros((L, 2, 128, 8), np.float32)
    cw = np.asarray(inp["rg_conv_w"])[:L]
    for k in range(4):
        rgp[:, :, :, k] = cw[:, k, :].reshape(L, 2, 128)
    rgp[:, :, :, 4] = np.asarray(inp["rg_conv_b"])[:L].reshape(L, 2, 128)
    rgp[:, :, :, 5] = np.asarray(inp["rg_ba"])[:L].reshape(L, 2, 128)
    rgp[:, :, :, 6] = np.asarray(inp["rg_bi"])[:L].reshape(L, 2, 128)
    rgp[:, :, :, 7] = np.asarray(inp["rg_lambda"])[:L].reshape(L, 2, 128)
    w["rgp"] = rgp
    for nm, src in (("wa_bd", "rg_wa"), ("wi_bd", "rg_wi")):
        a = np.asarray(inp[src])[:L]
        bd = np.zeros((L, 2, 128, 128), np.float32)
        for h in range(4):
            j, o = h // 2, (h % 2) * 64
            bd[:, j, o:o + 64, o:o + 64] = a[:, h]
        w[nm] = bd
    w["fox_bf"] = f(np.asarray(inp["fox_bf"])[:L].reshape(L, 4, 1))
    scp = np.zeros((L, 2, 128, 4), np.float32)
    sw = np.asarray(inp["sc_conv_w"])[:L]
    for k in range(3):
        scp[:, :, :, k] = sw[:, k, :].reshape(L, 2, 128)
    w["scp"] = scp
    w["mng"] = f(np.asarray(inp["mix_norm_g"])[:L].reshape(L, 8, 128).transpose(0, 2, 1))
    w["ln_g"] = f(np.stack([np.asarray(inp["ln1_g"])[:L], np.asarray(inp["ln2_g"])[:L], np.asarray(inp["ln3_g"])[:L]], axis=1))
    w["ln_b"] = f(np.stack([np.asarray(inp["ln1_b"])[:L], np.asarray(inp["ln2_b"])[:L], np.asarray(inp["ln3_b"])[:L]], axis=1))
    w["xa_wq"] = f(inp["xa_wq"][:L]); w["xa_wkv"] = f(inp["xa_wkv"][:L]); w["xa_wo"] = f(inp["xa_wo"][:L])
    w["peer_wq"] = f(inp["peer_wq"][:L])
    w["k12T"] = f(np.stack([np.asarray(inp["peer_k1"])[:L].transpose(0, 2, 1),
                            np.asarray(inp["peer_k2"])[:L].transpose(0, 2, 1)], axis=2))
    for l in range(L):
        w[f"peer_u{l}"] = f(inp["peer_u"][l]); w[f"peer_v{l}"] = f(inp["peer_v"][l])
    w.update(host_consts())
    return w


def run(inp, n_cores=8, NSEQ=2, S=2048, L=2, stop_after=None, trace=False, phases=("mixer", "xattn", "peer")):
    nc = build_program(NSEQ=NSEQ, S=S, L=L, stop_after=stop_after, phases=phases)
    w = host_weights(inp, L)
    x = np.asarray(inp["x"], dtype=np.float32)
    mem = np.asarray(inp["mem"], dtype=np.float32)
    in_maps = []
    for ci in range(n_cores):
        m = dict(w)
        m["x"] = np.ascontiguousarray(x[ci * NSEQ:(ci + 1) * NSEQ, :S])
        m["memT"] = np.ascontiguousarray(mem[ci * NSEQ:(ci + 1) * NSEQ].transpose(0, 2, 1))
        in_maps.append(m)
    res = run_bass_kernel_spmd(nc, in_maps, core_ids=list(range(n_cores)), trace=trace)
    out = np.concatenate([r["out"] for r in res.results], axis=0)
    return out, res


def kernel(**inputs):
    out, _ = run(inputs)
    return out.astype(np.float32)
```

```python
import numpy as np
from contextlib import ExitStack
import concourse.bass as bass
import concourse.mybir as mybir
from concourse.bass_utils import run_bass_kernel_spmd

F32 = mybir.dt.float32
BF16 = mybir.dt.bfloat16
U32 = mybir.dt.uint32
AF = mybir.ActivationFunctionType
ALU = mybir.AluOpType
AX = mybir.AxisListType

D = 1024
KC = 8
GW = 256
OFF_RG_X = 0
OFF_RG_G = 256
OFF_FOX = 512
OFF_FOX_F = 1280
OFF_SB = 1284
OFF_SC = 2052
N_IN = 2820
NMEM = 256
NE = 16384
DEPTH = 2
ALPHA = (2.0 * DEPTH) ** 0.25
LN_EPS = 1e-5
NEG = -1.0e30

ENGS = ("pe", "act", "dve", "pool", "sp")


class Buf:
    __slots__ = ("name", "w", "r", "sem", "cnt")

    def __init__(self, name=""):
        self.name = name
        self.w = None
        self.r = {}
        self.sem = None
        self.cnt = 0


def bufs(n, name=""):
    return [Buf(f"{name}{i}") for i in range(n)]


class Ctx:
    def __init__(self, nc, stack):
        self.nc = nc
        self.stack = stack
        self.q = {e: [] for e in ENGS}
        self.esem = {}
        self.ecnt = {e: 0 for e in ENGS}
        for e in ("pe", "act", "dve", "pool"):
            self.esem[e] = stack.enter_context(nc.semaphore("s_" + e))
        self.nsem = 4
        self.dsb = []

    def barrier(self):
        evs = [(self.esem[e], self.ecnt[e], "x") for e in self.esem if self.ecnt[e] > 0]
        evs += [(b.sem, b.cnt, "dma") for b in self.dsb if b.cnt > 0]
        for e in ENGS:
            self.q[e].append((list(evs), None, None, 0))

    def newsem(self, name):
        self.nsem += 1
        return self.stack.enter_context(self.nc.semaphore(name))

    def _deps(self, reads, writes):
        deps = []
        for b in reads:
            if b.w is not None:
                deps.append(b.w)
        for b in writes:
            if b.w is not None:
                deps.append(b.w)
            deps.extend(b.r.values())
        return deps

    def _commit(self, ev, reads, writes):
        k = id(ev[0])
        for b in reads:
            o = b.r.get(k)
            if o is None or o[1] < ev[1]:
                b.r[k] = ev
        for b in writes:
            b.w = ev
            b.r = {}

    def op(self, eng, fn, reads=(), writes=()):
        deps = self._deps(reads, writes)
        self.ecnt[eng] += 1
        ev = (self.esem[eng], self.ecnt[eng], eng)
        self.q[eng].append((deps, fn, ev[0], 1))
        self._commit(ev, reads, writes)
        return ev

    def dma(self, eng, fns, reads=(), writes=(), sembuf=None):
        if not isinstance(fns, (list, tuple)):
            fns = [fns]
        sb = sembuf if sembuf is not None else writes[0]
        if sb.sem is None:
            sb.sem = self.newsem("d%d" % self.nsem)
            self.dsb.append(sb)
        deps = self._deps(reads, writes)
        if sb.cnt > 0:
            deps.append((sb.sem, sb.cnt, "dma"))
        for i, fn in enumerate(fns):
            self.q[eng].append((deps if i == 0 else [], fn, sb.sem, 16))
        sb.cnt += 16 * len(fns)
        ev = (sb.sem, sb.cnt, "dma")
        self._commit(ev, reads, writes)
        return ev

    def emit(self, final_events):
        nc = self.nc
        engmap = {"pe": "tensor", "act": "scalar", "dve": "vector", "pool": "gpsimd", "sp": "sync"}
        with nc.Block() as block:
            for e in ENGS:
                ops = self.q[e]
                fin = final_events if e == "sp" else []

                def body(engine, ops=ops, e=e, fin=fin):
                    waited = {}
                    for deps, fn, sem, inc in ops:
                        need = {}
                        for (s, v, pe) in deps:
                            if pe == e and e == "pe":
                                continue
                            k = id(s)
                            if waited.get(k, 0) >= v:
                                continue
                            if k not in need or need[k][1] < v:
                                need[k] = (s, v)
                        for k, (s, v) in need.items():
                            engine.wait_ge(s, v)
                            waited[k] = v
                        if fn is None:
                            continue
                        ins = fn(engine)
                        ins.then_inc(sem, inc)
                    for (s, v, pe) in fin:
                        engine.wait_ge(s, v)

                getattr(block, engmap[e])(body)


ARENA_BYTES = 87 * 1024 + 512
LN_OFF = 79 * 1024 + 512


def build_program(NSEQ=2, S=2048, L=2, stop_after=None, phases=("mixer", "xattn", "peer")):
    nc = bass.Bass("TRN2", target_bir_lowering=False)
    NT = S // 128
    NQ = S // 512
    NB2 = S // 256

    def din(name, shape, dt=F32):
        return nc.dram_tensor(name, list(shape), dt, kind="ExternalInput").ap()

    x_d = din("x", [NSEQ, S, D])
    memT_d = din("memT", [NSEQ, D, NMEM])
    w_in_d = din("w_in", [L, D, N_IN])
    w_out_d = din("w_out", [L, D, D])
    rgp_d = din("rgp", [L, 2, 128, 8])
    wa_d = din("wa_bd", [L, 2, 128, 128])
    wi_d = din("wi_bd", [L, 2, 128, 128])
    foxbf_d = din("fox_bf", [L, 4, 1])
    scp_d = din("scp", [L, 2, 128, 4])
    mng_d = din("mng", [L, 128, 8])
    lng_d = din("ln_g", [L, 3, D])
    lnb_d = din("ln_b", [L, 3, D])
    xwq_d = din("xa_wq", [L, D, D])
    xwkv_d = din("xa_wkv", [L, D, 2 * D])
    xwo_d = din("xa_wo", [L, D, D])
    pwq_d = din("peer_wq", [L, D, 2048])
    k12T_d = din("k12T", [L, 128, 2, 128])
    pu_d = [din(f"peer_u{l}", [NE, D]) for l in range(L)]
    pv_d = [din(f"peer_v{l}", [NE, D]) for l in range(L)]
    cid_d = din("c_ident", [128, 128])
    cmle_d = din("c_nble", [128, 896])
    cmlt_d = din("c_mlt", [128, 896])
    cntri_d = din("c_negtri", [128, 128])
    ciota_d = din("c_iota16", [128, 16])
    ciota256_d = din("c_iota256", [128, 256])
    out_d = nc.dram_tensor("out", [NSEQ, S, D], F32, kind="ExternalOutput").ap()
    uvb_d = [nc.dram_tensor(f"uvb{l}", [NE, 2 * D], BF16, kind="Internal").ap() for l in range(L)]

    with ExitStack() as st:
        c = Ctx(nc, st)

        def sb(name, shape, dt=F32):
            return st.enter_context(nc.sbuf_tensor("sb_" + name, list(shape), dt))

        b_lnp_shared = Buf("lnp")
        b_misc = Buf("misc")
        b_outd = Buf("outd")
        xs = sb("xs", [128, NT, D]); b_xs = bufs(NT, "xs")
        xT = sb("xT", [128, KC, S], BF16); b_xT = bufs(NT, "xT")
        ps = [st.enter_context(nc.psum_tensor(f"ps{i}", [128, 512], F32)) for i in range(8)]
        b_ps = bufs(8, "ps")
        ident = sb("ident", [128, 128]); b_ident = Buf("ident")
        nbl = sb("nbl", [128, 896], BF16); b_mle = Buf("nbl")
        mlt = sb("mlt", [128, 896], BF16); b_mlt = Buf("mlt")
        negtri = sb("negtri", [128, 128], BF16); b_negtri = Buf("negtri")
        negid = sb("negid", [128, 128], BF16); b_negid = Buf("negid")
        negones = sb("negones", [128, 128], BF16); b_negones = Buf("negones")
        ones_bf = sb("ones_bf", [128, 128], BF16); b_ones_bf = Buf("ones_bf")
        ones_f = sb("ones_f", [128, 128]); b_ones_f = Buf("ones_f")
        iota16 = sb("iota16", [128, 16]); b_iota = Buf("iota")
        WST = 2
        wstg = [sb(f"wstg{i}", [128, KC, 128]) for i in range(WST)]; b_wstg = bufs(WST, "wstg")
        w_a = sb("w_a", [128, KC, 256], BF16); b_w_a = Buf("w_a")
        w_b = sb("w_b", [128, KC, 256], BF16); b_w_b = Buf("w_b")
        rgp = sb("rgp", [128, 2, 8]); b_rgp = Buf("rgp")
        rgc = sb("rgc", [128, 2, 4]); b_rgc = Buf("rgc")
        wabd = sb("wabd", [128, 2, 128]); wibd = sb("wibd", [128, 2, 128]); b_wbd = Buf("wbd")
        scp = sb("scp", [128, 2, 4]); b_scp = Buf("scp")
        mng = sb("mng", [128, 8]); b_mng = Buf("mng")
        foxbf = sb("foxbf", [4, 1]); b_foxbf = Buf("foxbf")
        lnst = sb("lnst", [128, 2, 6]); b_lnst = Buf("lnst")
        lnmv = sb("lnmv", [128, 8]); b_lnmv = Buf("lnmv")
        cst = sb("cst", [128, 4]); b_cst = Buf("cst")
        scr = sb("scr", [128, ARENA_BYTES // 4])

        def V(off, shape, dt=F32):
            esz = 4 if dt in (F32, U32) else 2
            n = 1
            for d_ in shape[1:]:
                n *= d_
            nb = n * esz
            assert off % 4 == 0 and nb % 4 == 0 and off + nb <= ARENA_BYTES, (off, nb)
            a = scr[:, off // 4:(off + nb) // 4]
            if dt != F32:
                a = a.bitcast(dt)
            if len(shape) == 3:
                a = a.rearrange("p (a b) -> p a b", a=shape[1])
            elif len(shape) == 4:
                a = a.rearrange("p (a b c) -> p a b c", a=shape[1], b=shape[2])
            if shape[0] != 128:
                a = a[0:shape[0]]
            return a

        wst_i = [0]
        wst_n = [WST]

        def load_w(dram2d, col0, ncols, dst, b_dst, eng_cast="pool"):
            done = 0
            while done < ncols:
                n = min(128, ncols - done)
                i = wst_i[0] % wst_n[0]
                wst_i[0] += 1
                src = dram2d[:, col0 + done:col0 + done + n].rearrange("(kc p) c -> p kc c", p=128)
                c.dma("sp", lambda e, i=i, n=n, src=src: e.dma_start(out=wstg[i][:, :, 0:n], in_=src), writes=[b_wstg[i]])
                if eng_cast == "act":
                    c.op("act", lambda e, i=i, n=n, done=done: e.activation(out=dst[:, :, done:done + n], in_=wstg[i][:, :, 0:n], func=AF.Copy),
                         reads=[b_wstg[i]], writes=[b_dst])
                else:
                    c.op(eng_cast, lambda e, i=i, n=n, done=done: e.tensor_copy(out=dst[:, :, done:done + n], in_=wstg[i][:, :, 0:n]),
                         reads=[b_wstg[i]], writes=[b_dst])
                done += n

        def load_rows(dram_rows, dst, b_dst, eng_cast="pool"):
            for hf in range(2):
                i = wst_i[0] % wst_n[0]
                wst_i[0] += 1
                stg = wstg[i][:].rearrange("p a b -> p (a b)").rearrange("p (a b) -> p a b", a=2)
                src = dram_rows[:, hf * 512:(hf + 1) * 512].rearrange("(c p) d -> p c d", p=128)
                c.dma("sp", lambda e, stg=stg, src=src: e.dma_start(out=stg, in_=src), writes=[b_wstg[i]])
                if eng_cast == "act":
                    c.op("act", lambda e, stg=stg, hf=hf: e.activation(out=dst[:, :, hf * 512:(hf + 1) * 512], in_=stg, func=AF.Copy),
                         reads=[b_wstg[i]], writes=[b_dst])
                else:
                    c.op(eng_cast, lambda e, stg=stg, hf=hf: e.tensor_copy(out=dst[:, :, hf * 512:(hf + 1) * 512], in_=stg),
                         reads=[b_wstg[i]], writes=[b_dst])

        cstage = V(0, [128, 896]); b_cstage = Buf("cstage")
        c.dma("sp", lambda e: e.dma_start(out=ident[:], in_=cid_d[:, :]), writes=[b_ident], sembuf=b_misc)
        c.dma("sp", lambda e: e.dma_start(out=iota16[:], in_=ciota_d[:, :]), writes=[b_iota], sembuf=b_misc)
        c.dma("sp", lambda e: e.dma_start(out=cstage, in_=cmle_d[:, :]), writes=[b_cstage], sembuf=b_misc)
        c.op("dve", lambda e: e.tensor_copy(out=nbl[:], in_=cstage), reads=[b_cstage], writes=[b_mle])
        c.dma("sp", lambda e: e.dma_start(out=cstage, in_=cmlt_d[:, :]), writes=[b_cstage], sembuf=b_misc)
        c.op("dve", lambda e: e.tensor_copy(out=mlt[:], in_=cstage), reads=[b_cstage], writes=[b_mlt])
        c.dma("sp", lambda e: e.dma_start(out=cstage[:, 0:128], in_=cntri_d[:, :]), writes=[b_cstage], sembuf=b_misc)
        c.op("dve", lambda e: e.tensor_copy(out=negtri[:], in_=cstage[:, 0:128]), reads=[b_cstage], writes=[b_negtri])
        c.op("dve", lambda e: e.tensor_scalar(out=negid[:], in0=ident[:], scalar1=-1.0, scalar2=None, op0=ALU.mult),
             reads=[b_ident], writes=[b_negid])
        c.op("dve", lambda e: e.memset(negones[:], -1.0), writes=[b_negones])
        c.op("dve", lambda e: e.memset(ones_bf[:], 1.0), writes=[b_ones_bf])
        c.op("dve", lambda e: e.memset(ones_f[:], 1.0), writes=[b_ones_f])
        c.op("dve", lambda e: e.memset(cst[:, 0:1], LN_EPS), writes=[b_cst])
        c.op("dve", lambda e: e.memset(cst[:, 1:2], 1e-6), writes=[b_cst])
        c.op("dve", lambda e: e.memset(cst[:, 2:3], 1.0), writes=[b_cst])
        EPS, EPS6, ONE = cst[:, 0:1], cst[:, 1:2], cst[:, 2:3]

        rr = [0]

        def evac_eng():
            rr[0] += 1
            return "act" if rr[0] % 2 == 0 else "dve"

        def copy_op(eng, out_ap, in_ap, reads, writes, scale=None):
            if eng == "act":
                if scale is None:
                    c.op("act", lambda e: e.activation(out=out_ap, in_=in_ap, func=AF.Copy), reads, writes)
                else:
                    c.op("act", lambda e: e.activation(out=out_ap, in_=in_ap, func=AF.Copy, scale=scale), reads, writes)
            else:
                if scale is None:
                    c.op(eng, lambda e: e.tensor_copy(out=out_ap, in_=in_ap), reads, writes)
                else:
                    c.op(eng, lambda e: e.tensor_scalar(out=out_ap, in0=in_ap, scalar1=scale, scalar2=None, op0=ALU.mult),
                         reads, writes)

        def mm(out_ap, lhsT, rhs, start, stop, reads, writes):
            c.op("pe", lambda e: e.matmul(out_ap, lhsT, rhs, start=start, stop=stop), reads, writes)

        def make_xT():
            for tt in range(NT):
                for g in range(2):
                    bk = (tt * 2 + g) % 8
                    for j in range(4):
                        kc = 4 * g + j
                        c.op("pe", lambda e, tt=tt, kc=kc, j=j, bk=bk: e.transpose(
                            ps[bk][:, j * 128:(j + 1) * 128], xs[:, tt, kc * 128:(kc + 1) * 128], ident[:]),
                            reads=[b_xs[tt], b_ident], writes=[b_ps[bk]])
                    copy_op(evac_eng(), xT[:, 4 * g:4 * g + 4, tt * 128:(tt + 1) * 128],
                            ps[bk][:].rearrange("p (j q) -> p j q", j=4), [b_ps[bk]], [b_xT[tt]])

        def projT(w_ap, b_w, M, q, bk, c0=0):
            for kc in range(KC):
                mm(ps[bk][0:M, :], w_ap[:, kc, c0:c0 + M], xT[:, kc, q * 512:(q + 1) * 512], kc == 0, kc == KC - 1,
                   [b_w] + b_xT[4 * q:4 * q + 4], [b_ps[bk]])

        lng = V(LN_OFF, [128, D]); lnb = V(LN_OFF + 4096, [128, D])

        def load_ln(l, which):
            b = b_lnp_shared
            c.dma("sp", [lambda e: e.dma_start(out=lng, in_=lng_d[l, which].partition_broadcast(128)),
                         lambda e: e.dma_start(out=lnb, in_=lnb_d[l, which].partition_broadcast(128))], writes=[b])
            return b

        def layer_norm(tts, b_lnp, aff_eng="dve", add_eng="dve"):
            for tt in tts:
                c.op("dve", lambda e, tt=tt: e.bn_stats(out=lnst[:, 0, :], in_=xs[:, tt, 0:512]),
                     reads=[b_xs[tt]], writes=[b_lnst])
                c.op("dve", lambda e, tt=tt: e.bn_stats(out=lnst[:, 1, :], in_=xs[:, tt, 512:1024]),
                     reads=[b_xs[tt]], writes=[b_lnst])
                c.op("dve", lambda e: e.bn_aggr(out=lnmv[:, 0:2], in_=lnst[:].rearrange("p a b -> p (a b)")),
                     reads=[b_lnst], writes=[b_lnmv])
                c.op("act", lambda e: e.activation(out=lnmv[:, 2:3], in_=lnmv[:, 1:2], func=AF.Sqrt, bias=EPS),
                     reads=[b_cst], writes=[b_lnmv])
                c.op("dve", lambda e: e.reciprocal(out=lnmv[:, 3:4], in_=lnmv[:, 2:3]), reads=[], writes=[b_lnmv])
                c.op("dve", lambda e: e.scalar_tensor_tensor(out=lnmv[:, 4:5], in0=lnmv[:, 0:1], scalar=-1.0,
                                                              in1=lnmv[:, 3:4], op0=ALU.mult, op1=ALU.mult),
                     reads=[], writes=[b_lnmv])
                c.op("act", lambda e, tt=tt: e.activation(out=xs[:, tt, :], in_=xs[:, tt, :], func=AF.Identity,
                                                          scale=lnmv[:, 3:4], bias=lnmv[:, 4:5]),
                     reads=[b_lnmv], writes=[b_xs[tt]])
                c.op(aff_eng, lambda e, tt=tt: e.tensor_tensor(out=xs[:, tt, :], in0=xs[:, tt, :], in1=lng, op=ALU.mult),
                     reads=[b_lnp], writes=[b_xs[tt]])
                c.op(add_eng, lambda e, tt=tt: e.tensor_tensor(out=xs[:, tt, :], in0=xs[:, tt, :], in1=lnb, op=ALU.add),
                     reads=[b_lnp], writes=[b_xs[tt]])

        def mixer(l):
            c.barrier()
            w_in_l = w_in_d[l]
            K1 = 1024
            Yg = V(0, [128, 2, S]); b_Yg = bufs(2, "Yg")
            yTn = V(16 * K1, [128, 2, S], BF16); b_yTn = Buf("yTn")
            woutg = V(24 * K1, [128, 2, D], BF16); b_woutg = Buf("woutg")
            rden = V(28 * K1, [128, 512]); b_rden = Buf("rden")
            TB = 30 * K1
            TW = 8256

            def out_proj_group(g, first):
                load_rows(w_out_d[l][g * 256:(g + 1) * 256, :], woutg, b_woutg)
                for tt in range(NT):
                    for h in range(2):
                        bk = (tt * 2 + h) % 8
                        for cc in range(2):
                            mm(ps[bk][:, :], yTn[:, cc, tt * 128:(tt + 1) * 128], woutg[:, cc, h * 512:(h + 1) * 512],
                               cc == 0, cc == 1, [b_yTn, b_woutg], [b_ps[bk]])
                        if first:
                            c.op("dve", lambda e, tt=tt, h=h, bk=bk: e.scalar_tensor_tensor(
                                out=xs[:, tt, h * 512:(h + 1) * 512], in0=xs[:, tt, h * 512:(h + 1) * 512], scalar=ALPHA,
                                in1=ps[bk][:, :], op0=ALU.mult, op1=ALU.add), reads=[b_ps[bk]], writes=[b_xs[tt]])
                        else:
                            c.op("dve", lambda e, tt=tt, h=h, bk=bk: e.tensor_tensor(
                                out=xs[:, tt, h * 512:(h + 1) * 512], in0=xs[:, tt, h * 512:(h + 1) * 512],
                                in1=ps[bk][:, :], op=ALU.add), reads=[b_ps[bk]], writes=[b_xs[tt]])

            def group_rms_and_proj(g, first, sqA, b_sqA, sqB, b_sqB):
                for q in range(NQ):
                    sl = slice(q * 512, (q + 1) * 512)
                    bk = q % 4
                    c.op("act", lambda e, sl=sl: e.activation(out=sqA[:, 0:512], in_=Yg[:, 0, sl], func=AF.Square),
                         reads=[b_Yg[0]], writes=[b_sqA])
                    c.op("act", lambda e, sl=sl: e.activation(out=sqB[:, 0:512], in_=Yg[:, 1, sl], func=AF.Square),
                         reads=[b_Yg[1]], writes=[b_sqB])
                    mm(ps[bk][:, :], ones_f[:], sqA[:, 0:512], True, False, [b_ones_f, b_sqA], [b_ps[bk]])
                    mm(ps[bk][:, :], ones_f[:], sqB[:, 0:512], False, True, [b_ones_f, b_sqB], [b_ps[bk]])
                    c.op("act", lambda e, bk=bk: e.activation(out=rden, in_=ps[bk][:, :], func=AF.Sqrt,
                                                              scale=1.0 / 256.0, bias=EPS6),
                         reads=[b_ps[bk], b_cst], writes=[b_rden])
                    c.op("dve", lambda e: e.reciprocal(out=rden, in_=rden), reads=[], writes=[b_rden])
                    for cc in range(2):
                        c.op("dve", lambda e, cc=cc, sl=sl: e.scalar_tensor_tensor(
                            out=yTn[:, cc, sl], in0=Yg[:, cc, sl], scalar=mng[:, 2 * g + cc:2 * g + cc + 1], in1=rden,
                            op0=ALU.mult, op1=ALU.mult), reads=[b_Yg[cc], b_mng, b_rden], writes=[b_yTn])
                out_proj_group(g, first)

            c.dma("sp", lambda e: e.dma_start(out=rgp[:], in_=rgp_d[l].rearrange("j p k -> p j k")), writes=[b_rgp], sembuf=b_misc)
            c.dma("sp", [lambda e: e.dma_start(out=wabd[:], in_=wa_d[l].rearrange("j p k -> p j k")),
                         lambda e: e.dma_start(out=wibd[:], in_=wi_d[l].rearrange("j p k -> p j k"))], writes=[b_wbd], sembuf=b_misc)
            c.dma("sp", lambda e: e.dma_start(out=scp[:], in_=scp_d[l].rearrange("j p k -> p j k")), writes=[b_scp], sembuf=b_misc)
            c.dma("sp", lambda e: e.dma_start(out=mng[:], in_=mng_d[l]), writes=[b_mng], sembuf=b_misc)
            c.dma("sp", lambda e: e.dma_start(out=foxbf[:], in_=foxbf_d[l]), writes=[b_foxbf], sembuf=b_misc)
            c.op("act", lambda e: e.activation(out=rgc[:, :, 2], in_=rgp[:, :, 7], func=AF.Exp, scale=-1.0),
                 reads=[b_rgp], writes=[b_rgc])
            c.op("act", lambda e: e.activation(out=rgc[:, :, 3], in_=rgc[:, :, 2], func=AF.Ln, bias=ONE),
                 reads=[b_cst], writes=[b_rgc])
            c.op("dve", lambda e: e.tensor_scalar(out=rgc[:, :, 0], in0=rgc[:, :, 3], scalar1=-8.0, scalar2=None, op0=ALU.mult),
                 reads=[], writes=[b_rgc])
            c.op("dve", lambda e: e.tensor_scalar(out=rgc[:, :, 1], in0=rgc[:, :, 3], scalar1=-16.0, scalar2=None, op0=ALU.mult),
                 reads=[], writes=[b_rgc])

            T1 = V(TB, [128, S + 4]); b_T1 = Buf("T1")
            T2 = V(TB + TW, [128, S + 4]); b_T2 = Buf("T2")
            T3 = V(TB + 2 * TW, [128, S + 4]); b_T3 = Buf("T3")
            T4 = V(TB + 3 * TW, [128, S + 4]); b_T4 = Buf("T4")

            conv_tick()
            for j in range(2):
                T5 = Yg[:, j, :]; b_T5 = b_Yg[j]
                load_w(w_in_l, OFF_RG_X + j * 128, 128, w_a, b_w_a)
                load_w(w_in_l, OFF_RG_G + j * 128, 128, w_b, b_w_b)
                c.op("pool", lambda e: e.memset(T1[:, 0:4], 0.0), writes=[b_T1])
                for q in range(NQ):
                    projT(w_a, b_w_a, 128, q, q)
                    copy_op("act", T1[:, 4 + q * 512:4 + (q + 1) * 512], ps[q][:, :], [b_ps[q]], [b_T1])
                c.op("dve", lambda e, j=j: e.tensor_scalar(out=T2[:, 0:S], in0=T1[:, 4:4 + S], scalar1=rgp[:, j, 3:4],
                                                           scalar2=rgp[:, j, 4:5], op0=ALU.mult, op1=ALU.add),
                     reads=[b_T1, b_rgp], writes=[b_T2])
                for k in range(3):
                    c.op("dve", lambda e, j=j, k=k: e.scalar_tensor_tensor(
                        out=T2[:, 0:S], in0=T1[:, 1 + k:1 + k + S], scalar=rgp[:, j, k:k + 1], in1=T2[:, 0:S],
                        op0=ALU.mult, op1=ALU.add), reads=[b_T1, b_rgp], writes=[b_T2])
                for q in range(NQ):
                    sl = slice(q * 512, (q + 1) * 512)
                    mm(ps[4 + q % 4][:, :], wabd[:, j, :], T2[:, sl], True, True, [b_wbd, b_T2], [b_ps[4 + q % 4]])
                    c.op("act", lambda e, j=j, q=q, sl=sl: e.activation(out=T3[:, sl], in_=ps[4 + q % 4][:, :], func=AF.Sigmoid,
                                                                       bias=rgp[:, j, 5:6]),
                         reads=[b_ps[4 + q % 4], b_rgp], writes=[b_T3])
                for q in range(NQ):
                    sl = slice(q * 512, (q + 1) * 512)
                    mm(ps[q % 4][:, :], wibd[:, j, :], T2[:, sl], True, True, [b_wbd, b_T2], [b_ps[q % 4]])
                    c.op("act", lambda e, j=j, q=q, sl=sl: e.activation(out=T4[:, sl], in_=ps[q % 4][:, :], func=AF.Sigmoid,
                                                                       bias=rgp[:, j, 6:7]),
                         reads=[b_ps[q % 4], b_rgp], writes=[b_T4])
                c.op("act", lambda e, j=j, T5=T5: e.activation(out=T5, in_=T3[:, 0:S], func=AF.Exp, scale=rgc[:, j, 0:1]),
                     reads=[b_T3, b_rgc], writes=[b_T5])
                c.op("act", lambda e, j=j: e.activation(out=T3[:, 0:S], in_=T3[:, 0:S], func=AF.Exp, scale=rgc[:, j, 1:2]),
                     reads=[b_rgc], writes=[b_T3])
                c.op("act", lambda e: e.activation(out=T3[:, 0:S], in_=T3[:, 0:S], func=AF.Sqrt, scale=-1.0, bias=ONE),
                     reads=[b_cst], writes=[b_T3])
                c.op("dve", lambda e: e.tensor_tensor(out=T4[:, 0:S], in0=T4[:, 0:S], in1=T3[:, 0:S], op=ALU.mult),
                     reads=[b_T3], writes=[b_T4])
                c.op("dve", lambda e: e.tensor_tensor(out=T4[:, 0:S], in0=T4[:, 0:S], in1=T2[:, 0:S], op=ALU.mult),
                     reads=[b_T2], writes=[b_T4])
                c.op("dve", lambda e, T5=T5: e.tensor_tensor_scan(out=T3[:, 0:S], data0=T5, data1=T4[:, 0:S], initial=0.0,
                                                                  op0=ALU.mult, op1=ALU.add),
                     reads=[b_T5, b_T4], writes=[b_T3])
                for q in range(NQ):
                    sl = slice(q * 512, (q + 1) * 512)
                    projT(w_b, b_w_b, 128, q, 4 + q % 4)
                    c.op("act", lambda e, q=q, sl=sl: e.activation(out=T2[:, sl], in_=ps[4 + q % 4][:, :], func=AF.Gelu_apprx_tanh),
                         reads=[b_ps[4 + q % 4]], writes=[b_T2])
                c.op("dve", lambda e, j=j: e.tensor_tensor(out=Yg[:, j, :], in0=T3[:, 0:S], in1=T2[:, 0:S], op=ALU.mult),
                     reads=[b_T3, b_T2], writes=[b_Yg[j]])
            group_rms_and_proj(0, True, T1, b_T1, T4, b_T4)

            qTh = V(30 * K1, [68, 4, S], BF16)
            kTh = V(46 * K1, [68, 4, S], BF16)
            Vt = V(62 * K1, [128, NT, 256], BF16)
            PB0 = 70 * K1
            NPB = 3
            Pb = [V(PB0 + i * K1, [128, 512], BF16) for i in range(NPB)]
            Eb = [V(PB0 + 3 * K1 + i * 2 * K1, [128, 512]) for i in range(2)]
            SPb = [V(PB0 + 7 * K1 + i * K1, [128, 512], BF16) for i in range(3)]
            Lpb = [V(PB0 + 10 * K1 + i * K1, [128, 512], BF16) for i in range(3)]
            Lcb = [V(PB0 + 13 * K1 + i * K1, [128, 512], BF16) for i in range(3)]
            for grp, off in ((1, OFF_FOX), (2, OFF_SB)):
                is_fox = grp == 1
                c.barrier()
                conv_tick()
                b_qTh = bufs(4, "qTh"); b_kTh = bufs(4, "kTh"); b_Vt = bufs(NT, "Vt")
                b_Pb = bufs(NPB, "Pb"); b_Eb = bufs(2, "Eb"); b_SPb = bufs(3, "SPb"); b_Lpb = bufs(3, "Lpb"); b_Lcb = bufs(3, "Lcb")
                b_Yg = bufs(2, "Yg"); b_yTn = Buf("yTn"); b_rden = Buf("rden"); b_woutg = Buf("woutg")
                if is_fox:
                    fl = V(30 * K1, [4, S]); fl2 = V(38 * K1, [4, S])
                    chi = V(46 * K1, [4, S], BF16); clo = V(50 * K1, [4, S], BF16)
                    nchi = V(54 * K1, [4, S], BF16); nclo = V(58 * K1, [4, S], BF16)
                    b_fl, b_fl2, b_chi, b_clo, b_nchi, b_nclo = (Buf("fl"), Buf("fl2"), Buf("chi"), Buf("clo"), Buf("nchi"), Buf("nclo"))
                    fones = V(62 * K1, [4, S]); b_fones = Buf("fones")
                    c.op("dve", lambda e: e.memset(fones, 1.0), writes=[b_fones])
                    load_w(w_in_l, OFF_FOX_F, 4, w_a, b_w_a)
                    for q in range(NQ):
                        sl = slice(q * 512, (q + 1) * 512)
                        projT(w_a, b_w_a, 4, q, q % 4)
                        c.op("dve", lambda e, q=q, sl=sl: e.tensor_scalar(out=fl[:, sl], in0=ps[q % 4][0:4, :], scalar1=foxbf[:, 0:1],
                                                                            scalar2=None, op0=ALU.add),
                             reads=[b_ps[q % 4], b_foxbf], writes=[b_fl])
                    c.op("act", lambda e: e.activation(out=fl, in_=fl, func=AF.Exp, scale=-1.0), reads=[], writes=[b_fl])
                    c.op("act", lambda e: e.activation(out=fl, in_=fl, func=AF.Ln, bias=cst[0:4, 2:3]),
                         reads=[b_cst], writes=[b_fl])
                    c.op("dve", lambda e: e.tensor_tensor_scan(out=fl2, data0=fones, data1=fl, initial=0.0,
                                                               op0=ALU.mult, op1=ALU.add),
                         reads=[b_fl, b_fones], writes=[b_fl2])
                    c.op("dve", lambda e: e.tensor_copy(out=nchi, in_=fl2), reads=[b_fl2], writes=[b_nchi])
                    c.op("dve", lambda e: e.tensor_tensor(out=nclo, in0=fl2, in1=nchi, op=ALU.subtract),
                         reads=[b_fl2, b_nchi], writes=[b_nclo])
                    c.op("dve", lambda e: e.tensor_scalar(out=chi, in0=nchi, scalar1=-1.0, scalar2=None, op0=ALU.mult),
                         reads=[b_nchi], writes=[b_chi])
                    c.op("dve", lambda e: e.tensor_scalar(out=clo, in0=nclo, scalar1=-1.0, scalar2=None, op0=ALU.mult),
                         reads=[b_nclo], writes=[b_clo])
                    for h in range(4):
                        c.op("dve", lambda e, h=h: e.memset(qTh[64:68, h, :], 1.0), writes=[b_qTh[h]])
                        c.op("dve", lambda e, h=h: e.memset(kTh[64:68, h, :], 1.0), writes=[b_kTh[h]])
                        c.dma("sp", [lambda e, h=h: e.dma_start(out=qTh[64:65, h, :], in_=chi[h:h + 1, :]),
                                     lambda e, h=h: e.dma_start(out=qTh[65:66, h, :], in_=clo[h:h + 1, :])],
                              reads=[b_chi, b_clo], writes=[b_qTh[h]], sembuf=b_misc)
                        c.dma("sp", [lambda e, h=h: e.dma_start(out=kTh[66:67, h, :], in_=nchi[h:h + 1, :]),
                                     lambda e, h=h: e.dma_start(out=kTh[67:68, h, :], in_=nclo[h:h + 1, :])],
                              reads=[b_nchi, b_nclo], writes=[b_kTh[h]], sembuf=b_misc)
                    c.barrier()
                KR = 68 if is_fox else 64
                for which, dst, b_dst, coff, scl in ((0, qTh, b_qTh, off, 0.125), (1, kTh, b_kTh, off + GW, None)):
                    for hp in range(2):
                        load_w(w_in_l, coff + hp * 128, 128, w_a, b_w_a)
                        for hh in range(2):
                            h = hp * 2 + hh
                            for q in range(NQ):
                                bk = (h * NQ + q) % 8
                                projT(w_a, b_w_a, 64, q, bk, c0=hh * 64)
                                copy_op(evac_eng(), dst[0:64, h, q * 512:(q + 1) * 512], ps[bk][0:64, :], [b_ps[bk]], [b_dst[h]],
                                        scale=scl)
                load_w(w_in_l, off + 2 * GW, 256, w_b, b_w_b)
                for tt in range(NT):
                    bk = tt % 8
                    for kc in range(KC):
                        mm(ps[bk][:, 0:256], xT[:, kc, tt * 128:(tt + 1) * 128], w_b[:, kc, 0:256], kc == 0, kc == KC - 1,
                           [b_w_b, b_xT[tt]], [b_ps[bk]])
                    copy_op(evac_eng(), Vt[:, tt, :], ps[bk][:, 0:256], [b_ps[bk]], [b_Vt[tt]])
                pairs = []
                for h in range(4):
                    for q in range(NQ):
                        nA = 4 * q + 4
                        order = list(range(nA)) if is_fox else list(range(nA - 1, -1, -1))
                        for idx, A in enumerate(order):
                            pairs.append((h, q, idx, A, nA))

                def geom(m):
                    h, q, idx, A, nA = pairs[m]
                    diag = A >= 4 * q
                    moff = 384 - 128 * (A - 4 * q)
                    par = (h * NQ + q) % 2
                    return dict(h=h, q=q, idx=idx, A=A, nA=nA, first=idx == 0, last=idx == nA - 1, diag=diag, moff=moff,
                                ks=slice(A * 128, (A + 1) * 128), qs=slice(q * 512, (q + 1) * 512),
                                sbk=m % 3, pb=m % 3, eb=m % 2, s3=m % 3, wbk=3 + (m % 2), par=par, lc=idx % 3)

                def st_score(m):
                    g = geom(m)
                    h, sbk = g["h"], g["sbk"]
                    mm(ps[sbk][:, :], kTh[0:KR, h, g["ks"]], qTh[0:KR, h, g["qs"]], True, True, [b_kTh[h], b_qTh[h]], [b_ps[sbk]])
                    if is_fox and g["diag"]:
                        mo = g["moff"]
                        c.op("dve", lambda e: e.tensor_tensor(out=ps[sbk][:, :], in0=ps[sbk][:, :], in1=nbl[:, mo:mo + 512], op=ALU.add),
                             reads=[b_mle], writes=[b_ps[sbk]])

                def st_fox_exp(m):
                    g = geom(m)
                    sbk, pb = g["sbk"], g["pb"]
                    c.op("act", lambda e: e.activation(out=Pb[pb], in_=ps[sbk][:, :], func=AF.Exp), reads=[b_ps[sbk]], writes=[b_Pb[pb]])

                def st_fox_pv(m):
                    g = geom(m)
                    h, q, A, pb, par = g["h"], g["q"], g["A"], g["pb"], g["par"]
                    cc, hp = h // 2, h % 2
                    vsl = slice(cc * 128, (cc + 1) * 128)
                    prt = slice(hp * 64, (hp + 1) * 64)
                    ob, db = 4 + par, 6 + par
                    mm(ps[ob][:, :], Vt[:, A, vsl], Pb[pb], g["first"], g["last"], [b_Vt[A], b_Pb[pb]], [b_ps[ob]])
                    mm(ps[db][:, :], ones_bf[:], Pb[pb], g["first"], g["last"], [b_ones_bf, b_Pb[pb]], [b_ps[db]])
                    if g["last"]:
                        qs = g["qs"]
                        c.op("dve", lambda e: e.reciprocal(out=rden, in_=ps[db][:, :]), reads=[b_ps[db]], writes=[b_rden])
                        c.op("dve", lambda e: e.tensor_tensor(out=Yg[prt, cc, qs], in0=ps[ob][prt, :], in1=rden[prt, :], op=ALU.mult),
                             reads=[b_ps[ob], b_rden], writes=[b_Yg[cc]])

                def st_sb_elem(m):
                    g = geom(m)
                    sbk, eb, s3, lc = g["sbk"], g["eb"], g["s3"], g["lc"]
                    c.op("act", lambda e: e.activation(out=Eb[eb], in_=ps[sbk][:, :], func=AF.Exp, scale=-1.0), reads=[b_ps[sbk]], writes=[b_Eb[eb]])
                    c.op("act", lambda e: e.activation(out=SPb[s3], in_=Eb[eb], func=AF.Ln, bias=ONE), reads=[b_Eb[eb], b_cst], writes=[b_SPb[s3]])
                    c.op("dve", lambda e: e.tensor_tensor(out=Lpb[s3], in0=ps[sbk][:, :], in1=SPb[s3], op=ALU.add),
                         reads=[b_ps[sbk], b_SPb[s3]], writes=[b_Lpb[s3]])
                    if g["diag"]:
                        mo = g["moff"]
                        c.op("pool", lambda e: e.tensor_tensor(out=Lpb[s3], in0=Lpb[s3], in1=mlt[:, mo:mo + 512], op=ALU.mult),
                             reads=[b_mlt], writes=[b_Lpb[s3]])
                    if not g["last"]:
                        nx = (lc + 1) % 3
                        if g["first"]:
                            c.op("dve", lambda e: e.tensor_copy(out=Lcb[nx], in_=Lpb[s3]), reads=[b_Lpb[s3]], writes=[b_Lcb[nx]])
                        else:
                            c.op("dve", lambda e: e.tensor_tensor(out=Lcb[nx], in0=Lcb[lc], in1=Lpb[s3], op=ALU.add),
                                 reads=[b_Lcb[lc], b_Lpb[s3]], writes=[b_Lcb[nx]])

                def st_sb_w(m):
                    g = geom(m)
                    s3, wbk, lc, first = g["s3"], g["wbk"], g["lc"], g["first"]
                    mm(ps[wbk][:, :], negid[:], SPb[s3], True, False, [b_negid, b_SPb[s3]], [b_ps[wbk]])
                    mm(ps[wbk][:, :], negtri[:], Lpb[s3], False, first, [b_negtri, b_Lpb[s3]], [b_ps[wbk]])
                    if not first:
                        mm(ps[wbk][:, :], negones[:], Lcb[lc], False, True, [b_negones, b_Lcb[lc]], [b_ps[wbk]])

                def st_sb_expw(m):
                    g = geom(m)
                    wbk, pb = g["wbk"], g["pb"]
                    c.op("act", lambda e: e.activation(out=Pb[pb], in_=ps[wbk][:, :], func=AF.Exp), reads=[b_ps[wbk]], writes=[b_Pb[pb]])
                    if g["diag"]:
                        mo = g["moff"]
                        c.op("pool", lambda e: e.tensor_tensor(out=Pb[pb], in0=Pb[pb], in1=mlt[:, mo:mo + 512], op=ALU.mult),
                             reads=[b_mlt], writes=[b_Pb[pb]])

                def st_sb_pv(m):
                    g = geom(m)
                    h, q, A, pb, par = g["h"], g["q"], g["A"], g["pb"], g["par"]
                    cc, hp = h // 2, h % 2
                    vsl = slice(cc * 128, (cc + 1) * 128)
                    prt = slice(hp * 64, (hp + 1) * 64)
                    ob = 5 + par
                    mm(ps[ob][:, :], Vt[:, A, vsl], Pb[pb], g["first"], g["last"], [b_Vt[A], b_Pb[pb]], [b_ps[ob]])
                    if g["last"]:
                        qs = g["qs"]
                        c.op("act", lambda e: e.activation(out=Yg[prt, cc, qs], in_=ps[ob][prt, :], func=AF.Copy), reads=[b_ps[ob]], writes=[b_Yg[cc]])

                stages = [st_score, st_fox_exp, st_fox_pv] if is_fox else [st_score, st_sb_elem, st_sb_w, st_sb_expw, st_sb_pv]
                NP_ = len(pairs)
                for n in range(NP_ + len(stages) - 1):
                    for k in range(len(stages) - 1, -1, -1):
                        m = n - k
                        if 0 <= m < NP_:
                            stages[k](m)
                group_rms_and_proj(grp, False, Eb[0], b_Eb[0], Eb[1], b_Eb[1])

            c.barrier()
            conv_tick()
            b_T1, b_T2, b_T3, b_T4 = Buf("T1"), Buf("T2"), Buf("T3"), Buf("T4")
            b_Yg = bufs(2, "Yg"); b_yTn = Buf("yTn"); b_rden = Buf("rden"); b_woutg = Buf("woutg")
            for j in range(2):
                load_w(w_in_l, OFF_SC + GW + j * 128, 128, w_a, b_w_a)
                load_w(w_in_l, OFF_SC + 2 * GW + j * 128, 128, w_b, b_w_b)
                for q in range(NQ):
                    sl = slice(q * 512, (q + 1) * 512)
                    projT(w_a, b_w_a, 128, q, q % 4)
                    copy_op("act", T3[:, sl], ps[q % 4][:, :], [b_ps[q % 4]], [b_T3])
                c.op("pool", lambda e: e.memset(T1[:, 0:4], 0.0), writes=[b_T1])
                for q in range(NQ):
                    projT(w_b, b_w_b, 128, q, 4 + q % 4)
                    c.op("dve", lambda e, q=q: e.tensor_tensor(out=T1[:, 4 + q * 512:4 + (q + 1) * 512], in0=ps[4 + q % 4][:, :],
                                                               in1=T3[:, q * 512:(q + 1) * 512], op=ALU.mult),
                         reads=[b_ps[4 + q % 4], b_T3], writes=[b_T1])
                c.op("dve", lambda e, j=j: e.tensor_scalar(out=T2[:, 0:S], in0=T1[:, 4:4 + S], scalar1=scp[:, j, 2:3], scalar2=None,
                                                           op0=ALU.mult), reads=[b_T1, b_scp], writes=[b_T2])
                for k in range(2):
                    c.op("dve", lambda e, j=j, k=k: e.scalar_tensor_tensor(
                        out=T2[:, 0:S], in0=T1[:, 2 + k:2 + k + S], scalar=scp[:, j, k:k + 1], in1=T2[:, 0:S],
                        op0=ALU.mult, op1=ALU.add), reads=[b_T1, b_scp], writes=[b_T2])
                load_w(w_in_l, OFF_SC + j * 128, 128, w_a, b_w_a)
                for q in range(NQ):
                    projT(w_a, b_w_a, 128, q, q % 4)
                    c.op("dve", lambda e, q=q, j=j: e.tensor_tensor(out=Yg[:, j, q * 512:(q + 1) * 512], in0=ps[q % 4][:, :],
                                                                    in1=T2[:, q * 512:(q + 1) * 512], op=ALU.mult),
                         reads=[b_ps[q % 4], b_T2], writes=[b_Yg[j]])
            group_rms_and_proj(3, False, T3, b_T3, T4, b_T4)
            c.barrier()

        def xattn(l, s):
            c.barrier()
            K1 = 1024
            wqx = V(0, [128, KC, D], BF16); b_wqx = Buf("wqx")
            woh = V(16 * K1, [128, 2, D], BF16); b_woh = Buf("woh")
            KT = V(20 * K1, [128, 8, NMEM], BF16); b_KT = Buf("KT")
            Vx = V(24 * K1, [128, 2, D], BF16); b_Vx = Buf("Vx")
            memf = V(28 * K1, [128, KC, NMEM]); b_memf = Buf("memf")
            memb = V(36 * K1, [128, KC, NMEM], BF16); b_memb = Buf("memb")
            qTx = V(40 * K1, [128, 2, 512], BF16); b_qTx = Buf("qTx")
            pTx = [V(42 * K1 + i * K1, [128, 512], BF16) for i in range(2)]; b_pTx = bufs(2, "pTx")
            oTn = V(46 * K1, [128, 2, 512], BF16); b_oTn = Buf("oTn")
            rdx = V(48 * K1, [128, 512]); b_rdx = Buf("rdx")
            load_w(xwq_d[l], 0, D, wqx, b_wqx, eng_cast="act")
            c.dma("sp", lambda e: e.dma_start(out=memf, in_=memT_d[s].rearrange("(kc p) m -> p kc m", p=128)), writes=[b_memf], sembuf=b_misc)
            c.op("act", lambda e: e.activation(out=memb, in_=memf, func=AF.Copy), reads=[b_memf], writes=[b_memb])
            for cch in range(8):
                load_w(xwkv_d[l], cch * 128, 128, w_a, b_w_a, eng_cast="act")
                bk = cch % 8
                for kc in range(KC):
                    mm(ps[bk][:, 0:NMEM], w_a[:, kc, 0:128], memb[:, kc, :], kc == 0, kc == KC - 1, [b_w_a, b_memb], [b_ps[bk]])
                copy_op(evac_eng(), KT[:, cch, :], ps[bk][:, 0:NMEM], [b_ps[bk]], [b_KT])
            for cch in range(8):
                load_w(xwkv_d[l], D + cch * 128, 128, w_b, b_w_b, eng_cast="act")
                for mt in range(2):
                    bk = (cch * 2 + mt) % 8
                    for kc in range(KC):
                        mm(ps[bk][:, 0:128], memb[:, kc, mt * 128:(mt + 1) * 128], w_b[:, kc, 0:128], kc == 0, kc == KC - 1,
                           [b_w_b, b_memb], [b_ps[bk]])
                    copy_op(evac_eng(), Vx[:, mt, cch * 128:(cch + 1) * 128], ps[bk][:, 0:128], [b_ps[bk]], [b_Vx])
            woh2 = [woh, V(50 * K1, [128, 2, D], BF16)]; b_woh2 = bufs(2, "woh2")
            qTx2 = [qTx, V(54 * K1, [128, 2, 512], BF16)]; b_qTx2 = bufs(2, "qTx2")
            pTx2 = [[V(56 * K1 + (2 * i + j) * K1, [128, 512], BF16) for j in range(2)] for i in range(2)]
            b_pTx2 = [bufs(2, "pTxa"), bufs(2, "pTxb")]
            oTn2 = [oTn, V(60 * K1, [128, 2, 512], BF16)]; b_oTn2 = bufs(2, "oTn2")
            its = [(h, q) for h in range(4) for q in range(NQ)]

            def stA(i):
                h, q = its[i]
                if q == 0:
                    load_rows(xwo_d[l][h * 256:(h + 1) * 256, :], woh2[h % 2], b_woh2[h % 2], eng_cast="act")
                for c2 in range(2):
                    projT(wqx, b_wqx, 128, q, 0, c0=h * 256 + c2 * 128)
                    copy_op(evac_eng(), qTx2[i % 2][:, c2, :], ps[0][:, :], [b_ps[0]], [b_qTx2[i % 2]], scale=1.0 / 16.0)

            def stB(i):
                h, q = its[i]
                for mt in range(2):
                    for c2 in range(2):
                        mm(ps[1 + mt][:, :], KT[:, 2 * h + c2, mt * 128:(mt + 1) * 128], qTx2[i % 2][:, c2, :], c2 == 0, c2 == 1,
                           [b_KT, b_qTx2[i % 2]], [b_ps[1 + mt]])
                    c.op("act", lambda e, mt=mt: e.activation(out=pTx2[i % 2][mt], in_=ps[1 + mt][:, :], func=AF.Exp),
                         reads=[b_ps[1 + mt]], writes=[b_pTx2[i % 2][mt]])

            def stC(i):
                h, q = its[i]
                pt, bpt = pTx2[i % 2], b_pTx2[i % 2]
                for mt in range(2):
                    mm(ps[3][:, :], ones_bf[:], pt[mt], mt == 0, mt == 1, [b_ones_bf, bpt[mt]], [b_ps[3]])
                for c2 in range(2):
                    for mt in range(2):
                        mm(ps[4 + c2][:, :], Vx[:, mt, h * 256 + c2 * 128:h * 256 + (c2 + 1) * 128], pt[mt], mt == 0, mt == 1,
                           [b_Vx, bpt[mt]], [b_ps[4 + c2]])
                c.op("dve", lambda e: e.reciprocal(out=rdx, in_=ps[3][:, :]), reads=[b_ps[3]], writes=[b_rdx])
                for c2 in range(2):
                    c.op("dve", lambda e, c2=c2: e.tensor_tensor(out=oTn2[i % 2][:, c2, :], in0=ps[4 + c2][:, :], in1=rdx, op=ALU.mult),
                         reads=[b_ps[4 + c2], b_rdx], writes=[b_oTn2[i % 2]])

            def stD(i):
                h, q = its[i]
                for tsub in range(4):
                    tt = 4 * q + tsub
                    for hf in range(2):
                        bk = 6 + (tsub * 2 + hf) % 2
                        for c2 in range(2):
                            mm(ps[bk][:, :], oTn2[i % 2][:, c2, tsub * 128:(tsub + 1) * 128], woh2[h % 2][:, c2, hf * 512:(hf + 1) * 512],
                               c2 == 0, c2 == 1, [b_oTn2[i % 2], b_woh2[h % 2]], [b_ps[bk]])
                        if h == 0:
                            c.op("dve", lambda e, tt=tt, hf=hf, bk=bk: e.scalar_tensor_tensor(
                                out=xs[:, tt, hf * 512:(hf + 1) * 512], in0=xs[:, tt, hf * 512:(hf + 1) * 512], scalar=ALPHA,
                                in1=ps[bk][:, :], op0=ALU.mult, op1=ALU.add), reads=[b_ps[bk]], writes=[b_xs[tt]])
                        else:
                            c.op("dve", lambda e, tt=tt, hf=hf, bk=bk: e.tensor_tensor(
                                out=xs[:, tt, hf * 512:(hf + 1) * 512], in0=xs[:, tt, hf * 512:(hf + 1) * 512],
                                in1=ps[bk][:, :], op=ALU.add), reads=[b_ps[bk]], writes=[b_xs[tt]])

            xst = [stA, stB, stC, stD]
            NI = len(its)
            for n in range(NI + len(xst) - 1):
                for k in range(len(xst) - 1, -1, -1):
                    i = n - k
                    if 0 <= i < NI:
                        xst[k](i)
            c.barrier()

        def peer(l, s, is_last):
            c.barrier()
            K1 = 1024
            s_all = V(0, [128, 2, 16, 128]); b_sall = bufs(2, "sall")
            s_all_u = V(0, [128, 2, 16, 128], U32)
            idx_all = V(16 * K1, [128, 4, 128], U32); b_idx = bufs(4, "idx")
            gate_all = V(18 * K1, [128, 4, 128]); b_gate = bufs(4, "gate")
            NGA = 9
            NG = NGA + 2
            G = [V(20 * K1 + i * 4 * K1, [128, 2 * D], BF16) for i in range(NGA)]
            G.append(w_b[:].rearrange("p a b -> p (a b)"))
            G.append(wstg[1][:].rearrange("p a b -> p (a b)").bitcast(BF16))
            wst_n[0] = 1
            o = 20 * K1 + NGA * 4 * K1
            qTp = V(o, [128, 256], BF16); b_qTp = Buf("qTp"); o += 512
            o_qtp2 = o; o += 512
            kTb = V(o, [128, 2, 128], BF16); b_kTb = Buf("kTb"); o += 512
            kst = V(o, [128, 2, 128]); b_kst = Buf("kst"); o += 1024
            xb = V(o, [128, D], BF16); b_xb = Buf("xb"); o += 2048
            Vt_ = V(o, [128, 8, 2, 16]); Vt_u = V(o, [128, 8, 2, 16], U32); b_V = Buf("V"); o += 1024
            tmp128 = V(o, [128, 128]); b_tmp = Buf("tmp128"); o += 512
            I12u = V(o, [128, 8, 2, 16], U32); b_I12u = Buf("I12u"); o += 1024
            I12f = V(o, [128, 8, 2, 16], BF16); b_I12f = Buf("I12f"); o += 512
            iota16b = V(o, [128, 16], BF16); b_iota16b = Buf("iota16b"); o += 32
            e12b = V(o, [128, 8, 2, 16], BF16); o += 512
            cand2 = V(o, [128, 256]); b_cand2 = Buf("cand2"); o += 1024
            SC = V(o, [128, 8, 16]); SCu = V(o, [128, 8, 16], U32); b_SC = Buf("SC"); o += 512
            abu = V(o, [128, 8, 2, 16], U32); b_abu = Buf("abu"); o += 1024
            abf = V(o, [128, 8, 2, 16], BF16); b_abf = Buf("abf"); o += 512
            oh = V(o, [128, 8, 16, 16], BF16); b_oh = Buf("oh"); o += 4096
            e12 = V(o, [128, 8, 2, 16]); b_e12 = Buf("e12"); o += 1024
            exf = V(o, [128, 8, 16]); b_exf = Buf("exf"); o += 512
            exg = V(o, [128, 8, 16]); b_exg = Buf("exg"); o += 512
            sm = V(o, [128, 2, 8]); b_sm = Buf("sm"); o += 64
            actt = V(o, [128, 128]); b_act = bufs(64, "act"); o += 512
            coef = V(o, [128, 128]); b_coef = bufs(64, "coef"); o += 512
            NDG = 8
            dg = [V(o + i * 256, [128, 128], BF16) for i in range(NDG)]; b_dg = bufs(NDG, "dg"); o += NDG * 256
            identb = V(o, [128, 128], BF16); b_identb = Buf("identb"); o += 256
            io128u = V(o, [128, 128], U32); b_io128 = Buf("io128"); o += 512
            io256u = V(o, [128, 256], U32); b_io256 = Buf("io256"); o += 1024
            assert o <= LN_OFF, o
            c.op("dve", lambda e: e.tensor_copy(out=identb, in_=ident[:]), reads=[b_ident], writes=[b_identb])
            c.op("dve", lambda e: e.tensor_copy(out=iota16b, in_=iota16[:]), reads=[b_iota], writes=[b_iota16b])
            c.dma("sp", lambda e: e.dma_start(out=cand2, in_=ciota256_d[:, :]), writes=[b_cand2], sembuf=b_misc)
            c.op("dve", lambda e: e.tensor_copy(out=io256u, in_=cand2), reads=[b_cand2], writes=[b_io256])
            c.op("dve", lambda e: e.tensor_copy(out=io128u, in_=cand2[:, 0:128]), reads=[b_cand2], writes=[b_io128])
            b_lnp = load_ln(l, 2)
            c.dma("sp", lambda e: e.dma_start(out=kst, in_=k12T_d[l]), writes=[b_kst], sembuf=b_misc)
            c.op("dve", lambda e: e.tensor_copy(out=kTb, in_=kst), reads=[b_kst], writes=[b_kTb])
            out_evs = []

            b_wa2 = bufs(2, "wa2")
            b_qTp2 = bufs(2, "qTp2")
            qTp2 = [qTp, V(o_qtp2, [128, 256], BF16)]

            def score_thunks(blk):
                def t1(ch):
                    wv = w_a[:, :, (ch % 2) * 128:(ch % 2 + 1) * 128]
                    load_w(pwq_d[l], ch * 128, 128, wv, b_wa2[ch % 2], eng_cast="act")

                def t2(ch):
                    wv = w_a[:, :, (ch % 2) * 128:(ch % 2 + 1) * 128]
                    for kc in range(KC):
                        mm(ps[0][:, 0:256], wv[:, kc, :], xT[:, kc, blk * 256:(blk + 1) * 256], kc == 0, kc == KC - 1,
                           [b_wa2[ch % 2]] + b_xT[2 * blk:2 * blk + 2], [b_ps[0]])
                    copy_op("act", qTp2[ch % 2], ps[0][:, 0:256], [b_ps[0]], [b_qTp2[ch % 2]])

                def t3(ch):
                    for ti in range(2):
                        bk2 = 1 + ti
                        mm(ps[bk2][:, 0:128], qTp2[ch % 2][:, ti * 128:(ti + 1) * 128], kTb[:, ch % 2, :], True, True,
                           [b_qTp2[ch % 2], b_kTb], [b_ps[bk2]])
                        copy_op("act", s_all[:, ti, ch, :], ps[bk2][:, 0:128], [b_ps[bk2]], [b_sall[ti]])

                th = []
                for i in range(18):
                    def one(i=i):
                        if i - 2 >= 0:
                            t3(i - 2)
                        if 0 <= i - 1 < 16:
                            t2(i - 1)
                        if i < 16:
                            t1(i)
                    th.append(one)
                    th.append(lambda: None)
                return th

            def routing_thunks(blk):
                th = []
                A = th.append
                for ti in range(2):
                    tt = blk * 2 + ti
                    t4 = tt % 4
                    bs = b_sall[ti]
                    Su = s_all_u[:, ti].rearrange("p a b -> p (a b)")
                    S3u = s_all_u[:, ti]
                    A(lambda Su=Su, bs=bs: c.op("dve", lambda e: e.tensor_single_scalar(out=Su, in_=Su, scalar=0xFFFFFF80, op=ALU.bitwise_and),
                                                reads=[], writes=[bs]))
                    A(lambda S3u=S3u, bs=bs: c.op("dve", lambda e: e.tensor_tensor(out=S3u, in0=S3u, in1=io128u.unsqueeze(1).to_broadcast([128, 16, 128]),
                                                                                     op=ALU.bitwise_or), reads=[b_io128], writes=[bs]))
                    for ch in range(16):
                        h, sd = ch // 2, ch % 2
                        sv = s_all[:, ti, ch, :]
                        A(lambda sv=sv, h=h, sd=sd, bs=bs: c.op("dve", lambda e: e.max(out=Vt_[:, h, sd, 0:8], in_=sv), reads=[bs], writes=[b_V]))
                        A(lambda sv=sv, h=h, sd=sd, bs=bs: c.op("dve", lambda e: e.match_replace(out=tmp128, in_to_replace=Vt_[:, h, sd, 0:8], in_values=sv, imm_value=NEG),
                                                                 reads=[bs, b_V], writes=[b_tmp]))
                        A(lambda h=h, sd=sd: c.op("dve", lambda e: e.max(out=Vt_[:, h, sd, 8:16], in_=tmp128), reads=[b_tmp], writes=[b_V]))
                    A(lambda: c.op("dve", lambda e: e.tensor_single_scalar(out=I12u, in_=Vt_u, scalar=127, op=ALU.bitwise_and), reads=[b_V], writes=[b_I12u]))
                    A(lambda: c.op("dve", lambda e: e.tensor_copy(out=I12f, in_=I12u), reads=[b_I12u], writes=[b_I12f]))
                    cand4 = s_all[:, ti].rearrange("p a b -> p (a b)").rearrange("p (h a b) -> p h a b", h=8, a=16)
                    cand3 = s_all[:, ti].rearrange("p a b -> p (a b)").rearrange("p (h q) -> p h q", h=8)
                    cand3u = s_all_u[:, ti].rearrange("p a b -> p (a b)").rearrange("p (h q) -> p h q", h=8)
                    A(lambda cand4=cand4, bs=bs: c.op("dve", lambda e: e.tensor_tensor(
                        out=cand4, in0=Vt_[:, :, 0, :].unsqueeze(3).to_broadcast([128, 8, 16, 16]),
                        in1=Vt_[:, :, 1, :].unsqueeze(2).to_broadcast([128, 8, 16, 16]), op=ALU.add), reads=[b_V], writes=[bs]))
                    A(lambda cand3u=cand3u, bs=bs: c.op("dve", lambda e: e.tensor_single_scalar(out=cand3u, in_=cand3u, scalar=0xFFFFFF00, op=ALU.bitwise_and),
                                                        reads=[], writes=[bs]))
                    A(lambda cand3u=cand3u, bs=bs: c.op("dve", lambda e: e.tensor_tensor(out=cand3u, in0=cand3u, in1=io256u.unsqueeze(1).to_broadcast([128, 8, 256]),
                                                                                         op=ALU.bitwise_or), reads=[b_io256], writes=[bs]))
                    for h in range(8):
                        cv = cand3[:, h, :]
                        A(lambda cv=cv, h=h, bs=bs: c.op("dve", lambda e: e.max(out=SC[:, h, 0:8], in_=cv), reads=[bs], writes=[b_SC]))
                        A(lambda cv=cv, h=h, bs=bs: c.op("dve", lambda e: e.match_replace(out=cand2, in_to_replace=SC[:, h, 0:8], in_values=cv, imm_value=NEG),
                                                         reads=[bs, b_SC], writes=[b_cand2]))
                        A(lambda h=h: c.op("dve", lambda e: e.max(out=SC[:, h, 8:16], in_=cand2), reads=[b_cand2], writes=[b_SC]))
                    A(lambda: c.op("dve", lambda e: e.tensor_scalar(out=abu[:, :, 0, :], in0=SCu, scalar1=255, scalar2=4, op0=ALU.bitwise_and,
                                                                    op1=ALU.logical_shift_right), reads=[b_SC], writes=[b_abu]))
                    A(lambda: c.op("dve", lambda e: e.tensor_single_scalar(out=abu[:, :, 1, :], in_=SCu, scalar=15, op=ALU.bitwise_and), reads=[b_SC], writes=[b_abu]))
                    A(lambda: c.op("dve", lambda e: e.tensor_copy(out=abf, in_=abu), reads=[b_abu], writes=[b_abf]))
                    for sd in range(2):
                        A(lambda sd=sd: c.op("dve", lambda e: e.tensor_tensor(
                            out=oh, in0=abf[:, :, sd, :].unsqueeze(3).to_broadcast([128, 8, 16, 16]),
                            in1=iota16b.unsqueeze(1).unsqueeze(1).to_broadcast([128, 8, 16, 16]), op=ALU.is_equal), reads=[b_abf, b_iota16b], writes=[b_oh]))
                        A(lambda sd=sd: c.op("dve", lambda e: e.tensor_tensor(
                            out=oh, in0=oh, in1=I12f[:, :, sd, :].unsqueeze(2).to_broadcast([128, 8, 16, 16]), op=ALU.mult), reads=[b_I12f], writes=[b_oh]))
                        A(lambda sd=sd: c.op("dve", lambda e: e.reduce_sum(out=e12[:, :, sd, :], in_=oh, axis=AX.X), reads=[b_oh], writes=[b_e12]))
                    A(lambda: c.op("dve", lambda e: e.scalar_tensor_tensor(out=exf, in0=e12[:, :, 0, :], scalar=128.0, in1=e12[:, :, 1, :],
                                                                           op0=ALU.mult, op1=ALU.add), reads=[b_e12], writes=[b_exf]))
                    A(lambda t4=t4: c.op("dve", lambda e: e.tensor_copy(out=idx_all[:, t4, :].rearrange("p (h k) -> p h k", h=8), in_=exf),
                                         reads=[b_exf], writes=[b_idx[t4]]))
                    A(lambda: c.op("dve", lambda e: e.tensor_tensor(out=exg, in0=SC, in1=SC[:, :, 0:1].to_broadcast([128, 8, 16]), op=ALU.subtract),
                                   reads=[b_SC], writes=[b_exg]))
                    A(lambda: c.op("act", lambda e: e.activation(out=exg, in_=exg, func=AF.Exp), reads=[], writes=[b_exg]))
                    A(lambda: c.op("dve", lambda e: e.reduce_sum(out=sm[:, 0, :], in_=exg, axis=AX.X), reads=[b_exg], writes=[b_sm]))
                    A(lambda: c.op("dve", lambda e: e.reciprocal(out=sm[:, 1, :], in_=sm[:, 0, :]), reads=[], writes=[b_sm]))
                    A(lambda t4=t4: c.op("dve", lambda e: e.tensor_tensor(out=gate_all[:, t4, :].rearrange("p (h k) -> p h k", h=8), in0=exg,
                                                                         in1=sm[:, 1, :].unsqueeze(2).to_broadcast([128, 8, 16]), op=ALU.mult),
                                         reads=[b_exg, b_sm], writes=[b_gate[t4]]))
                return th

            gi = [0]
            cgt = V(o - 0, [128, 1]) if False else None
            b_act1 = bufs(16, "act1"); b_coef1 = bufs(16, "coef1")
            D_DOT, D_ACC, D_GELU, D_CG, D_DG, D_MM = 2, 3, 4, 5, 6, 7

            def gathers(blk, pending):
                stream = [(blk * 2 + ti, sl_) for ti in range(2) for sl_ in range(128)]
                N = len(stream)
                per = (len(pending) + 239) // 240 if pending else 0
                gmap = {}
                for n in range(N + D_MM):
                    m = n
                    if 0 <= m < N:
                        tt, sl_ = stream[m]
                        t4 = tt % 4
                        g = gi[0] % NG; gi[0] += 1
                        gmap[m] = g
                        c.dma("pool", lambda e, g=g, t4=t4, sl_=sl_: e.indirect_dma_start(
                            out=G[g], out_offset=None, in_=uvb_d[l][:, :],
                            in_offset=bass.IndirectOffsetOnAxis(ap=idx_all[:, t4, sl_:sl_ + 1], axis=0)),
                            reads=[b_idx[t4]] + b_uvb[l], writes=[b_G[g]])
                    m = n - D_DOT
                    if 0 <= m < N:
                        tt, sl_ = stream[m]
                        g = gmap[m]
                        if sl_ == 0:
                            c.op("act", lambda e, tt=tt: e.activation(out=xb, in_=xs[:, tt, :], func=AF.Copy), reads=[b_xs[tt]], writes=[b_xb])
                        if sl_ % 2 == 0:
                            c.op("dve", lambda e, g=g, tt=tt, sl_=sl_: e.scalar_tensor_tensor(
                                out=G[g][:, 0:D], in0=G[g][:, 0:D], scalar=1.0, in1=xs[:, tt, :], op0=ALU.mult, op1=ALU.mult,
                                accum_out=actt[:, sl_:sl_ + 1]), reads=[b_xs[tt]], writes=[b_G[g], b_act1[m % 16]])
                        else:
                            c.op("dve", lambda e, g=g: e.tensor_tensor(out=G[g][:, 0:D], in0=G[g][:, 0:D], in1=xb, op=ALU.mult),
                                 reads=[b_xb], writes=[b_G[g]])
                    m = n - D_ACC
                    if 0 <= m < N and stream[m][1] % 2 == 1:
                        tt, sl_ = stream[m]
                        g = gmap[m]
                        c.op("act", lambda e, sl_=sl_, g=g: e.activation(out=G[g][:, 0:D], in_=G[g][:, 0:D], func=AF.Copy, accum_out=actt[:, sl_:sl_ + 1]),
                             reads=[], writes=[b_G[g], b_act1[m % 16]])
                    for m in (n - D_GELU, n - D_GELU + 1):
                        if not (0 <= m < N) or (stream[m][1] % 2 == 1) != (m == n - D_GELU):
                            continue
                        tt, sl_ = stream[m]
                        c.op("act", lambda e, sl_=sl_: e.activation(out=coef[:, sl_:sl_ + 1], in_=actt[:, sl_:sl_ + 1], func=AF.Gelu),
                             reads=[b_act1[m % 16]], writes=[b_coef1[m % 16]])
                    for m in (n - D_CG, n - D_CG + 1):
                        if not (0 <= m < N) or (stream[m][1] % 2 == 1) != (m == n - D_CG):
                            continue
                        tt, sl_ = stream[m]
                        t4 = tt % 4
                        c.op("dve", lambda e, sl_=sl_, t4=t4: e.tensor_tensor(out=coef[:, sl_:sl_ + 1], in0=coef[:, sl_:sl_ + 1],
                                                                             in1=gate_all[:, t4, sl_:sl_ + 1], op=ALU.mult),
                             reads=[b_gate[t4]], writes=[b_coef1[m % 16]])
                    for m in (n - D_DG, n - D_DG + 1):
                        if not (0 <= m < N) or (stream[m][1] % 2 == 1) != (m == n - D_DG):
                            continue
                        tt, sl_ = stream[m]
                        d_ = m % NDG
                        c.op("act", lambda e, sl_=sl_, d_=d_: e.activation(out=dg[d_], in_=identb, func=AF.Copy, scale=coef[:, sl_:sl_ + 1]),
                             reads=[b_identb, b_coef1[m % 16]], writes=[b_dg[d_]])
                    for m in (n - D_MM, n - D_MM + 1):
                        if not (0 <= m < N) or (stream[m][1] % 2 == 1) != (m == n - D_MM):
                            continue
                        tt, sl_ = stream[m]
                        g = gmap[m]
                        d_ = m % NDG
                        AB = 4 + 2 * (tt % 2)
                        for hf in range(2):
                            mm(ps[AB + hf][:, :], dg[d_], G[g][:, D + hf * 512:D + (hf + 1) * 512], sl_ == 0, sl_ == 127,
                               [b_dg[d_], b_G[g]], [b_ps[AB + hf]])
                        if sl_ == 127:
                            for hf in range(2):
                                c.op("dve", lambda e, tt=tt, hf=hf, AB=AB: e.scalar_tensor_tensor(
                                    out=xs[:, tt, hf * 512:(hf + 1) * 512], in0=xs[:, tt, hf * 512:(hf + 1) * 512], scalar=ALPHA,
                                    in1=ps[AB + hf][:, :], op0=ALU.mult, op1=ALU.add), reads=[b_ps[AB + hf]], writes=[b_xs[tt]])
                            layer_norm([tt], b_lnp, aff_eng="dve")
                            if is_last:
                                out_evs.append(c.dma("sp", lambda e, tt=tt: e.dma_start(out=out_d[s, tt * 128:(tt + 1) * 128, :], in_=xs[:, tt, :]),
                                                     reads=[b_xs[tt]], writes=[], sembuf=b_outd))
                    for _ in range(per):
                        if pending:
                            pending.pop(0)()
                while pending:
                    pending.pop(0)()

            b_G = bufs(NG, "G")
            for t_ in score_thunks(0) + routing_thunks(0):
                t_()
            for blk in range(NB2):
                pending = []
                if blk + 1 < NB2:
                    pending = score_thunks(blk + 1) + routing_thunks(blk + 1)
                gathers(blk, pending)
            c.barrier()
            wst_n[0] = WST
            return out_evs

        b_uvb = [bufs(4, f"uvb{l}") for l in range(L)]
        CH = 2048

        def emit_conv(l, part):
            if "peer" not in phases:
                return
            fns = []
            for r in range(part * 2, part * 2 + 2):
                fns.append(lambda e, r=r: e.dma_start(out=uvb_d[l][r * CH:(r + 1) * CH, 0:D], in_=pu_d[l][r * CH:(r + 1) * CH, :]))
                fns.append(lambda e, r=r: e.dma_start(out=uvb_d[l][r * CH:(r + 1) * CH, D:2 * D], in_=pv_d[l][r * CH:(r + 1) * CH, :]))
            c.dma("pool", fns, writes=[b_uvb[l][part]])

        conv_state = {"todo": []}

        def conv_tick():
            if conv_state["todo"]:
                l_, p_ = conv_state["todo"].pop(0)
                emit_conv(l_, p_)

        out_events = []
        for s in range(NSEQ):
            c.barrier()
            c.dma("sp", [(lambda e, s=s, tt=tt: e.dma_start(out=xs[:, tt, :], in_=x_d[s, tt * 128:(tt + 1) * 128, :]))
                         for tt in range(NT)], writes=list(b_xs), sembuf=b_misc)
            dumped = False
            for l in range(L):
                if "mixer" in phases:
                    if s == 0:
                        conv_state["todo"] = [(l, p_) for p_ in range(4)]
                    make_xT()
                    mixer(l)
                    while conv_state["todo"]:
                        conv_tick()
                    b_lnp = load_ln(l, 0)
                    layer_norm(range(NT), b_lnp, add_eng="pool")
                if stop_after == (l, "ln1"):
                    break
                if "xattn" in phases:
                    make_xT()
                    xattn(l, s)
                    b_lnp = load_ln(l, 1)
                    layer_norm(range(NT), b_lnp, add_eng="pool")
                if stop_after == (l, "ln2"):
                    break
                if "peer" in phases:
                    make_xT()
                    last = (l == L - 1) or stop_after == (l, "ln3")
                    evs = peer(l, s, last)
                    if last:
                        out_events.extend(evs)
                        dumped = True
                if stop_after == (l, "ln3"):
                    break
            if not dumped:
                for tt in range(NT):
                    out_events.append(c.dma("sp", lambda e, s=s, tt=tt: e.dma_start(out=out_d[s, tt * 128:(tt + 1) * 128, :], in_=xs[:, tt, :]),
                                            reads=[b_xs[tt]], writes=[], sembuf=b_outd))
        c.emit(out_events[-1:])
    return nc


def host_consts():
    kp = np.arange(128)[:, None]
    xx = np.arange(896)[None, :]
    nble = np.where((xx - 384) >= kp, 0.0, -30000.0).astype(np.float32)
    mlt = ((xx - 384) > kp).astype(np.float32)
    jj = np.arange(128)[:, None]
    ss = np.arange(128)[None, :]
    negtri = -(jj > ss).astype(np.float32)
    return {
        "c_ident": np.eye(128, dtype=np.float32),
        "c_nble": nble, "c_mlt": mlt, "c_negtri": negtri,
        "c_iota16": np.tile(np.arange(16, dtype=np.float32)[None, :], (128, 1)),
        "c_iota256": np.tile(np.arange(256, dtype=np.float32)[None, :], (128, 1)),
    }


def host_weights(inp, L):
    f = lambda a: np.ascontiguousarray(np.asarray(a, dtype=np.float32))
    w = {}
    w["w_in"] = f(inp["w_in"][:L]); w["w_out"] = f(inp["w_out"][:L])
    rgp = np.zeros((L, 2, 128, 8), np.float32)
    cw = np.asarray(inp["rg_conv_w"])[:L]
    for k in range(4):
        rgp[:, :, :, k] = cw[:, k, :].reshape(L, 2, 128)
    rgp[:, :, :, 4] = np.asarray(inp["rg_conv_b"])[:L].reshape(L, 2, 128)
    rgp[:, :, :, 5] = np.asarray(inp["rg_ba"])[:L].reshape(L, 2, 128)
    rgp[:, :, :, 6] = np.asarray(inp["rg_bi"])[:L].reshape(L, 2, 128)
    rgp[:, :, :, 7] = np.asarray(inp["rg_lambda"])[:L].reshape(L, 2, 128)
    w["rgp"] = rgp
    for nm, src in (("wa_bd", "rg_wa"), ("wi_bd", "rg_wi")):
        a = np.asarray(inp[src])[:L]
        bd = np.zeros((L, 2, 128, 128), np.float32)
        for h in range(4):
            j, o = h // 2, (h % 2) * 64
            bd[:, j, o:o + 64, o:o + 64] = a[:, h]
        w[nm] = bd
    w["fox_bf"] = f(np.asarray(inp["fox_bf"])[:L].reshape(L, 4, 1))
    scp = np.zeros((L, 2, 128, 4), np.float32)
    sw = np.asarray(inp["sc_conv_w"])[:L]
    for k in range(3):
        scp[:, :, :, k] = sw[:, k, :].reshape(L, 2, 128)
    w["scp"] = scp
    w["mng"] = f(np.asarray(inp["mix_norm_g"])[:L].reshape(L, 8, 128).transpose(0, 2, 1))
    w["ln_g"] = f(np.stack([np.asarray(inp["ln1_g"])[:L], np.asarray(inp["ln2_g"])[:L], np.asarray(inp["ln3_g"])[:L]], axis=1))
    w["ln_b"] = f(np.stack([np.asarray(inp["ln1_b"])[:L], np.asarray(inp["ln2_b"])[:L], np.asarray(inp["ln3_b"])[:L]], axis=1))
    w["xa_wq"] = f(inp["xa_wq"][:L]); w["xa_wkv"] = f(inp["xa_wkv"][:L]); w["xa_wo"] = f(inp["xa_wo"][:L])
    w["peer_wq"] = f(inp["peer_wq"][:L])
    w["k12T"] = f(np.stack([np.asarray(inp["peer_k1"])[:L].transpose(0, 2, 1),
                            np.asarray(inp["peer_k2"])[:L].transpose(0, 2, 1)], axis=2))
    for l in range(L):
        w[f"peer_u{l}"] = f(inp["peer_u"][l]); w[f"peer_v{l}"] = f(inp["peer_v"][l])
    w.update(host_consts())
    return w


def run(inp, n_cores=8, NSEQ=2, S=2048, L=2, stop_after=None, trace=False, phases=("mixer", "xattn", "peer")):
    nc = build_program(NSEQ=NSEQ, S=S, L=L, stop_after=stop_after, phases=phases)
    w = host_weights(inp, L)
    x = np.asarray(inp["x"], dtype=np.float32)
    mem = np.asarray(inp["mem"], dtype=np.float32)
    in_maps = []
    for ci in range(n_cores):
        m = dict(w)
        m["x"] = np.ascontiguousarray(x[ci * NSEQ:(ci + 1) * NSEQ, :S])
        m["memT"] = np.ascontiguousarray(mem[ci * NSEQ:(ci + 1) * NSEQ].transpose(0, 2, 1))
        in_maps.append(m)
    res = run_bass_kernel_spmd(nc, in_maps, core_ids=list(range(n_cores)), trace=trace)
    out = np.concatenate([r["out"] for r in res.results], axis=0)
    return out, res


def kernel(**inputs):
    out, _ = run(inputs)
    return out.astype(np.float32)
```

```python
import numpy as np
from contextlib import ExitStack
import concourse.bass as bass
import concourse.mybir as mybir
from concourse.bass_utils import run_bass_kernel_spmd

F32 = mybir.dt.float32
BF16 = mybir.dt.bfloat16
U32 = mybir.dt.uint32
AF = mybir.ActivationFunctionType
ALU = mybir.AluOpType
AX = mybir.AxisListType

D = 1024
KC = 8
GW = 256
OFF_RG_X = 0
OFF_RG_G = 256
OFF_FOX = 512
OFF_FOX_F = 1280
OFF_SB = 1284
OFF_SC = 2052
N_IN = 2820
NMEM = 256
NE = 16384
DEPTH = 2
ALPHA = (2.0 * DEPTH) ** 0.25
LN_EPS = 1e-5
NEG = -1.0e30

ENGS = ("pe", "act", "dve", "pool", "sp")


class Buf:
    __slots__ = ("name", "w", "r", "sem", "cnt")

    def __init__(self, name=""):
        self.name = name
        self.w = None
        self.r = {}
        self.sem = None
        self.cnt = 0


def bufs(n, name=""):
    return [Buf(f"{name}{i}") for i in range(n)]


class Ctx:
    def __init__(self, nc, stack):
        self.nc = nc
        self.stack = stack
        self.q = {e: [] for e in ENGS}
        self.esem = {}
        self.ecnt = {e: 0 for e in ENGS}
        for e in ("pe", "act", "dve", "pool"):
            self.esem[e] = stack.enter_context(nc.semaphore("s_" + e))
        self.nsem = 4
        self.dsb = []

    def barrier(self):
        evs = [(self.esem[e], self.ecnt[e], "x") for e in self.esem if self.ecnt[e] > 0]
        evs += [(b.sem, b.cnt, "dma") for b in self.dsb if b.cnt > 0]
        for e in ENGS:
            self.q[e].append((list(evs), None, None, 0))

    def newsem(self, name):
        self.nsem += 1
        return self.stack.enter_context(self.nc.semaphore(name))

    def _deps(self, reads, writes):
        deps = []
        for b in reads:
            if b.w is not None:
                deps.append(b.w)
        for b in writes:
            if b.w is not None:
                deps.append(b.w)
            deps.extend(b.r.values())
        return deps

    def _commit(self, ev, reads, writes):
        k = id(ev[0])
        for b in reads:
            o = b.r.get(k)
            if o is None or o[1] < ev[1]:
                b.r[k] = ev
        for b in writes:
            b.w = ev
            b.r = {}

    def op(self, eng, fn, reads=(), writes=()):
        deps = self._deps(reads, writes)
        self.ecnt[eng] += 1
        ev = (self.esem[eng], self.ecnt[eng], eng)
        self.q[eng].append((deps, fn, ev[0], 1))
        self._commit(ev, reads, writes)
        return ev

    def dma(self, eng, fns, reads=(), writes=(), sembuf=None):
        if not isinstance(fns, (list, tuple)):
            fns = [fns]
        sb = sembuf if sembuf is not None else writes[0]
        if sb.sem is None:
            sb.sem = self.newsem("d%d" % self.nsem)
            self.dsb.append(sb)
        deps = self._deps(reads, writes)
        if sb.cnt > 0:
            deps.append((sb.sem, sb.cnt, "dma"))
        for i, fn in enumerate(fns):
            self.q[eng].append((deps if i == 0 else [], fn, sb.sem, 16))
        sb.cnt += 16 * len(fns)
        ev = (sb.sem, sb.cnt, "dma")
        self._commit(ev, reads, writes)
        return ev

    def emit(self, final_events):
        nc = self.nc
        engmap = {"pe": "tensor", "act": "scalar", "dve": "vector", "pool": "gpsimd", "sp": "sync"}
        with nc.Block() as block:
            for e in ENGS:
                ops = self.q[e]
                fin = final_events if e == "sp" else []

                def body(engine, ops=ops, e=e, fin=fin):
                    waited = {}
                    for deps, fn, sem, inc in ops:
                        need = {}
                        for (s, v, pe) in deps:
                            if pe == e and e == "pe":
                                continue
                            k = id(s)
                            if waited.get(k, 0) >= v:
                                continue
                            if k not in need or need[k][1] < v:
                                need[k] = (s, v)
                        for k, (s, v) in need.items():
                            engine.wait_ge(s, v)
                            waited[k] = v
                        if fn is None:
                            continue
                        ins = fn(engine)
                        ins.then_inc(sem, inc)
                    for (s, v, pe) in fin:
                        engine.wait_ge(s, v)

                getattr(block, engmap[e])(body)


ARENA_BYTES = 87 * 1024 + 512
LN_OFF = 79 * 1024 + 512


def build_program(NSEQ=2, S=2048, L=2, stop_after=None, phases=("mixer", "xattn", "peer")):
    nc = bass.Bass("TRN2", target_bir_lowering=False)
    NT = S // 128
    NQ = S // 512
    NB2 = S // 256

    def din(name, shape, dt=F32):
        return nc.dram_tensor(name, list(shape), dt, kind="ExternalInput").ap()

    x_d = din("x", [NSEQ, S, D])
    memT_d = din("memT", [NSEQ, D, NMEM])
    w_in_d = din("w_in", [L, D, N_IN])
    w_out_d = din("w_out", [L, D, D])
    rgp_d = din("rgp", [L, 2, 128, 8])
    wa_d = din("wa_bd", [L, 2, 128, 128])
    wi_d = din("wi_bd", [L, 2, 128, 128])
    foxbf_d = din("fox_bf", [L, 4, 1])
    scp_d = din("scp", [L, 2, 128, 4])
    mng_d = din("mng", [L, 128, 8])
    lng_d = din("ln_g", [L, 3, D])
    lnb_d = din("ln_b", [L, 3, D])
    xwq_d = din("xa_wq", [L, D, D])
    xwkv_d = din("xa_wkv", [L, D, 2 * D])
    xwo_d = din("xa_wo", [L, D, D])
    pwq_d = din("peer_wq", [L, D, 2048])
    k12T_d = din("k12T", [L, 128, 2, 128])
    pu_d = [din(f"peer_u{l}", [NE, D]) for l in range(L)]
    pv_d = [din(f"peer_v{l}", [NE, D]) for l in range(L)]
    cid_d = din("c_ident", [128, 128])
    cmle_d = din("c_nble", [128, 896])
    cmlt_d = din("c_mlt", [128, 896])
    cntri_d = din("c_negtri", [128, 128])
    ciota_d = din("c_iota16", [128, 16])
    ciota256_d = din("c_iota256", [128, 256])
    out_d = nc.dram_tensor("out", [NSEQ, S, D], F32, kind="ExternalOutput").ap()
    uvb_d = [nc.dram_tensor(f"uvb{l}", [NE, 2 * D], BF16, kind="Internal").ap() for l in range(L)]

    with ExitStack() as st:
        c = Ctx(nc, st)

        def sb(name, shape, dt=F32):
            return st.enter_context(nc.sbuf_tensor("sb_" + name, list(shape), dt))

        b_lnp_shared = Buf("lnp")
        b_misc = Buf("misc")
        b_outd = Buf("outd")
        xs = sb("xs", [128, NT, D]); b_xs = bufs(NT, "xs")
        xT = sb("xT", [128, KC, S], BF16); b_xT = bufs(NT, "xT")
        ps = [st.enter_context(nc.psum_tensor(f"ps{i}", [128, 512], F32)) for i in range(8)]
        b_ps = bufs(8, "ps")
        ident = sb("ident", [128, 128]); b_ident = Buf("ident")
        nbl = sb("nbl", [128, 896], BF16); b_mle = Buf("nbl")
        mlt = sb("mlt", [128, 896], BF16); b_mlt = Buf("mlt")
        negtri = sb("negtri", [128, 128], BF16); b_negtri = Buf("negtri")
        negid = sb("negid", [128, 128], BF16); b_negid = Buf("negid")
        negones = sb("negones", [128, 128], BF16); b_negones = Buf("negones")
        ones_bf = sb("ones_bf", [128, 128], BF16); b_ones_bf = Buf("ones_bf")
        ones_f = sb("ones_f", [128, 128]); b_ones_f = Buf("ones_f")
        iota16 = sb("iota16", [128, 16]); b_iota = Buf("iota")
        WST = 2
        wstg = [sb(f"wstg{i}", [128, KC, 128]) for i in range(WST)]; b_wstg = bufs(WST, "wstg")
        w_a = sb("w_a", [128, KC, 256], BF16); b_w_a = Buf("w_a")
        w_b = sb("w_b", [128, KC, 256], BF16); b_w_b = Buf("w_b")
        rgp = sb("rgp", [128, 2, 8]); b_rgp = Buf("rgp")
        rgc = sb("rgc", [128, 2, 4]); b_rgc = Buf("rgc")
        wabd = sb("wabd", [128, 2, 128]); wibd = sb("wibd", [128, 2, 128]); b_wbd = Buf("wbd")
        scp = sb("scp", [128, 2, 4]); b_scp = Buf("scp")
        mng = sb("mng", [128, 8]); b_mng = Buf("mng")
        foxbf = sb("foxbf", [4, 1]); b_foxbf = Buf("foxbf")
        lnst = sb("lnst", [128, 2, 6]); b_lnst = Buf("lnst")
        lnmv = sb("lnmv", [128, 8]); b_lnmv = Buf("lnmv")
        cst = sb("cst", [128, 4]); b_cst = Buf("cst")
        scr = sb("scr", [128, ARENA_BYTES // 4])

        def V(off, shape, dt=F32):
            esz = 4 if dt in (F32, U32) else 2
            n = 1
            for d_ in shape[1:]:
                n *= d_
            nb = n * esz
            assert off % 4 == 0 and nb % 4 == 0 and off + nb <= ARENA_BYTES, (off, nb)
            a = scr[:, off // 4:(off + nb) // 4]
            if dt != F32:
                a = a.bitcast(dt)
            if len(shape) == 3:
                a = a.rearrange("p (a b) -> p a b", a=shape[1])
            elif len(shape) == 4:
                a = a.rearrange("p (a b c) -> p a b c", a=shape[1], b=shape[2])
            if shape[0] != 128:
                a = a[0:shape[0]]
            return a

        wst_i = [0]
        wst_n = [WST]

        def load_w(dram2d, col0, ncols, dst, b_dst, eng_cast="pool"):
            done = 0
            while done < ncols:
                n = min(128, ncols - done)
                i = wst_i[0] % wst_n[0]
                wst_i[0] += 1
                src = dram2d[:, col0 + done:col0 + done + n].rearrange("(kc p) c -> p kc c", p=128)
                c.dma("sp", lambda e, i=i, n=n, src=src: e.dma_start(out=wstg[i][:, :, 0:n], in_=src), writes=[b_wstg[i]])
                if eng_cast == "act":
                    c.op("act", lambda e, i=i, n=n, done=done: e.activation(out=dst[:, :, done:done + n], in_=wstg[i][:, :, 0:n], func=AF.Copy),
                         reads=[b_wstg[i]], writes=[b_dst])
                else:
                    c.op(eng_cast, lambda e, i=i, n=n, done=done: e.tensor_copy(out=dst[:, :, done:done + n], in_=wstg[i][:, :, 0:n]),
                         reads=[b_wstg[i]], writes=[b_dst])
                done += n

        def load_rows(dram_rows, dst, b_dst, eng_cast="pool"):
            for hf in range(2):
                i = wst_i[0] % wst_n[0]
                wst_i[0] += 1
                stg = wstg[i][:].rearrange("p a b -> p (a b)").rearrange("p (a b) -> p a b", a=2)
                src = dram_rows[:, hf * 512:(hf + 1) * 512].rearrange("(c p) d -> p c d", p=128)
                c.dma("sp", lambda e, stg=stg, src=src: e.dma_start(out=stg, in_=src), writes=[b_wstg[i]])
                if eng_cast == "act":
                    c.op("act", lambda e, stg=stg, hf=hf: e.activation(out=dst[:, :, hf * 512:(hf + 1) * 512], in_=stg, func=AF.Copy),
                         reads=[b_wstg[i]], writes=[b_dst])
                else:
                    c.op(eng_cast, lambda e, stg=stg, hf=hf: e.tensor_copy(out=dst[:, :, hf * 512:(hf + 1) * 512], in_=stg),
                         reads=[b_wstg[i]], writes=[b_dst])

        cstage = V(0, [128, 896]); b_cstage = Buf("cstage")
        c.dma("sp", lambda e: e.dma_start(out=ident[:], in_=cid_d[:, :]), writes=[b_ident], sembuf=b_misc)
        c.dma("sp", lambda e: e.dma_start(out=iota16[:], in_=ciota_d[:, :]), writes=[b_iota], sembuf=b_misc)
        c.dma("sp", lambda e: e.dma_start(out=cstage, in_=cmle_d[:, :]), writes=[b_cstage], sembuf=b_misc)
        c.op("dve", lambda e: e.tensor_copy(out=nbl[:], in_=cstage), reads=[b_cstage], writes=[b_mle])
        c.dma("sp", lambda e: e.dma_start(out=cstage, in_=cmlt_d[:, :]), writes=[b_cstage], sembuf=b_misc)
        c.op("dve", lambda e: e.tensor_copy(out=mlt[:], in_=cstage), reads=[b_cstage], writes=[b_mlt])
        c.dma("sp", lambda e: e.dma_start(out=cstage[:, 0:128], in_=cntri_d[:, :]), writes=[b_cstage], sembuf=b_misc)
        c.op("dve", lambda e: e.tensor_copy(out=negtri[:], in_=cstage[:, 0:128]), reads=[b_cstage], writes=[b_negtri])
        c.op("dve", lambda e: e.tensor_scalar(out=negid[:], in0=ident[:], scalar1=-1.0, scalar2=None, op0=ALU.mult),
             reads=[b_ident], writes=[b_negid])
        c.op("dve", lambda e: e.memset(negones[:], -1.0), writes=[b_negones])
        c.op("dve", lambda e: e.memset(ones_bf[:], 1.0), writes=[b_ones_bf])
        c.op("dve", lambda e: e.memset(ones_f[:], 1.0), writes=[b_ones_f])
        c.op("dve", lambda e: e.memset(cst[:, 0:1], LN_EPS), writes=[b_cst])
        c.op("dve", lambda e: e.memset(cst[:, 1:2], 1e-6), writes=[b_cst])
        c.op("dve", lambda e: e.memset(cst[:, 2:3], 1.0), writes=[b_cst])
        EPS, EPS6, ONE = cst[:, 0:1], cst[:, 1:2], cst[:, 2:3]

        rr = [0]

        def evac_eng():
            rr[0] += 1
            return "act" if rr[0] % 2 == 0 else "dve"

        def copy_op(eng, out_ap, in_ap, reads, writes, scale=None):
            if eng == "act":
                if scale is None:
                    c.op("act", lambda e: e.activation(out=out_ap, in_=in_ap, func=AF.Copy), reads, writes)
                else:
                    c.op("act", lambda e: e.activation(out=out_ap, in_=in_ap, func=AF.Copy, scale=scale), reads, writes)
            else:
                if scale is None:
                    c.op(eng, lambda e: e.tensor_copy(out=out_ap, in_=in_ap), reads, writes)
                else:
                    c.op(eng, lambda e: e.tensor_scalar(out=out_ap, in0=in_ap, scalar1=scale, scalar2=None, op0=ALU.mult),
                         reads, writes)

        def mm(out_ap, lhsT, rhs, start, stop, reads, writes):
            c.op("pe", lambda e: e.matmul(out_ap, lhsT, rhs, start=start, stop=stop), reads, writes)

        def make_xT():
            for tt in range(NT):
                for g in range(2):
                    bk = (tt * 2 + g) % 8
                    for j in range(4):
                        kc = 4 * g + j
                        c.op("pe", lambda e, tt=tt, kc=kc, j=j, bk=bk: e.transpose(
                            ps[bk][:, j * 128:(j + 1) * 128], xs[:, tt, kc * 128:(kc + 1) * 128], ident[:]),
                            reads=[b_xs[tt], b_ident], writes=[b_ps[bk]])
                    copy_op(evac_eng(), xT[:, 4 * g:4 * g + 4, tt * 128:(tt + 1) * 128],
                            ps[bk][:].rearrange("p (j q) -> p j q", j=4), [b_ps[bk]], [b_xT[tt]])

        def projT(w_ap, b_w, M, q, bk, c0=0):
            for kc in range(KC):
                mm(ps[bk][0:M, :], w_ap[:, kc, c0:c0 + M], xT[:, kc, q * 512:(q + 1) * 512], kc == 0, kc == KC - 1,
                   [b_w] + b_xT[4 * q:4 * q + 4], [b_ps[bk]])

        lng = V(LN_OFF, [128, D]); lnb = V(LN_OFF + 4096, [128, D])

        def load_ln(l, which):
            b = b_lnp_shared
            c.dma("sp", [lambda e: e.dma_start(out=lng, in_=lng_d[l, which].partition_broadcast(128)),
                         lambda e: e.dma_start(out=lnb, in_=lnb_d[l, which].partition_broadcast(128))], writes=[b])
            return b

        def layer_norm(tts, b_lnp, aff_eng="dve", add_eng="dve"):
            for tt in tts:
                c.op("dve", lambda e, tt=tt: e.bn_stats(out=lnst[:, 0, :], in_=xs[:, tt, 0:512]),
                     reads=[b_xs[tt]], writes=[b_lnst])
                c.op("dve", lambda e, tt=tt: e.bn_stats(out=lnst[:, 1, :], in_=xs[:, tt, 512:1024]),
                     reads=[b_xs[tt]], writes=[b_lnst])
                c.op("dve", lambda e: e.bn_aggr(out=lnmv[:, 0:2], in_=lnst[:].rearrange("p a b -> p (a b)")),
                     reads=[b_lnst], writes=[b_lnmv])
                c.op("act", lambda e: e.activation(out=lnmv[:, 2:3], in_=lnmv[:, 1:2], func=AF.Sqrt, bias=EPS),
                     reads=[b_cst], writes=[b_lnmv])
                c.op("dve", lambda e: e.reciprocal(out=lnmv[:, 3:4], in_=lnmv[:, 2:3]), reads=[], writes=[b_lnmv])
                c.op("dve", lambda e: e.scalar_tensor_tensor(out=lnmv[:, 4:5], in0=lnmv[:, 0:1], scalar=-1.0,
                                                              in1=lnmv[:, 3:4], op0=ALU.mult, op1=ALU.mult),
                     reads=[], writes=[b_lnmv])
                c.op("act", lambda e, tt=tt: e.activation(out=xs[:, tt, :], in_=xs[:, tt, :], func=AF.Identity,
                                                          scale=lnmv[:, 3:4], bias=lnmv[:, 4:5]),
                     reads=[b_lnmv], writes=[b_xs[tt]])
                c.op(aff_eng, lambda e, tt=tt: e.tensor_tensor(out=xs[:, tt, :], in0=xs[:, tt, :], in1=lng, op=ALU.mult),
                     reads=[b_lnp], writes=[b_xs[tt]])
                c.op(add_eng, lambda e, tt=tt: e.tensor_tensor(out=xs[:, tt, :], in0=xs[:, tt, :], in1=lnb, op=ALU.add),
                     reads=[b_lnp], writes=[b_xs[tt]])

        def mixer(l):
            c.barrier()
            w_in_l = w_in_d[l]
            K1 = 1024
            Yg = V(0, [128, 2, S]); b_Yg = bufs(2, "Yg")
            yTn = V(16 * K1, [128, 2, S], BF16); b_yTn = Buf("yTn")
            woutg = V(24 * K1, [128, 2, D], BF16); b_woutg = Buf("woutg")
            rden = V(28 * K1, [128, 512]); b_rden = Buf("rden")
            TB = 30 * K1
            TW = 8256

            def out_proj_group(g, first):
                load_rows(w_out_d[l][g * 256:(g + 1) * 256, :], woutg, b_woutg)
                for tt in range(NT):
                    for h in range(2):
                        bk = (tt * 2 + h) % 8
                        for cc in range(2):
                            mm(ps[bk][:, :], yTn[:, cc, tt * 128:(tt + 1) * 128], woutg[:, cc, h * 512:(h + 1) * 512],
                               cc == 0, cc == 1, [b_yTn, b_woutg], [b_ps[bk]])
                        if first:
                            c.op("dve", lambda e, tt=tt, h=h, bk=bk: e.scalar_tensor_tensor(
                                out=xs[:, tt, h * 512:(h + 1) * 512], in0=xs[:, tt, h * 512:(h + 1) * 512], scalar=ALPHA,
                                in1=ps[bk][:, :], op0=ALU.mult, op1=ALU.add), reads=[b_ps[bk]], writes=[b_xs[tt]])
                        else:
                            c.op("dve", lambda e, tt=tt, h=h, bk=bk: e.tensor_tensor(
                                out=xs[:, tt, h * 512:(h + 1) * 512], in0=xs[:, tt, h * 512:(h + 1) * 512],
                                in1=ps[bk][:, :], op=ALU.add), reads=[b_ps[bk]], writes=[b_xs[tt]])

            def group_rms_and_proj(g, first, sqA, b_sqA, sqB, b_sqB):
                for q in range(NQ):
                    sl = slice(q * 512, (q + 1) * 512)
                    bk = q % 4
                    c.op("act", lambda e, sl=sl: e.activation(out=sqA[:, 0:512], in_=Yg[:, 0, sl], func=AF.Square),
                         reads=[b_Yg[0]], writes=[b_sqA])
                    c.op("act", lambda e, sl=sl: e.activation(out=sqB[:, 0:512], in_=Yg[:, 1, sl], func=AF.Square),
                         reads=[b_Yg[1]], writes=[b_sqB])
                    mm(ps[bk][:, :], ones_f[:], sqA[:, 0:512], True, False, [b_ones_f, b_sqA], [b_ps[bk]])
                    mm(ps[bk][:, :], ones_f[:], sqB[:, 0:512], False, True, [b_ones_f, b_sqB], [b_ps[bk]])
                    c.op("act", lambda e, bk=bk: e.activation(out=rden, in_=ps[bk][:, :], func=AF.Ln,
                                                              scale=1.0 / 256.0, bias=EPS6),
                         reads=[b_ps[bk], b_cst], writes=[b_rden])
                    c.op("act", lambda e: e.activation(out=rden, in_=rden, func=AF.Exp, scale=-0.5), reads=[], writes=[b_rden])
                    for cc in range(2):
                        c.op("dve", lambda e, cc=cc, sl=sl: e.scalar_tensor_tensor(
                            out=yTn[:, cc, sl], in0=Yg[:, cc, sl], scalar=mng[:, 2 * g + cc:2 * g + cc + 1], in1=rden,
                            op0=ALU.mult, op1=ALU.mult), reads=[b_Yg[cc], b_mng, b_rden], writes=[b_yTn])
                out_proj_group(g, first)

            c.dma("sp", lambda e: e.dma_start(out=rgp[:], in_=rgp_d[l].rearrange("j p k -> p j k")), writes=[b_rgp], sembuf=b_misc)
            c.dma("sp", [lambda e: e.dma_start(out=wabd[:], in_=wa_d[l].rearrange("j p k -> p j k")),
                         lambda e: e.dma_start(out=wibd[:], in_=wi_d[l].rearrange("j p k -> p j k"))], writes=[b_wbd], sembuf=b_misc)
            c.dma("sp", lambda e: e.dma_start(out=scp[:], in_=scp_d[l].rearrange("j p k -> p j k")), writes=[b_scp], sembuf=b_misc)
            c.dma("sp", lambda e: e.dma_start(out=mng[:], in_=mng_d[l]), writes=[b_mng], sembuf=b_misc)
            c.dma("sp", lambda e: e.dma_start(out=foxbf[:], in_=foxbf_d[l]), writes=[b_foxbf], sembuf=b_misc)
            c.op("act", lambda e: e.activation(out=rgc[:, :, 2], in_=rgp[:, :, 7], func=AF.Exp, scale=-1.0),
                 reads=[b_rgp], writes=[b_rgc])
            c.op("act", lambda e: e.activation(out=rgc[:, :, 3], in_=rgc[:, :, 2], func=AF.Ln, bias=ONE),
                 reads=[b_cst], writes=[b_rgc])
            c.op("dve", lambda e: e.tensor_scalar(out=rgc[:, :, 0], in0=rgc[:, :, 3], scalar1=-8.0, scalar2=None, op0=ALU.mult),
                 reads=[], writes=[b_rgc])
            c.op("dve", lambda e: e.tensor_scalar(out=rgc[:, :, 1], in0=rgc[:, :, 3], scalar1=-16.0, scalar2=None, op0=ALU.mult),
                 reads=[], writes=[b_rgc])

            T1 = V(TB, [128, S + 4]); b_T1 = Buf("T1")
            T2 = V(TB + TW, [128, S + 4]); b_T2 = Buf("T2")
            T3 = V(TB + 2 * TW, [128, S + 4]); b_T3 = Buf("T3")
            T4 = V(TB + 3 * TW, [128, S + 4]); b_T4 = Buf("T4")

            conv_tick()
            for j in range(2):
                T5 = Yg[:, j, :]; b_T5 = b_Yg[j]
                load_w(w_in_l, OFF_RG_X + j * 128, 128, w_a, b_w_a, eng_cast="act")
                load_w(w_in_l, OFF_RG_G + j * 128, 128, w_b, b_w_b, eng_cast="act")
                c.op("pool", lambda e: e.memset(T1[:, 0:4], 0.0), writes=[b_T1])
                for q in range(NQ):
                    projT(w_a, b_w_a, 128, q, q)
                    copy_op("act", T1[:, 4 + q * 512:4 + (q + 1) * 512], ps[q][:, :], [b_ps[q]], [b_T1])
                c.op("dve", lambda e, j=j: e.tensor_scalar(out=T2[:, 0:S], in0=T1[:, 4:4 + S], scalar1=rgp[:, j, 3:4],
                                                           scalar2=rgp[:, j, 4:5], op0=ALU.mult, op1=ALU.add),
                     reads=[b_T1, b_rgp], writes=[b_T2])
                for k in range(3):
                    c.op("dve", lambda e, j=j, k=k: e.scalar_tensor_tensor(
                        out=T2[:, 0:S], in0=T1[:, 1 + k:1 + k + S], scalar=rgp[:, j, k:k + 1], in1=T2[:, 0:S],
                        op0=ALU.mult, op1=ALU.add), reads=[b_T1, b_rgp], writes=[b_T2])
                for q in range(NQ):
                    sl = slice(q * 512, (q + 1) * 512)
                    mm(ps[4 + q % 4][:, :], wabd[:, j, :], T2[:, sl], True, True, [b_wbd, b_T2], [b_ps[4 + q % 4]])
                    c.op("act", lambda e, j=j, q=q, sl=sl: e.activation(out=T3[:, sl], in_=ps[4 + q % 4][:, :], func=AF.Sigmoid,
                                                                       bias=rgp[:, j, 5:6]),
                         reads=[b_ps[4 + q % 4], b_rgp], writes=[b_T3])
                for q in range(NQ):
                    sl = slice(q * 512, (q + 1) * 512)
                    mm(ps[q % 4][:, :], wibd[:, j, :], T2[:, sl], True, True, [b_wbd, b_T2], [b_ps[q % 4]])
                    c.op("act", lambda e, j=j, q=q, sl=sl: e.activation(out=T4[:, sl], in_=ps[q % 4][:, :], func=AF.Sigmoid,
                                                                       bias=rgp[:, j, 6:7]),
                         reads=[b_ps[q % 4], b_rgp], writes=[b_T4])
                c.op("act", lambda e, j=j, T5=T5: e.activation(out=T5, in_=T3[:, 0:S], func=AF.Exp, scale=rgc[:, j, 0:1]),
                     reads=[b_T3, b_rgc], writes=[b_T5])
                c.op("act", lambda e, j=j: e.activation(out=T3[:, 0:S], in_=T3[:, 0:S], func=AF.Exp, scale=rgc[:, j, 1:2]),
                     reads=[b_rgc], writes=[b_T3])
                c.op("act", lambda e: e.activation(out=T3[:, 0:S], in_=T3[:, 0:S], func=AF.Sqrt, scale=-1.0, bias=ONE),
                     reads=[b_cst], writes=[b_T3])
                c.op("dve", lambda e: e.tensor_tensor(out=T4[:, 0:S], in0=T4[:, 0:S], in1=T3[:, 0:S], op=ALU.mult),
                     reads=[b_T3], writes=[b_T4])
                c.op("dve", lambda e: e.tensor_tensor(out=T4[:, 0:S], in0=T4[:, 0:S], in1=T2[:, 0:S], op=ALU.mult),
                     reads=[b_T2], writes=[b_T4])
                c.op("dve", lambda e, T5=T5: e.tensor_tensor_scan(out=T3[:, 0:S], data0=T5, data1=T4[:, 0:S], initial=0.0,
                                                                  op0=ALU.mult, op1=ALU.add),
                     reads=[b_T5, b_T4], writes=[b_T3])
                for q in range(NQ):
                    sl = slice(q * 512, (q + 1) * 512)
                    projT(w_b, b_w_b, 128, q, 4 + q % 4)
                    c.op("act", lambda e, q=q, sl=sl: e.activation(out=T2[:, sl], in_=ps[4 + q % 4][:, :], func=AF.Gelu_apprx_tanh),
                         reads=[b_ps[4 + q % 4]], writes=[b_T2])
                c.op("dve", lambda e, j=j: e.tensor_tensor(out=Yg[:, j, :], in0=T3[:, 0:S], in1=T2[:, 0:S], op=ALU.mult),
                     reads=[b_T3, b_T2], writes=[b_Yg[j]])
            group_rms_and_proj(0, True, T1, b_T1, T4, b_T4)

            qTh = V(30 * K1, [68, 4, S], BF16)
            kTh = V(46 * K1, [68, 4, S], BF16)
            Vt = V(62 * K1, [128, NT, 256], BF16)
            PB0 = 70 * K1
            NPB = 3
            Pb = [V(PB0 + i * K1, [128, 512], BF16) for i in range(NPB)]
            Eb = [V(PB0 + 3 * K1 + i * 2 * K1, [128, 512]) for i in range(2)]
            SPb = [V(PB0 + 7 * K1 + i * K1, [128, 512], BF16) for i in range(3)]
            Lpb = [V(PB0 + 10 * K1 + i * K1, [128, 512], BF16) for i in range(3)]
            Lcb = [V(PB0 + 13 * K1 + i * K1, [128, 512], BF16) for i in range(3)]
            for grp, off in ((1, OFF_FOX), (2, OFF_SB)):
                is_fox = grp == 1
                c.barrier()
                conv_tick()
                b_qTh = bufs(4, "qTh"); b_kTh = bufs(4, "kTh"); b_Vt = bufs(NT, "Vt")
                b_Pb = bufs(NPB, "Pb"); b_Eb = bufs(2, "Eb"); b_SPb = bufs(3, "SPb"); b_Lpb = bufs(3, "Lpb"); b_Lcb = bufs(3, "Lcb")
                b_Yg = bufs(2, "Yg"); b_yTn = Buf("yTn"); b_rden = Buf("rden"); b_woutg = Buf("woutg")
                if is_fox:
                    fl = V(30 * K1, [4, S]); fl2 = V(38 * K1, [4, S])
                    chi = V(46 * K1, [4, S], BF16); clo = V(50 * K1, [4, S], BF16)
                    nchi = V(54 * K1, [4, S], BF16); nclo = V(58 * K1, [4, S], BF16)
                    b_fl, b_fl2, b_chi, b_clo, b_nchi, b_nclo = (Buf("fl"), Buf("fl2"), Buf("chi"), Buf("clo"), Buf("nchi"), Buf("nclo"))
                    fones = V(62 * K1, [4, S]); b_fones = Buf("fones")
                    c.op("dve", lambda e: e.memset(fones, 1.0), writes=[b_fones])
                    load_w(w_in_l, OFF_FOX_F, 4, w_a, b_w_a)
                    for q in range(NQ):
                        sl = slice(q * 512, (q + 1) * 512)
                        projT(w_a, b_w_a, 4, q, q % 4)
                        c.op("dve", lambda e, q=q, sl=sl: e.tensor_scalar(out=fl[:, sl], in0=ps[q % 4][0:4, :], scalar1=foxbf[:, 0:1],
                                                                            scalar2=None, op0=ALU.add),
                             reads=[b_ps[q % 4], b_foxbf], writes=[b_fl])
                    c.op("act", lambda e: e.activation(out=fl, in_=fl, func=AF.Exp, scale=-1.0), reads=[], writes=[b_fl])
                    c.op("act", lambda e: e.activation(out=fl, in_=fl, func=AF.Ln, bias=cst[0:4, 2:3]),
                         reads=[b_cst], writes=[b_fl])
                    c.op("dve", lambda e: e.tensor_tensor_scan(out=fl2, data0=fones, data1=fl, initial=0.0,
                                                               op0=ALU.mult, op1=ALU.add),
                         reads=[b_fl, b_fones], writes=[b_fl2])
                    c.op("dve", lambda e: e.tensor_copy(out=nchi, in_=fl2), reads=[b_fl2], writes=[b_nchi])
                    c.op("dve", lambda e: e.tensor_tensor(out=nclo, in0=fl2, in1=nchi, op=ALU.subtract),
                         reads=[b_fl2, b_nchi], writes=[b_nclo])
                    c.op("dve", lambda e: e.tensor_scalar(out=chi, in0=nchi, scalar1=-1.0, scalar2=None, op0=ALU.mult),
                         reads=[b_nchi], writes=[b_chi])
                    c.op("dve", lambda e: e.tensor_scalar(out=clo, in0=nclo, scalar1=-1.0, scalar2=None, op0=ALU.mult),
                         reads=[b_nclo], writes=[b_clo])
                    for h in range(4):
                        c.op("dve", lambda e, h=h: e.memset(qTh[64:68, h, :], 1.0), writes=[b_qTh[h]])
                        c.op("dve", lambda e, h=h: e.memset(kTh[64:68, h, :], 1.0), writes=[b_kTh[h]])
                        c.dma("sp", [lambda e, h=h: e.dma_start(out=qTh[64:65, h, :], in_=chi[h:h + 1, :]),
                                     lambda e, h=h: e.dma_start(out=qTh[65:66, h, :], in_=clo[h:h + 1, :])],
                              reads=[b_chi, b_clo], writes=[b_qTh[h]], sembuf=b_misc)
                        c.dma("sp", [lambda e, h=h: e.dma_start(out=kTh[66:67, h, :], in_=nchi[h:h + 1, :]),
                                     lambda e, h=h: e.dma_start(out=kTh[67:68, h, :], in_=nclo[h:h + 1, :])],
                              reads=[b_nchi, b_nclo], writes=[b_kTh[h]], sembuf=b_misc)
                    c.barrier()
                KR = 68 if is_fox else 64
                for which, dst, b_dst, coff, scl in ((0, qTh, b_qTh, off, 0.125), (1, kTh, b_kTh, off + GW, None)):
                    for hp in range(2):
                        wt, bwt = (w_a, b_w_a) if hp == 0 else (w_b, b_w_b)
                        load_w(w_in_l, coff + hp * 128, 128, wt, bwt)
                        for hh in range(2):
                            h = hp * 2 + hh
                            for q in range(NQ):
                                bk = (h * NQ + q) % 8
                                projT(wt, bwt, 64, q, bk, c0=hh * 64)
                                copy_op(evac_eng(), dst[0:64, h, q * 512:(q + 1) * 512], ps[bk][0:64, :], [b_ps[bk]], [b_dst[h]],
                                        scale=scl)
                load_w(w_in_l, off + 2 * GW, 256, w_b, b_w_b)
                for tt in range(NT):
                    bk = tt % 8
                    for kc in range(KC):
                        mm(ps[bk][:, 0:256], xT[:, kc, tt * 128:(tt + 1) * 128], w_b[:, kc, 0:256], kc == 0, kc == KC - 1,
                           [b_w_b, b_xT[tt]], [b_ps[bk]])
                    copy_op(evac_eng(), Vt[:, tt, :], ps[bk][:, 0:256], [b_ps[bk]], [b_Vt[tt]])
                pairs = []
                for h in range(4):
                    for q in range(NQ):
                        nA = 4 * q + 4
                        order = list(range(nA)) if is_fox else list(range(nA - 1, -1, -1))
                        for idx, A in enumerate(order):
                            pairs.append((h, q, idx, A, nA))

                def geom(m):
                    h, q, idx, A, nA = pairs[m]
                    diag = A >= 4 * q
                    moff = 384 - 128 * (A - 4 * q)
                    par = (h * NQ + q) % 2
                    return dict(h=h, q=q, idx=idx, A=A, nA=nA, first=idx == 0, last=idx == nA - 1, diag=diag, moff=moff,
                                ks=slice(A * 128, (A + 1) * 128), qs=slice(q * 512, (q + 1) * 512),
                                sbk=m % 3, pb=m % 3, eb=m % 2, s3=m % 3, wbk=3 + (m % 2), par=par, lc=idx % 3)

                def st_score(m):
                    g = geom(m)
                    h, sbk = g["h"], g["sbk"]
                    mm(ps[sbk][:, :], kTh[0:KR, h, g["ks"]], qTh[0:KR, h, g["qs"]], True, True, [b_kTh[h], b_qTh[h]], [b_ps[sbk]])
                    if is_fox and g["diag"]:
                        mo = g["moff"]
                        c.op("dve", lambda e: e.tensor_tensor(out=ps[sbk][:, :], in0=ps[sbk][:, :], in1=nbl[:, mo:mo + 512], op=ALU.add),
                             reads=[b_mle], writes=[b_ps[sbk]])

                def st_fox_exp(m):
                    g = geom(m)
                    sbk, pb = g["sbk"], g["pb"]
                    c.op("act", lambda e: e.activation(out=Pb[pb], in_=ps[sbk][:, :], func=AF.Exp), reads=[b_ps[sbk]], writes=[b_Pb[pb]])

                def st_fox_pv(m):
                    g = geom(m)
                    h, q, A, pb, par = g["h"], g["q"], g["A"], g["pb"], g["par"]
                    cc, hp = h // 2, h % 2
                    vsl = slice(cc * 128, (cc + 1) * 128)
                    prt = slice(hp * 64, (hp + 1) * 64)
                    ob, db = 4 + par, 6 + par
                    mm(ps[ob][:, :], Vt[:, A, vsl], Pb[pb], g["first"], g["last"], [b_Vt[A], b_Pb[pb]], [b_ps[ob]])
                    mm(ps[db][:, :], ones_bf[:], Pb[pb], g["first"], g["last"], [b_ones_bf, b_Pb[pb]], [b_ps[db]])
                    if g["last"]:
                        qs = g["qs"]
                        c.op("dve", lambda e: e.reciprocal(out=rden, in_=ps[db][:, :]), reads=[b_ps[db]], writes=[b_rden])
                        c.op("dve", lambda e: e.tensor_tensor(out=Yg[prt, cc, qs], in0=ps[ob][prt, :], in1=rden[prt, :], op=ALU.mult),
                             reads=[b_ps[ob], b_rden], writes=[b_Yg[cc]])

                def st_sb_elem(m):
                    g = geom(m)
                    sbk, eb, s3, lc = g["sbk"], g["eb"], g["s3"], g["lc"]
                    c.op("act", lambda e: e.activation(out=Eb[eb], in_=ps[sbk][:, :], func=AF.Exp, scale=-1.0), reads=[b_ps[sbk]], writes=[b_Eb[eb]])
                    c.op("act", lambda e: e.activation(out=SPb[s3], in_=Eb[eb], func=AF.Ln, bias=ONE), reads=[b_Eb[eb], b_cst], writes=[b_SPb[s3]])
                    c.op("dve", lambda e: e.tensor_tensor(out=Lpb[s3], in0=ps[sbk][:, :], in1=SPb[s3], op=ALU.add),
                         reads=[b_ps[sbk], b_SPb[s3]], writes=[b_Lpb[s3]])
                    if g["diag"]:
                        mo = g["moff"]
                        c.op("pool", lambda e: e.tensor_tensor(out=Lpb[s3], in0=Lpb[s3], in1=mlt[:, mo:mo + 512], op=ALU.mult),
                             reads=[b_mlt], writes=[b_Lpb[s3]])
                    if not g["last"]:
                        nx = (lc + 1) % 3
                        if g["first"]:
                            c.op("dve", lambda e: e.tensor_copy(out=Lcb[nx], in_=Lpb[s3]), reads=[b_Lpb[s3]], writes=[b_Lcb[nx]])
                        else:
                            c.op("dve", lambda e: e.tensor_tensor(out=Lcb[nx], in0=Lcb[lc], in1=Lpb[s3], op=ALU.add),
                                 reads=[b_Lcb[lc], b_Lpb[s3]], writes=[b_Lcb[nx]])

                def st_sb_w(m):
                    g = geom(m)
                    s3, wbk, lc, first = g["s3"], g["wbk"], g["lc"], g["first"]
                    mm(ps[wbk][:, :], negid[:], SPb[s3], True, False, [b_negid, b_SPb[s3]], [b_ps[wbk]])
                    mm(ps[wbk][:, :], negtri[:], Lpb[s3], False, first, [b_negtri, b_Lpb[s3]], [b_ps[wbk]])
                    if not first:
                        mm(ps[wbk][:, :], negones[:], Lcb[lc], False, True, [b_negones, b_Lcb[lc]], [b_ps[wbk]])

                def st_sb_expw(m):
                    g = geom(m)
                    wbk, pb = g["wbk"], g["pb"]
                    c.op("act", lambda e: e.activation(out=Pb[pb], in_=ps[wbk][:, :], func=AF.Exp), reads=[b_ps[wbk]], writes=[b_Pb[pb]])
                    if g["diag"]:
                        mo = g["moff"]
                        c.op("pool", lambda e: e.tensor_tensor(out=Pb[pb], in0=Pb[pb], in1=mlt[:, mo:mo + 512], op=ALU.mult),
                             reads=[b_mlt], writes=[b_Pb[pb]])

                def st_sb_pv(m):
                    g = geom(m)
                    h, q, A, pb, par = g["h"], g["q"], g["A"], g["pb"], g["par"]
                    cc, hp = h // 2, h % 2
                    vsl = slice(cc * 128, (cc + 1) * 128)
                    prt = slice(hp * 64, (hp + 1) * 64)
                    ob = 5 + par
                    mm(ps[ob][:, :], Vt[:, A, vsl], Pb[pb], g["first"], g["last"], [b_Vt[A], b_Pb[pb]], [b_ps[ob]])
                    if g["last"]:
                        qs = g["qs"]
                        c.op("act", lambda e: e.activation(out=Yg[prt, cc, qs], in_=ps[ob][prt, :], func=AF.Copy), reads=[b_ps[ob]], writes=[b_Yg[cc]])

                stages = [st_score, st_fox_exp, st_fox_pv] if is_fox else [st_score, st_sb_elem, st_sb_w, st_sb_expw, st_sb_pv]
                NP_ = len(pairs)
                for n in range(NP_ + len(stages) - 1):
                    for k in range(len(stages) - 1, -1, -1):
                        m = n - k
                        if 0 <= m < NP_:
                            stages[k](m)
                group_rms_and_proj(grp, False, Eb[0], b_Eb[0], Eb[1], b_Eb[1])

            c.barrier()
            conv_tick()
            b_T1, b_T2, b_T3, b_T4 = Buf("T1"), Buf("T2"), Buf("T3"), Buf("T4")
            b_Yg = bufs(2, "Yg"); b_yTn = Buf("yTn"); b_rden = Buf("rden"); b_woutg = Buf("woutg")
            for j in range(2):
                load_w(w_in_l, OFF_SC + GW + j * 128, 128, w_a, b_w_a, eng_cast="act")
                load_w(w_in_l, OFF_SC + 2 * GW + j * 128, 128, w_b, b_w_b, eng_cast="act")
                for q in range(NQ):
                    sl = slice(q * 512, (q + 1) * 512)
                    projT(w_a, b_w_a, 128, q, q % 4)
                    copy_op("act", T3[:, sl], ps[q % 4][:, :], [b_ps[q % 4]], [b_T3])
                c.op("pool", lambda e: e.memset(T1[:, 0:4], 0.0), writes=[b_T1])
                for q in range(NQ):
                    projT(w_b, b_w_b, 128, q, 4 + q % 4)
                    c.op("dve", lambda e, q=q: e.tensor_tensor(out=T1[:, 4 + q * 512:4 + (q + 1) * 512], in0=ps[4 + q % 4][:, :],
                                                               in1=T3[:, q * 512:(q + 1) * 512], op=ALU.mult),
                         reads=[b_ps[4 + q % 4], b_T3], writes=[b_T1])
                c.op("dve", lambda e, j=j: e.tensor_scalar(out=T2[:, 0:S], in0=T1[:, 4:4 + S], scalar1=scp[:, j, 2:3], scalar2=None,
                                                           op0=ALU.mult), reads=[b_T1, b_scp], writes=[b_T2])
                for k in range(2):
                    c.op("dve", lambda e, j=j, k=k: e.scalar_tensor_tensor(
                        out=T2[:, 0:S], in0=T1[:, 2 + k:2 + k + S], scalar=scp[:, j, k:k + 1], in1=T2[:, 0:S],
                        op0=ALU.mult, op1=ALU.add), reads=[b_T1, b_scp], writes=[b_T2])
                load_w(w_in_l, OFF_SC + j * 128, 128, w_a, b_w_a, eng_cast="act")
                for q in range(NQ):
                    projT(w_a, b_w_a, 128, q, q % 4)
                    c.op("dve", lambda e, q=q, j=j: e.tensor_tensor(out=Yg[:, j, q * 512:(q + 1) * 512], in0=ps[q % 4][:, :],
                                                                    in1=T2[:, q * 512:(q + 1) * 512], op=ALU.mult),
                         reads=[b_ps[q % 4], b_T2], writes=[b_Yg[j]])
            group_rms_and_proj(3, False, T3, b_T3, T4, b_T4)
            c.barrier()

        def xattn(l, s):
            c.barrier()
            K1 = 1024
            wqx = V(0, [128, KC, D], BF16); b_wqx = Buf("wqx")
            woh = V(16 * K1, [128, 2, D], BF16); b_woh = Buf("woh")
            KT = V(20 * K1, [128, 8, NMEM], BF16); b_KT = Buf("KT")
            Vx = V(24 * K1, [128, 2, D], BF16); b_Vx = Buf("Vx")
            memf = V(28 * K1, [128, KC, NMEM]); b_memf = Buf("memf")
            memb = V(36 * K1, [128, KC, NMEM], BF16); b_memb = Buf("memb")
            qTx = V(40 * K1, [128, 2, 512], BF16); b_qTx = Buf("qTx")
            pTx = [V(42 * K1 + i * K1, [128, 512], BF16) for i in range(2)]; b_pTx = bufs(2, "pTx")
            oTn = V(46 * K1, [128, 2, 512], BF16); b_oTn = Buf("oTn")
            rdx = V(48 * K1, [128, 512]); b_rdx = Buf("rdx")
            load_w(xwq_d[l], 0, D, wqx, b_wqx, eng_cast="act")
            c.dma("sp", lambda e: e.dma_start(out=memf, in_=memT_d[s].rearrange("(kc p) m -> p kc m", p=128)), writes=[b_memf], sembuf=b_misc)
            c.op("act", lambda e: e.activation(out=memb, in_=memf, func=AF.Copy), reads=[b_memf], writes=[b_memb])
            b_wa_h = bufs(2, "wa_h"); b_wb_h = bufs(2, "wb_h")
            for cch in range(8):
                wv = w_a[:, :, (cch % 2) * 128:(cch % 2 + 1) * 128]
                load_w(xwkv_d[l], cch * 128, 128, wv, b_wa_h[cch % 2], eng_cast="act")
                bk = cch % 8
                for kc in range(KC):
                    mm(ps[bk][:, 0:NMEM], wv[:, kc, :], memb[:, kc, :], kc == 0, kc == KC - 1, [b_wa_h[cch % 2], b_memb], [b_ps[bk]])
                copy_op(evac_eng(), KT[:, cch, :], ps[bk][:, 0:NMEM], [b_ps[bk]], [b_KT])
            for cch in range(8):
                wv = w_b[:, :, (cch % 2) * 128:(cch % 2 + 1) * 128]
                load_w(xwkv_d[l], D + cch * 128, 128, wv, b_wb_h[cch % 2], eng_cast="act")
                for mt in range(2):
                    bk = (cch * 2 + mt) % 8
                    for kc in range(KC):
                        mm(ps[bk][:, 0:128], memb[:, kc, mt * 128:(mt + 1) * 128], wv[:, kc, :], kc == 0, kc == KC - 1,
                           [b_wb_h[cch % 2], b_memb], [b_ps[bk]])
                    copy_op(evac_eng(), Vx[:, mt, cch * 128:(cch + 1) * 128], ps[bk][:, 0:128], [b_ps[bk]], [b_Vx])
            woh2 = [woh, V(50 * K1, [128, 2, D], BF16)]; b_woh2 = bufs(2, "woh2")
            qTx2 = [qTx, V(54 * K1, [128, 2, 512], BF16)]; b_qTx2 = bufs(2, "qTx2")
            pTx2 = [[V(56 * K1 + (2 * i + j) * K1, [128, 512], BF16) for j in range(2)] for i in range(2)]
            b_pTx2 = [bufs(2, "pTxa"), bufs(2, "pTxb")]
            oTn2 = [oTn, V(60 * K1, [128, 2, 512], BF16)]; b_oTn2 = bufs(2, "oTn2")
            its = [(h, q) for h in range(4) for q in range(NQ)]

            def stA(i):
                h, q = its[i]
                if q == 0:
                    load_rows(xwo_d[l][h * 256:(h + 1) * 256, :], woh2[h % 2], b_woh2[h % 2], eng_cast="act")
                for c2 in range(2):
                    projT(wqx, b_wqx, 128, q, 0, c0=h * 256 + c2 * 128)
                    copy_op(evac_eng(), qTx2[i % 2][:, c2, :], ps[0][:, :], [b_ps[0]], [b_qTx2[i % 2]], scale=1.0 / 16.0)

            def stB(i):
                h, q = its[i]
                for mt in range(2):
                    for c2 in range(2):
                        mm(ps[1 + mt][:, :], KT[:, 2 * h + c2, mt * 128:(mt + 1) * 128], qTx2[i % 2][:, c2, :], c2 == 0, c2 == 1,
                           [b_KT, b_qTx2[i % 2]], [b_ps[1 + mt]])
                    c.op("act", lambda e, mt=mt: e.activation(out=pTx2[i % 2][mt], in_=ps[1 + mt][:, :], func=AF.Exp),
                         reads=[b_ps[1 + mt]], writes=[b_pTx2[i % 2][mt]])

            def stC(i):
                h, q = its[i]
                pt, bpt = pTx2[i % 2], b_pTx2[i % 2]
                for mt in range(2):
                    mm(ps[3][:, :], ones_bf[:], pt[mt], mt == 0, mt == 1, [b_ones_bf, bpt[mt]], [b_ps[3]])
                for c2 in range(2):
                    for mt in range(2):
                        mm(ps[4 + c2][:, :], Vx[:, mt, h * 256 + c2 * 128:h * 256 + (c2 + 1) * 128], pt[mt], mt == 0, mt == 1,
                           [b_Vx, bpt[mt]], [b_ps[4 + c2]])
                c.op("act", lambda e: e.activation(out=rdx, in_=ps[3][:, :], func=AF.Ln), reads=[b_ps[3]], writes=[b_rdx])
                c.op("act", lambda e: e.activation(out=rdx, in_=rdx, func=AF.Exp, scale=-1.0), reads=[], writes=[b_rdx])
                for c2 in range(2):
                    c.op("dve", lambda e, c2=c2: e.tensor_tensor(out=oTn2[i % 2][:, c2, :], in0=ps[4 + c2][:, :], in1=rdx, op=ALU.mult),
                         reads=[b_ps[4 + c2], b_rdx], writes=[b_oTn2[i % 2]])

            def stD(i):
                h, q = its[i]
                for tsub in range(4):
                    tt = 4 * q + tsub
                    for hf in range(2):
                        bk = 6 + (tsub * 2 + hf) % 2
                        for c2 in range(2):
                            mm(ps[bk][:, :], oTn2[i % 2][:, c2, tsub * 128:(tsub + 1) * 128], woh2[h % 2][:, c2, hf * 512:(hf + 1) * 512],
                               c2 == 0, c2 == 1, [b_oTn2[i % 2], b_woh2[h % 2]], [b_ps[bk]])
                        if h == 0:
                            c.op("dve", lambda e, tt=tt, hf=hf, bk=bk: e.scalar_tensor_tensor(
                                out=xs[:, tt, hf * 512:(hf + 1) * 512], in0=xs[:, tt, hf * 512:(hf + 1) * 512], scalar=ALPHA,
                                in1=ps[bk][:, :], op0=ALU.mult, op1=ALU.add), reads=[b_ps[bk]], writes=[b_xs[tt]])
                        else:
                            c.op("dve", lambda e, tt=tt, hf=hf, bk=bk: e.tensor_tensor(
                                out=xs[:, tt, hf * 512:(hf + 1) * 512], in0=xs[:, tt, hf * 512:(hf + 1) * 512],
                                in1=ps[bk][:, :], op=ALU.add), reads=[b_ps[bk]], writes=[b_xs[tt]])

            xst = [stA, stB, stC, stD]
            NI = len(its)
            for n in range(NI + len(xst) - 1):
                for k in range(len(xst) - 1, -1, -1):
                    i = n - k
                    if 0 <= i < NI:
                        xst[k](i)
            c.barrier()

        def peer(l, s, is_last):
            c.barrier()
            K1 = 1024
            s_all = V(0, [128, 2, 16, 128]); b_sall = bufs(2, "sall")
            s_all_u = V(0, [128, 2, 16, 128], U32)
            idx_all = V(16 * K1, [128, 4, 128], U32); b_idx = bufs(4, "idx")
            gate_all = V(18 * K1, [128, 4, 128]); b_gate = bufs(4, "gate")
            NGA = 9
            NG = NGA + 2
            G = [V(20 * K1 + i * 4 * K1, [128, 2 * D], BF16) for i in range(NGA)]
            G.append(w_b[:].rearrange("p a b -> p (a b)"))
            G.append(wstg[1][:].rearrange("p a b -> p (a b)").bitcast(BF16))
            wst_n[0] = 1
            o = 20 * K1 + NGA * 4 * K1
            qTp = V(o, [128, 256], BF16); b_qTp = Buf("qTp"); o += 512
            o_qtp2 = o; o += 512
            kTb = V(o, [128, 2, 128], BF16); b_kTb = Buf("kTb"); o += 512
            kst = V(o, [128, 2, 128]); b_kst = Buf("kst"); o += 1024
            xb = V(o, [128, D], BF16); b_xb = Buf("xb"); o += 2048
            Vt_ = V(o, [128, 8, 2, 16]); Vt_u = V(o, [128, 8, 2, 16], U32); b_V = Buf("V"); o += 1024
            tmp128 = V(o, [128, 128]); b_tmp = Buf("tmp128"); o += 512
            I12u = V(o, [128, 8, 2, 16], U32); b_I12u = Buf("I12u"); o += 1024
            I12f = V(o, [128, 8, 2, 16], BF16); b_I12f = Buf("I12f"); o += 512
            iota16b = V(o, [128, 16], BF16); b_iota16b = Buf("iota16b"); o += 32
            e12b = V(o, [128, 8, 2, 16], BF16); o += 512
            cand2 = V(o, [128, 256]); b_cand2 = Buf("cand2"); o += 1024
            SC = V(o, [128, 8, 16]); SCu = V(o, [128, 8, 16], U32); b_SC = Buf("SC"); o += 512
            abu = V(o, [128, 8, 2, 16], U32); b_abu = Buf("abu"); o += 1024
            abf = V(o, [128, 8, 2, 16], BF16); b_abf = Buf("abf"); o += 512
            oh = V(o, [128, 8, 16, 16], BF16); b_oh = Buf("oh"); o += 4096
            e12 = V(o, [128, 8, 2, 16]); b_e12 = Buf("e12"); o += 1024
            exf = V(o, [128, 8, 16]); b_exf = Buf("exf"); o += 512
            exg = V(o, [128, 8, 16]); b_exg = Buf("exg"); o += 512
            sm = V(o, [128, 2, 8]); b_sm = Buf("sm"); o += 64
            actt = V(o, [128, 128]); b_act = bufs(64, "act"); o += 512
            coef = V(o, [128, 128]); b_coef = bufs(64, "coef"); o += 512
            NDG = 8
            dg = [V(o + i * 256, [128, 128], BF16) for i in range(NDG)]; b_dg = bufs(NDG, "dg"); o += NDG * 256
            identb = V(o, [128, 128], BF16); b_identb = Buf("identb"); o += 256
            io128u = V(o, [128, 128], U32); b_io128 = Buf("io128"); o += 512
            io256u = V(o, [128, 256], U32); b_io256 = Buf("io256"); o += 1024
            assert o <= LN_OFF, o
            c.op("dve", lambda e: e.tensor_copy(out=identb, in_=ident[:]), reads=[b_ident], writes=[b_identb])
            c.op("dve", lambda e: e.tensor_copy(out=iota16b, in_=iota16[:]), reads=[b_iota], writes=[b_iota16b])
            c.dma("sp", lambda e: e.dma_start(out=cand2, in_=ciota256_d[:, :]), writes=[b_cand2], sembuf=b_misc)
            c.op("dve", lambda e: e.tensor_copy(out=io256u, in_=cand2), reads=[b_cand2], writes=[b_io256])
            c.op("dve", lambda e: e.tensor_copy(out=io128u, in_=cand2[:, 0:128]), reads=[b_cand2], writes=[b_io128])
            b_lnp = load_ln(l, 2)
            c.dma("sp", lambda e: e.dma_start(out=kst, in_=k12T_d[l]), writes=[b_kst], sembuf=b_misc)
            c.op("dve", lambda e: e.tensor_copy(out=kTb, in_=kst), reads=[b_kst], writes=[b_kTb])
            out_evs = []

            b_wa2 = bufs(2, "wa2")
            b_qTp2 = bufs(2, "qTp2")
            qTp2 = [qTp, V(o_qtp2, [128, 256], BF16)]

            def score_thunks(blk):
                def t1(ch):
                    wv = w_a[:, :, (ch % 2) * 128:(ch % 2 + 1) * 128]
                    load_w(pwq_d[l], ch * 128, 128, wv, b_wa2[ch % 2], eng_cast="act")

                def t2(ch):
                    wv = w_a[:, :, (ch % 2) * 128:(ch % 2 + 1) * 128]
                    for kc in range(KC):
                        mm(ps[0][:, 0:256], wv[:, kc, :], xT[:, kc, blk * 256:(blk + 1) * 256], kc == 0, kc == KC - 1,
                           [b_wa2[ch % 2]] + b_xT[2 * blk:2 * blk + 2], [b_ps[0]])
                    copy_op("act", qTp2[ch % 2], ps[0][:, 0:256], [b_ps[0]], [b_qTp2[ch % 2]])

                def t3(ch):
                    for ti in range(2):
                        bk2 = 1 + ti
                        mm(ps[bk2][:, 0:128], qTp2[ch % 2][:, ti * 128:(ti + 1) * 128], kTb[:, ch % 2, :], True, True,
                           [b_qTp2[ch % 2], b_kTb], [b_ps[bk2]])
                        copy_op("act", s_all[:, ti, ch, :], ps[bk2][:, 0:128], [b_ps[bk2]], [b_sall[ti]])

                th = []
                for i in range(18):
                    def one(i=i):
                        if i - 2 >= 0:
                            t3(i - 2)
                        if 0 <= i - 1 < 16:
                            t2(i - 1)
                        if i < 16:
                            t1(i)
                    th.append(one)
                    th.append(lambda: None)
                return th

            def routing_thunks(blk):
                th = []
                A = th.append
                for ti in range(2):
                    tt = blk * 2 + ti
                    t4 = tt % 4
                    bs = b_sall[ti]
                    Su = s_all_u[:, ti].rearrange("p a b -> p (a b)")
                    S3u = s_all_u[:, ti]
                    A(lambda Su=Su, bs=bs: c.op("dve", lambda e: e.tensor_single_scalar(out=Su, in_=Su, scalar=0xFFFFFF80, op=ALU.bitwise_and),
                                                reads=[], writes=[bs]))
                    A(lambda S3u=S3u, bs=bs: c.op("dve", lambda e: e.tensor_tensor(out=S3u, in0=S3u, in1=io128u.unsqueeze(1).to_broadcast([128, 16, 128]),
                                                                                     op=ALU.bitwise_or), reads=[b_io128], writes=[bs]))
                    for ch in range(16):
                        h, sd = ch // 2, ch % 2
                        sv = s_all[:, ti, ch, :]
                        A(lambda sv=sv, h=h, sd=sd, bs=bs: c.op("dve", lambda e: e.max(out=Vt_[:, h, sd, 0:8], in_=sv), reads=[bs], writes=[b_V]))
                        A(lambda sv=sv, h=h, sd=sd, bs=bs: c.op("dve", lambda e: e.match_replace(out=tmp128, in_to_replace=Vt_[:, h, sd, 0:8], in_values=sv, imm_value=NEG),
                                                                 reads=[bs, b_V], writes=[b_tmp]))
                        A(lambda h=h, sd=sd: c.op("dve", lambda e: e.max(out=Vt_[:, h, sd, 8:16], in_=tmp128), reads=[b_tmp], writes=[b_V]))
                    A(lambda: c.op("dve", lambda e: e.tensor_single_scalar(out=I12u, in_=Vt_u, scalar=127, op=ALU.bitwise_and), reads=[b_V], writes=[b_I12u]))
                    A(lambda: c.op("dve", lambda e: e.tensor_copy(out=I12f, in_=I12u), reads=[b_I12u], writes=[b_I12f]))
                    cand4 = s_all[:, ti].rearrange("p a b -> p (a b)").rearrange("p (h a b) -> p h a b", h=8, a=16)
                    cand3 = s_all[:, ti].rearrange("p a b -> p (a b)").rearrange("p (h q) -> p h q", h=8)
                    cand3u = s_all_u[:, ti].rearrange("p a b -> p (a b)").rearrange("p (h q) -> p h q", h=8)
                    A(lambda cand4=cand4, bs=bs: c.op("dve", lambda e: e.tensor_tensor(
                        out=cand4, in0=Vt_[:, :, 0, :].unsqueeze(3).to_broadcast([128, 8, 16, 16]),
                        in1=Vt_[:, :, 1, :].unsqueeze(2).to_broadcast([128, 8, 16, 16]), op=ALU.add), reads=[b_V], writes=[bs]))
                    A(lambda cand3u=cand3u, bs=bs: c.op("dve", lambda e: e.tensor_single_scalar(out=cand3u, in_=cand3u, scalar=0xFFFFFF00, op=ALU.bitwise_and),
                                                        reads=[], writes=[bs]))
                    A(lambda cand3u=cand3u, bs=bs: c.op("dve", lambda e: e.tensor_tensor(out=cand3u, in0=cand3u, in1=io256u.unsqueeze(1).to_broadcast([128, 8, 256]),
                                                                                         op=ALU.bitwise_or), reads=[b_io256], writes=[bs]))
                    for h in range(8):
                        cv = cand3[:, h, :]
                        A(lambda cv=cv, h=h, bs=bs: c.op("dve", lambda e: e.max(out=SC[:, h, 0:8], in_=cv), reads=[bs], writes=[b_SC]))
                        A(lambda cv=cv, h=h, bs=bs: c.op("dve", lambda e: e.match_replace(out=cand2, in_to_replace=SC[:, h, 0:8], in_values=cv, imm_value=NEG),
                                                         reads=[bs, b_SC], writes=[b_cand2]))
                        A(lambda h=h: c.op("dve", lambda e: e.max(out=SC[:, h, 8:16], in_=cand2), reads=[b_cand2], writes=[b_SC]))
                    A(lambda: c.op("dve", lambda e: e.tensor_scalar(out=abu[:, :, 0, :], in0=SCu, scalar1=255, scalar2=4, op0=ALU.bitwise_and,
                                                                    op1=ALU.logical_shift_right), reads=[b_SC], writes=[b_abu]))
                    A(lambda: c.op("dve", lambda e: e.tensor_single_scalar(out=abu[:, :, 1, :], in_=SCu, scalar=15, op=ALU.bitwise_and), reads=[b_SC], writes=[b_abu]))
                    A(lambda: c.op("dve", lambda e: e.tensor_copy(out=abf, in_=abu), reads=[b_abu], writes=[b_abf]))
                    for sd in range(2):
                        A(lambda sd=sd: c.op("dve", lambda e: e.tensor_tensor(
                            out=oh, in0=abf[:, :, sd, :].unsqueeze(3).to_broadcast([128, 8, 16, 16]),
                            in1=iota16b.unsqueeze(1).unsqueeze(1).to_broadcast([128, 8, 16, 16]), op=ALU.is_equal), reads=[b_abf, b_iota16b], writes=[b_oh]))
                        A(lambda sd=sd: c.op("dve", lambda e: e.tensor_tensor(
                            out=oh, in0=oh, in1=I12f[:, :, sd, :].unsqueeze(2).to_broadcast([128, 8, 16, 16]), op=ALU.mult), reads=[b_I12f], writes=[b_oh]))
                        A(lambda sd=sd: c.op("dve", lambda e: e.reduce_sum(out=e12[:, :, sd, :], in_=oh, axis=AX.X), reads=[b_oh], writes=[b_e12]))
                    A(lambda: c.op("dve", lambda e: e.scalar_tensor_tensor(out=exf, in0=e12[:, :, 0, :], scalar=128.0, in1=e12[:, :, 1, :],
                                                                           op0=ALU.mult, op1=ALU.add), reads=[b_e12], writes=[b_exf]))
                    A(lambda t4=t4: c.op("dve", lambda e: e.tensor_copy(out=idx_all[:, t4, :].rearrange("p (h k) -> p h k", h=8), in_=exf),
                                         reads=[b_exf], writes=[b_idx[t4]]))
                    A(lambda: c.op("dve", lambda e: e.tensor_tensor(out=exg, in0=SC, in1=SC[:, :, 0:1].to_broadcast([128, 8, 16]), op=ALU.subtract),
                                   reads=[b_SC], writes=[b_exg]))
                    A(lambda: c.op("act", lambda e: e.activation(out=exg, in_=exg, func=AF.Exp), reads=[], writes=[b_exg]))
                    A(lambda: c.op("dve", lambda e: e.reduce_sum(out=sm[:, 0, :], in_=exg, axis=AX.X), reads=[b_exg], writes=[b_sm]))
                    A(lambda: c.op("dve", lambda e: e.reciprocal(out=sm[:, 1, :], in_=sm[:, 0, :]), reads=[], writes=[b_sm]))
                    A(lambda t4=t4: c.op("dve", lambda e: e.tensor_tensor(out=gate_all[:, t4, :].rearrange("p (h k) -> p h k", h=8), in0=exg,
                                                                         in1=sm[:, 1, :].unsqueeze(2).to_broadcast([128, 8, 16]), op=ALU.mult),
                                         reads=[b_exg, b_sm], writes=[b_gate[t4]]))
                return th

            gi = [0]
            cgt = V(o - 0, [128, 1]) if False else None
            b_act1 = bufs(16, "act1"); b_coef1 = bufs(16, "coef1")
            D_DOT, D_ACC, D_GELU, D_CG, D_DG, D_MM = 2, 3, 4, 5, 6, 7

            def gathers(blk, pending):
                stream = [(blk * 2 + ti, sl_) for ti in range(2) for sl_ in range(128)]
                N = len(stream)
                per = (len(pending) + 239) // 240 if pending else 0
                gmap = {}
                for n in range(N + D_MM):
                    m = n
                    if 0 <= m < N:
                        tt, sl_ = stream[m]
                        t4 = tt % 4
                        g = gi[0] % NG; gi[0] += 1
                        gmap[m] = g
                        c.dma("pool", lambda e, g=g, t4=t4, sl_=sl_: e.indirect_dma_start(
                            out=G[g], out_offset=None, in_=uvb_d[l][:, :],
                            in_offset=bass.IndirectOffsetOnAxis(ap=idx_all[:, t4, sl_:sl_ + 1], axis=0)),
                            reads=[b_idx[t4]] + b_uvb[l], writes=[b_G[g]])
                    m = n - D_DOT
                    if 0 <= m < N:
                        tt, sl_ = stream[m]
                        g = gmap[m]
                        if sl_ == 0:
                            c.op("act", lambda e, tt=tt: e.activation(out=xb, in_=xs[:, tt, :], func=AF.Copy), reads=[b_xs[tt]], writes=[b_xb])
                        if sl_ % 2 == 0:
                            c.op("dve", lambda e, g=g, tt=tt, sl_=sl_: e.scalar_tensor_tensor(
                                out=G[g][:, 0:D], in0=G[g][:, 0:D], scalar=1.0, in1=xs[:, tt, :], op0=ALU.mult, op1=ALU.mult,
                                accum_out=actt[:, sl_:sl_ + 1]), reads=[b_xs[tt]], writes=[b_G[g], b_act1[m % 16]])
                        else:
                            c.op("dve", lambda e, g=g: e.tensor_tensor(out=G[g][:, 0:D], in0=G[g][:, 0:D], in1=xb, op=ALU.mult),
                                 reads=[b_xb], writes=[b_G[g]])
                    m = n - D_ACC
                    if 0 <= m < N and stream[m][1] % 2 == 1:
                        tt, sl_ = stream[m]
                        g = gmap[m]
                        c.op("act", lambda e, sl_=sl_, g=g: e.activation(out=G[g][:, 0:D], in_=G[g][:, 0:D], func=AF.Copy, accum_out=actt[:, sl_:sl_ + 1]),
                             reads=[], writes=[b_G[g], b_act1[m % 16]])
                    for m in (n - D_GELU, n - D_GELU + 1):
                        if not (0 <= m < N) or (stream[m][1] % 2 == 1) != (m == n - D_GELU):
                            continue
                        tt, sl_ = stream[m]
                        c.op("act", lambda e, sl_=sl_: e.activation(out=coef[:, sl_:sl_ + 1], in_=actt[:, sl_:sl_ + 1], func=AF.Gelu),
                             reads=[b_act1[m % 16]], writes=[b_coef1[m % 16]])
                    for m in (n - D_CG, n - D_CG + 1):
                        if not (0 <= m < N) or (stream[m][1] % 2 == 1) != (m == n - D_CG):
                            continue
                        tt, sl_ = stream[m]
                        t4 = tt % 4
                        c.op("dve", lambda e, sl_=sl_, t4=t4: e.tensor_tensor(out=coef[:, sl_:sl_ + 1], in0=coef[:, sl_:sl_ + 1],
                                                                             in1=gate_all[:, t4, sl_:sl_ + 1], op=ALU.mult),
                             reads=[b_gate[t4]], writes=[b_coef1[m % 16]])
                    for m in (n - D_DG, n - D_DG + 1):
                        if not (0 <= m < N) or (stream[m][1] % 2 == 1) != (m == n - D_DG):
                            continue
                        tt, sl_ = stream[m]
                        d_ = m % NDG
                        c.op("act", lambda e, sl_=sl_, d_=d_: e.activation(out=dg[d_], in_=identb, func=AF.Copy, scale=coef[:, sl_:sl_ + 1]),
                             reads=[b_identb, b_coef1[m % 16]], writes=[b_dg[d_]])
                    for m in (n - D_MM, n - D_MM + 1):
                        if not (0 <= m < N) or (stream[m][1] % 2 == 1) != (m == n - D_MM):
                            continue
                        tt, sl_ = stream[m]
                        g = gmap[m]
                        d_ = m % NDG
                        AB = 4 + 2 * (tt % 2)
                        for hf in range(2):
                            mm(ps[AB + hf][:, :], dg[d_], G[g][:, D + hf * 512:D + (hf + 1) * 512], sl_ == 0, sl_ == 127,
                               [b_dg[d_], b_G[g]], [b_ps[AB + hf]])
                        if sl_ == 127:
                            for hf in range(2):
                                c.op("dve", lambda e, tt=tt, hf=hf, AB=AB: e.scalar_tensor_tensor(
                                    out=xs[:, tt, hf * 512:(hf + 1) * 512], in0=xs[:, tt, hf * 512:(hf + 1) * 512], scalar=ALPHA,
                                    in1=ps[AB + hf][:, :], op0=ALU.mult, op1=ALU.add), reads=[b_ps[AB + hf]], writes=[b_xs[tt]])
                            layer_norm([tt], b_lnp, aff_eng="dve")
                            if is_last:
                                out_evs.append(c.dma("sp", lambda e, tt=tt: e.dma_start(out=out_d[s, tt * 128:(tt + 1) * 128, :], in_=xs[:, tt, :]),
                                                     reads=[b_xs[tt]], writes=[], sembuf=b_outd))
                    for _ in range(per):
                        if pending:
                            pending.pop(0)()
                while pending:
                    pending.pop(0)()

            b_G = bufs(NG, "G")
            for t_ in score_thunks(0) + routing_thunks(0):
                t_()
            for blk in range(NB2):
                pending = []
                if blk + 1 < NB2:
                    pending = score_thunks(blk + 1) + routing_thunks(blk + 1)
                gathers(blk, pending)
            c.barrier()
            wst_n[0] = WST
            return out_evs

        b_uvb = [bufs(4, f"uvb{l}") for l in range(L)]
        CH = 2048

        def emit_conv(l, part):
            if "peer" not in phases:
                return
            fns = []
            for r in range(part * 2, part * 2 + 2):
                fns.append(lambda e, r=r: e.dma_start(out=uvb_d[l][r * CH:(r + 1) * CH, 0:D], in_=pu_d[l][r * CH:(r + 1) * CH, :]))
                fns.append(lambda e, r=r: e.dma_start(out=uvb_d[l][r * CH:(r + 1) * CH, D:2 * D], in_=pv_d[l][r * CH:(r + 1) * CH, :]))
            c.dma("pool", fns, writes=[b_uvb[l][part]])

        conv_state = {"todo": []}

        def conv_tick():
            if conv_state["todo"]:
                l_, p_ = conv_state["todo"].pop(0)
                emit_conv(l_, p_)

        out_events = []
        for s in range(NSEQ):
            c.barrier()
            c.dma("sp", [(lambda e, s=s, tt=tt: e.dma_start(out=xs[:, tt, :], in_=x_d[s, tt * 128:(tt + 1) * 128, :]))
                         for tt in range(NT)], writes=list(b_xs), sembuf=b_misc)
            dumped = False
            for l in range(L):
                if "mixer" in phases:
                    if s == 0:
                        conv_state["todo"] = [(l, p_) for p_ in range(4)]
                    make_xT()
                    mixer(l)
                    while conv_state["todo"]:
                        conv_tick()
                    b_lnp = load_ln(l, 0)
                    layer_norm(range(NT), b_lnp, add_eng="pool")
                if stop_after == (l, "ln1"):
                    break
                if "xattn" in phases:
                    make_xT()
                    xattn(l, s)
                    b_lnp = load_ln(l, 1)
                    layer_norm(range(NT), b_lnp, add_eng="pool")
                if stop_after == (l, "ln2"):
                    break
                if "peer" in phases:
                    make_xT()
                    last = (l == L - 1) or stop_after == (l, "ln3")
                    evs = peer(l, s, last)
                    if last:
                        out_events.extend(evs)
                        dumped = True
                if stop_after == (l, "ln3"):
                    break
            if not dumped:
                for tt in range(NT):
                    out_events.append(c.dma("sp", lambda e, s=s, tt=tt: e.dma_start(out=out_d[s, tt * 128:(tt + 1) * 128, :], in_=xs[:, tt, :]),
                                            reads=[b_xs[tt]], writes=[], sembuf=b_outd))
        c.emit(out_events[-1:])
    return nc


def host_consts():
    kp = np.arange(128)[:, None]
    xx = np.arange(896)[None, :]
    nble = np.where((xx - 384) >= kp, 0.0, -30000.0).astype(np.float32)
    mlt = ((xx - 384) > kp).astype(np.float32)
    jj = np.arange(128)[:, None]
    ss = np.arange(128)[None, :]
    negtri = -(jj > ss).astype(np.float32)
    return {
        "c_ident": np.eye(128, dtype=np.float32),
        "c_nble": nble, "c_mlt": mlt, "c_negtri": negtri,
        "c_iota16": np.tile(np.arange(16, dtype=np.float32)[None, :], (128, 1)),
        "c_iota256": np.tile(np.arange(256, dtype=np.float32)[None, :], (128, 1)),
    }


def host_weights(inp, L):
    f = lambda a: np.ascontiguousarray(np.asarray(a, dtype=np.float32))
    w = {}
    w["w_in"] = f(inp["w_in"][:L]); w["w_out"] = f(inp["w_out"][:L])
    rgp = np.zeros((L, 2, 128, 8), np.float32)
    cw = np.asarray(inp["rg_conv_w"])[:L]
    for k in range(4):
        rgp[:, :, :, k] = cw[:, k, :].reshape(L, 2, 128)
    rgp[:, :, :, 4] = np.asarray(inp["rg_conv_b"])[:L].reshape(L, 2, 128)
    rgp[:, :, :, 5] = np.asarray(inp["rg_ba"])[:L].reshape(L, 2, 128)
    rgp[:, :, :, 6] = np.asarray(inp["rg_bi"])[:L].reshape(L, 2, 128)
    rgp[:, :, :, 7] = np.asarray(inp["rg_lambda"])[:L].reshape(L, 2, 128)
    w["rgp"] = rgp
    for nm, src in (("wa_bd", "rg_wa"), ("wi_bd", "rg_wi")):
        a = np.asarray(inp[src])[:L]
        bd = np.zeros((L, 2, 128, 128), np.float32)
        for h in range(4):
            j, o = h // 2, (h % 2) * 64
            bd[:, j, o:o + 64, o:o + 64] = a[:, h]
        w[nm] = bd
    w["fox_bf"] = f(np.asarray(inp["fox_bf"])[:L].reshape(L, 4, 1))
    scp = np.zeros((L, 2, 128, 4), np.float32)
    sw = np.asarray(inp["sc_conv_w"])[:L]
    for k in range(3):
        scp[:, :, :, k] = sw[:, k, :].reshape(L, 2, 128)
    w["scp"] = scp
    w["mng"] = f(np.asarray(inp["mix_norm_g"])[:L].reshape(L, 8, 128).transpose(0, 2, 1))
    w["ln_g"] = f(np.stack([np.asarray(inp["ln1_g"])[:L], np.asarray(inp["ln2_g"])[:L], np.asarray(inp["ln3_g"])[:L]], axis=1))
    w["ln_b"] = f(np.stack([np.asarray(inp["ln1_b"])[:L], np.asarray(inp["ln2_b"])[:L], np.asarray(inp["ln3_b"])[:L]], axis=1))
    w["xa_wq"] = f(inp["xa_wq"][:L]); w["xa_wkv"] = f(inp["xa_wkv"][:L]); w["xa_wo"] = f(inp["xa_wo"][:L])
    w["peer_wq"] = f(inp["peer_wq"][:L])
    w["k12T"] = f(np.stack([np.asarray(inp["peer_k1"])[:L].transpose(0, 2, 1),
                            np.asarray(inp["peer_k2"])[:L].transpose(0, 2, 1)], axis=2))
    for l in range(L):
        w[f"peer_u{l}"] = f(inp["peer_u"][l]); w[f"peer_v{l}"] = f(inp["peer_v"][l])
    w.update(host_consts())
    return w


def run(inp, n_cores=8, NSEQ=2, S=2048, L=2, stop_after=None, trace=False, phases=("mixer", "xattn", "peer")):
    nc = build_program(NSEQ=NSEQ, S=S, L=L, stop_after=stop_after, phases=phases)
    w = host_weights(inp, L)
    x = np.asarray(inp["x"], dtype=np.float32)
    mem = np.asarray(inp["mem"], dtype=np.float32)
    in_maps = []
    for ci in range(n_cores):
        m = dict(w)
        m["x"] = np.ascontiguousarray(x[ci * NSEQ:(ci + 1) * NSEQ, :S])
        m["memT"] = np.ascontiguousarray(mem[ci * NSEQ:(ci + 1) * NSEQ].transpose(0, 2, 1))
        in_maps.append(m)
    res = run_bass_kernel_spmd(nc, in_maps, core_ids=list(range(n_cores)), trace=trace)
    out = np.concatenate([r["out"] for r in res.results], axis=0)
    return out, res


def kernel(**inputs):
    out, _ = run(inputs)
    return out.astype(np.float32)
```
